# Optimizing a Trainium2 kernel written in Bass

```python
import jax, jax.numpy as jnp
from jax import lax
import numpy as np

D_MODEL = 1024
BATCH = 8
SEQ = 4096
DEPTH = 1

N_META = 16
CONV_WIDTH = D_MODEL
CONV_KERNEL = 31
RWKV_HEAD = 64
RWKV_HEADS = D_MODEL // RWKV_HEAD
RWKV_WIDTH = RWKV_HEADS * RWKV_HEAD
DECAY_LORA = 64
ICLR_LORA = 64
GATE_LORA = 160
D_FF = 2816
FFN_KERNEL = 3
ALPHA = (2.0 * DEPTH) ** 0.25
BETA = (8.0 * DEPTH) ** -0.25
LN_EPS = 1e-5
GN_EPS = 64e-5

N_RWKV_COLS = 3 * RWKV_WIDTH + DECAY_LORA + ICLR_LORA + GATE_LORA
N_IN = 2 * CONV_WIDTH + N_RWKV_COLS + 2 * D_MODEL

kernel_name = "hybrid_conformer_rwkv7_gated_deepnorm"


def layer_norm(x, g, b, eps=LN_EPS):
    xf = x.astype(jnp.float32)
    mu = jnp.mean(xf, axis=-1, keepdims=True)
    xc = xf - mu
    var = jnp.mean(xc * xc, axis=-1, keepdims=True)
    y = xc * lax.rsqrt(var + eps) * g.astype(jnp.float32) + b.astype(jnp.float32)
    return y.astype(x.dtype)


def causal_dwconv(x, w):
    k, c = w.shape
    return lax.conv_general_dilated(
        x, w[:, None, :].astype(x.dtype), window_strides=(1,), padding=[(k - 1, 0)],
        dimension_numbers=("NWC", "WIO", "NWC"), feature_group_count=c)


def token_shift(p):
    return jnp.pad(p, ((0, 0), (1, 0), (0, 0)))[:, :-1]


def rwkv7_recurrence(r, decay, k, v, a, b):
    bsz, _, h, n = r.shape

    def step(s, inp):
        r_t, w_t, k_t, v_t, a_t, b_t = inp
        sa = jnp.einsum("bhvk,bhk->bhv", s, a_t)
        s = s * w_t[:, :, None, :] + sa[..., None] * b_t[:, :, None, :] + v_t[..., None] * k_t[:, :, None, :]
        y = jnp.einsum("bhvk,bhk->bhv", s, r_t)
        return s, y

    xs = tuple(jnp.moveaxis(z, 1, 0) for z in (r, decay, k, v, a, b))
    s0 = jnp.zeros((bsz, h, n, n), jnp.float32)
    _, ys = lax.scan(step, s0, xs)
    return jnp.moveaxis(ys, 0, 1)


def rwkv7_time_mix(p, mu, w0, w_decay_up, a0, a_up, g_up, k_k, k_a, r_k, lnx_g, lnx_b, w_out):
    bsz, t, _ = p.shape
    h, n, wd = RWKV_HEADS, RWKV_HEAD, RWKV_WIDTH
    p = p + (token_shift(p) - p) * mu
    i1, i2, i3 = wd, 2 * wd, 3 * wd
    i4, i5 = i3 + DECAY_LORA, i3 + DECAY_LORA + ICLR_LORA
    r, k, v = p[..., :i1], p[..., i1:i2], p[..., i2:i3]
    xw, xa, xg = p[..., i3:i4], p[..., i4:i5], p[..., i5:]
    w = -jax.nn.softplus(-(w0 + jnp.tanh(xw) @ w_decay_up)) - 0.5
    a = jax.nn.sigmoid(a0 + xa @ a_up)
    g = jax.nn.sigmoid(xg) @ g_up
    heads = lambda z: z.reshape(bsz, t, h, n).astype(jnp.float32)
    kk = heads(k * k_k)
    kk = kk / jnp.maximum(jnp.sqrt(jnp.sum(kk * kk, axis=-1, keepdims=True)), 1e-12)
    k = k * (1.0 + (a - 1.0) * k_a)
    decay = jnp.exp(-jnp.exp(heads(w)))
    rh, kh, vh, ah = heads(r), heads(k), heads(v), heads(a)
    o = rwkv7_recurrence(rh, decay, kh, vh, -kk, kk * ah)
    o = layer_norm(o, lnx_g.reshape(h, n), lnx_b.reshape(h, n), GN_EPS)
    o = o + jnp.sum(rh * kh * r_k.astype(jnp.float32), axis=-1, keepdims=True) * vh
    o = o.reshape(bsz, t, wd).astype(p.dtype) * g
    return o @ w_out


def conformer_conv(p, conv_dw, conv_dw_b, conv_ln_g, conv_ln_b, w_conv_out):
    c = p[..., :CONV_WIDTH] * jax.nn.sigmoid(p[..., CONV_WIDTH:])
    c = causal_dwconv(c, conv_dw) + conv_dw_b
    c = jax.nn.silu(layer_norm(c, conv_ln_g, conv_ln_b))
    return c @ w_conv_out


def conv_gated_mlp(h, ffn_up, ffn_dw, ffn_down):
    u = causal_dwconv(h @ ffn_up, ffn_dw)
    return (jax.nn.silu(u[..., D_FF:]) * u[..., :D_FF]) @ ffn_down


def setup_inputs(seed: int = 0) -> dict:
    key = jax.random.key(seed)
    ks = iter(jax.random.split(key, 40))
    L, D = DEPTH, D_MODEL

    def nrm(shape, scale):
        return scale * jax.random.normal(next(ks), shape, jnp.float32)

    def gain(shape):
        return 1.0 + nrm(shape, 0.02)

    def unif(shape, lo, hi):
        return jax.random.uniform(next(ks), shape, jnp.float32, minval=lo, maxval=hi)

    return {
        "x": nrm((BATCH, SEQ, D), 1.0),
        "meta_tokens": nrm((N_META, D), 1.0),
        "ln_in_g": gain((D,)),
        "ln_in_b": nrm((D,), 0.02),
        "w_in": nrm((L, D, N_IN), D ** -0.5),
        "b_gate": nrm((L, 2 * D), 0.02),
        "conv_dw": nrm((L, CONV_KERNEL, CONV_WIDTH), CONV_KERNEL ** -0.5),
        "conv_dw_b": nrm((L, CONV_WIDTH), 0.02),
        "conv_ln_g": gain((L, CONV_WIDTH)),
        "conv_ln_b": nrm((L, CONV_WIDTH), 0.02),
        "w_conv_out": nrm((L, CONV_WIDTH, D), CONV_WIDTH ** -0.5),
        "rwkv_mu": unif((L, N_RWKV_COLS), 0.1, 0.9),
        "w0": unif((L, RWKV_WIDTH), -6.5, -1.5),
        "w_decay_up": nrm((L, DECAY_LORA, RWKV_WIDTH), 0.1 * DECAY_LORA ** -0.5),
        "a0": nrm((L, RWKV_WIDTH), 0.1),
        "a_up": nrm((L, ICLR_LORA, RWKV_WIDTH), 0.1 * ICLR_LORA ** -0.5),
        "g_up": nrm((L, GATE_LORA, RWKV_WIDTH), GATE_LORA ** -0.5),
        "k_k": 0.85 + nrm((L, RWKV_WIDTH), 0.02),
        "k_a": gain((L, RWKV_WIDTH)),
        "r_k": -0.04 + nrm((L, RWKV_HEADS, RWKV_HEAD), 0.02),
        "lnx_g": gain((L, RWKV_WIDTH)),
        "lnx_b": nrm((L, RWKV_WIDTH), 0.02),
        "w_rwkv_out": nrm((L, RWKV_WIDTH, D), RWKV_WIDTH ** -0.5),
        "w_o": nrm((L, D, D), BETA * D ** -0.5),
        "ln1_g": gain((L, D)),
        "ln1_b": nrm((L, D), 0.02),
        "ffn_up": nrm((L, D, 2 * D_FF), D ** -0.5),
        "ffn_dw": nrm((L, FFN_KERNEL, 2 * D_FF), FFN_KERNEL ** -0.5),
        "ffn_down": nrm((L, D_FF, D), BETA * D_FF ** -0.5),
        "ln2_g": gain((L, D)),
        "ln2_b": nrm((L, D), 0.02),
    }


def reference(x, meta_tokens, ln_in_g, ln_in_b, w_in, b_gate, conv_dw, conv_dw_b, conv_ln_g,
              conv_ln_b, w_conv_out, rwkv_mu, w0, w_decay_up, a0, a_up, g_up, k_k, k_a, r_k,
              lnx_g, lnx_b, w_rwkv_out, w_o, ln1_g, ln1_b, ffn_up, ffn_dw, ffn_down, ln2_g, ln2_b):
    bsz = x.shape[0]
    meta = jnp.broadcast_to(meta_tokens[None].astype(x.dtype), (bsz, N_META, D_MODEL))
    h = layer_norm(jnp.concatenate([meta, x], axis=1), ln_in_g, ln_in_b)
    c_end = 2 * CONV_WIDTH
    r_end = c_end + N_RWKV_COLS
    for l in range(DEPTH):
        p = h @ w_in[l]
        y_conv = conformer_conv(p[..., :c_end], conv_dw[l], conv_dw_b[l], conv_ln_g[l],
                                conv_ln_b[l], w_conv_out[l])
        y_rwkv = rwkv7_time_mix(p[..., c_end:r_end], rwkv_mu[l], w0[l], w_decay_up[l], a0[l],
                                a_up[l], g_up[l], k_k[l], k_a[l], r_k[l], lnx_g[l], lnx_b[l],
                                w_rwkv_out[l])
        gates = jax.nn.sigmoid(p[..., r_end:] + b_gate[l])
        y = gates[..., :D_MODEL] * y_conv + gates[..., D_MODEL:] * y_rwkv
        h = layer_norm(ALPHA * h + y @ w_o[l], ln1_g[l], ln1_b[l])
        h = layer_norm(ALPHA * h + conv_gated_mlp(h, ffn_up[l], ffn_dw[l], ffn_down[l]),
                       ln2_g[l], ln2_b[l])
    return h[:, N_META:]
```

```python
import os
import math
import types
import numpy as np
from contextlib import ExitStack
import concourse.bass as bass
import concourse.mybir as mybir
from concourse.bass_utils import run_bass_kernel_spmd

F32 = mybir.dt.float32
BF16 = mybir.dt.bfloat16
AF = mybir.ActivationFunctionType
ALU = mybir.AluOpType

D = 1024
DC = 8
N = 384
NCH = 3
CH = 128
NMETA = 16
SEQ = 4096
T_REAL = SEQ + NMETA
TP = 4224
NT_FULL = TP // N
DFF = 2816
FC = 22
ALPHA = 2.0 ** 0.25
LN_EPS = 1e-5
GN_EPS = 64e-5
C0 = math.exp(-0.5)
CONVK = 31
HALO = CONVK - 1
NRW = 3360
GSZ = 4096
NSLOT = 3
SEM_ROLL = 6000
ENGS = ("pe", "act", "dve", "pool", "sp")


def freeze(fn):
    if fn is None or fn.__closure__ is None:
        return fn
    cells = []
    for c in fn.__closure__:
        try:
            cells.append(types.CellType(c.cell_contents))
        except ValueError:
            cells.append(c)
    return types.FunctionType(fn.__code__, fn.__globals__, fn.__name__, fn.__defaults__, tuple(cells))


class Prog:
    def __init__(self, nc, es, n_dma_sems=12):
        self.nc = nc
        self.es = es
        self.streams = {e: [] for e in ENGS}
        self.cnt = {e: 0 for e in ENGS}
        self.cur_sem = {}
        self.nsem = 0
        for e in ENGS:
            self._new_eng_sem(e)
        self.dma_sems = [self._sem("dma%d" % i) for i in range(n_dma_sems)]
        self.dma_val = [0] * n_dma_sems
        self.dma_rr = 0
        self.lastw = {}
        self.readers = {}
        self.waited = {e: {} for e in ENGS}
        self.nops = 0

    def _sem(self, name):
        self.nsem += 1
        return self.es.enter_context(self.nc.semaphore(name))

    def _new_eng_sem(self, e):
        self.cur_sem[e] = self._sem("s_%s_%d" % (e, self.nsem))
        self.cnt[e] = 0

    def _deps(self, eng, reads, writes):
        toks = []
        for k in reads:
            t = self.lastw.get(k)
            if t is not None:
                toks.append(t)
        for k in writes:
            t = self.lastw.get(k)
            if t is not None:
                toks.append(t)
            toks.extend(self.readers.get(k, ()))
        waits = {}
        for (sem, val, src) in toks:
            if src == "pe" and eng == "pe":
                continue
            key = id(sem)
            if self.waited[eng].get(key, 0) >= val:
                continue
            if key not in waits or waits[key][1] < val:
                waits[key] = (sem, val)
        for key, (sem, val) in waits.items():
            self.waited[eng][key] = val
        return list(waits.values())

    def _commit(self, tok, reads, writes):
        for k in reads:
            if k in writes:
                continue
            self.readers.setdefault(k, []).append(tok)
        for k in writes:
            self.lastw[k] = tok
            self.readers[k] = []

    def op(self, eng, fn, reads=(), writes=()):
        fn = freeze(fn)
        reads = tuple(reads)
        writes = tuple(writes)
        waits = self._deps(eng, reads, writes)
        if self.cnt[eng] >= SEM_ROLL:
            self._new_eng_sem(eng)
        self.cnt[eng] += 1
        sem = self.cur_sem[eng]
        tok = (sem, self.cnt[eng], eng)
        self.streams[eng].append((waits, fn, sem, 1))
        self._commit(tok, reads, writes)
        self.nops += 1

    def dma(self, fn, reads=(), writes=(), eng="sp"):
        fn = freeze(fn)
        reads = tuple(reads)
        writes = tuple(writes)
        i = self.dma_rr
        self.dma_rr = (self.dma_rr + 1) % len(self.dma_sems)
        sem = self.dma_sems[i]
        waits = self._deps(eng, reads, writes)
        if self.dma_val[i] > 0 and self.waited[eng].get(id(sem), 0) < self.dma_val[i]:
            waits = [w for w in waits if w[0] is not sem] + [(sem, self.dma_val[i])]
            self.waited[eng][id(sem)] = self.dma_val[i]
        self.dma_val[i] += 16
        tok = (sem, self.dma_val[i], "dma")
        self.streams[eng].append((waits, fn, sem, 16))
        self._commit(tok, reads, writes)
        self.nops += 1

    def final_wait(self, eng="sp"):
        waits = []
        for i, sem in enumerate(self.dma_sems):
            if self.dma_val[i] > 0:
                waits.append((sem, self.dma_val[i]))
        self.streams[eng].append((waits, None, None, 0))

    def emit(self):
        nc = self.nc
        with nc.Block() as block:
            def run(engname):
                def body(e):
                    for (waits, fn, sem, inc) in self.streams[engname]:
                        for (s, v) in waits:
                            e.wait_ge(s, v)
                        if fn is not None:
                            ins = fn(e)
                            ins.then_inc(sem, inc)
                return body
            block.tensor(run("pe"))
            block.scalar(run("act"))
            block.vector(run("dve"))
            block.gpsimd(run("pool"))
            block.sync(run("sp"))


def weight_groups():
    g = []
    def add(name, kind, kc_src, m, src, cols):
        g.append(dict(name=name, kind=kind, kc_src=kc_src, m=m, src=src, cols=np.asarray(cols)))
    for i in range(2):
        add("ga%d" % i, "plain", 8, 512, "w_in", np.arange(i * 512, (i + 1) * 512))
        add("gg%d" % i, "plain", 8, 512, "w_in", 1024 + np.arange(i * 512, (i + 1) * 512))
    for i in range(2):
        add("co%d" % i, "plain", 8, 512, "w_conv_out", np.arange(i * 512, (i + 1) * 512))
        add("gA%d" % i, "plain", 8, 512, "w_in", 5408 + np.arange(i * 512, (i + 1) * 512))
    add("lr0", "rwkv", 8, 128, "w_in", 2048 + 3072 + np.arange(0, 128))
    add("lr1", "rwkv", 8, 160, "w_in", 2048 + 3072 + 128 + np.arange(0, 160))
    for i in range(4):
        for j, nm in enumerate("rkv"):
            add("%s%d" % (nm, i), "rwkv", 8, 256, "w_in", 2048 + j * 1024 + np.arange(i * 256, (i + 1) * 256))
    for i in range(2):
        add("ro%d" % i, "plain", 8, 512, "w_rwkv_out", np.arange(i * 512, (i + 1) * 512))
        add("gB%d" % i, "plain", 8, 512, "w_in", 6432 + np.arange(i * 512, (i + 1) * 512))
    for i in range(2):
        add("wo%d" % i, "plain", 8, 512, "w_o", np.arange(i * 512, (i + 1) * 512))
    for j in range(11):
        cols = np.concatenate([np.arange(2 * j * 128, (2 * j + 2) * 128),
                               DFF + np.arange(2 * j * 128, (2 * j + 2) * 128)])
        add("fu%d" % j, "plain", 8, 512, "ffn_up", cols)
    for m in range(8):
        add("fd%d" % m, "plain", 22, 128, "ffn_down", np.arange(m * 128, (m + 1) * 128))
    off = 0
    for i, x in enumerate(g):
        x["idx"] = i
        x["off"] = off
        x["kc_out"] = x["kc_src"] * (2 if x["kind"] == "rwkv" else 1)
        off += x["kc_src"] * x["m"]
    return g, off


def par_layout():
    cols = {}
    off = 0
    def add(name, n):
        nonlocal off
        cols[name] = off
        off += n
    add("ln_in_g", 8); add("ln_in_b", 8)
    add("conv_dw", CONVK * 8)
    add("conv_dw_b", 8); add("conv_ln_g", 8); add("conv_ln_b", 8)
    add("b_gate", 16)
    add("w0", 8); add("a0", 8); add("k_k", 8); add("k_a", 8); add("r_k", 8)
    add("lnx_g", 8); add("lnx_b", 8)
    add("ln1_g", 8); add("ln1_b", 8)
    add("ffn_dw", 3 * 44)
    add("ln2_g", 8); add("ln2_b", 8)
    add("omka", 8)
    return cols, off


def colvec(v):
    v = np.asarray(v, np.float32).reshape(-1, 128)
    return np.ascontiguousarray(v.T)


def build_program(nt_run, dbg_names=()):
    groups, totw = weight_groups()
    gidx = {x["name"]: x for x in groups}
    NG = len(groups)
    pcol, npar = par_layout()

    nc = bass.Bass("TRN2", target_bir_lowering=False)
    xT = nc.dram_tensor("xT", [D, TP], F32, kind="ExternalInput").ap()
    wall = nc.dram_tensor("wall", [128, totw], F32, kind="ExternalInput").ap()
    par = nc.dram_tensor("par", [128, npar], F32, kind="ExternalInput").ap()
    mub = nc.dram_tensor("mub", [128, NRW], F32, kind="ExternalInput").ap()
    lora = nc.dram_tensor("lora", [128, 4, D], F32, kind="ExternalInput").ap()
    outT = nc.dram_tensor("outT", [D, SEQ], F32, kind="ExternalOutput").ap()
    wsc = nc.dram_tensor("wsc", [NG, 128, GSZ], BF16, kind="Internal").ap()
    dbg_out = {}

    with ExitStack() as es:
        P = Prog(nc, es)

        def sb(name, shape, dt):
            return es.enter_context(nc.sbuf_tensor(name, shape, dt))

        def psum(name):
            return es.enter_context(nc.psum_tensor(name, [128, 512], F32))

        PT = sb("PT", [128, npar], F32)
        ident = sb("ident", [128, 128], BF16)
        onesln = sb("onesln", [128, 128], BF16)
        bo64 = sb("bo64", [128, 128], BF16)
        bo1 = sb("bo1", [128, 128], BF16)
        mask4 = sb("mask4", [128, 4, 128], F32)
        ones_f = sb("ones_f", [128, 128], F32)
        lorab = sb("lorab", [128, 4, D], BF16)
        ring = [sb("ring%d" % i, [128, GSZ], BF16) for i in range(NSLOT)]
        BIG0 = sb("BIG0", [128, 8 * N], F32)
        BIG1 = sb("BIG1", [128, 8 * N], F32)
        HB0 = sb("HB0", [128, 8 * N], BF16)
        HB1 = sb("HB1", [128, 8 * N], BF16)
        mut = [sb("mut%d" % i, [128, 256], F32) for i in range(2)]
        xres = sb("xres", [128, 8, N], F32)
        hT = sb("hT", [128, 8, N + 1], BF16)
        h1T = sb("h1T", [128, 8, N + 2], BF16)
        cb = sb("cb", [128, 8, N + HALO], BF16)
        diag = [sb("diag%d" % i, [128, CONVK, 128], BF16) for i in range(1)]
        NF = 16
        NH = 6
        Ft = [sb("F%d" % i, [128, N], F32) for i in range(NF)]
        Ht = [sb("H%d" % i, [128, N], BF16) for i in range(NH)]
        act = sb("act", [128, FC, N], BF16)
        tw = sb("tw", [128, N], BF16)
        xab = sb("xab", [128, N], BF16)
        sg0 = sb("sg0", [128, N], BF16)
        sg1 = sb("sg1", [128, N], BF16)
        GP = 2
        AR = [sb("AR%d" % i, [128, 2, N], BF16) for i in range(GP)]
        BK = [sb("BK%d" % i, [128, 2, N], BF16) for i in range(GP)]
        BKh = [sb("BKh%d" % i, [128, 2, N], BF16) for i in range(GP)]
        vb = [sb("vb%d" % i, [128, N], BF16) for i in range(GP)]
        vTf = [sb("vTf%d" % i, [128, N], F32) for i in range(GP)]
        rkb = [sb("rkb%d" % i, [128, N], BF16) for i in range(GP)]
        Yf = [sb("Yf%d" % i, [128, N], F32) for i in range(GP)]
        PCc = [sb("PCc%d" % i, [128, NCH], F32) for i in range(GP)]
        TM = [sb("TM%d" % i, [128, 4, 128], BF16) for i in range(GP)]
        QKs = [[sb("QKs%d_%d" % (i, h), [128, 4, 128], BF16) for h in range(2)] for i in range(GP)]
        NMb = [[sb("NMb%d_%d" % (i, k), [128, 4, 128], BF16) for k in range(2)] for i in range(GP)]
        Qb = [[sb("Qb%d_%d" % (i, k), [128, 2, 128], BF16) for k in range(2)] for i in range(GP)]
        G1Tb = [sb("G1Tb%d" % i, [128, 128], BF16) for i in range(GP)]
        Xb = [sb("Xb%d" % i, [128, 128], BF16) for i in range(GP)]
        G2f = [sb("G2f%d" % i, [128, 128], F32) for i in range(GP)]
        Ub = [sb("Ub%d" % i, [128, 128], BF16) for i in range(GP)]
        Stmp = [sb("Stmp%d" % i, [128, 128], F32) for i in range(GP)]
        Sf = sb("Sf", [128, 8, 128], F32)
        Sb = sb("Sb", [128, 8, 128], BF16)
        nbC = sb("nbC", [128, NCH], F32)

        pm = [psum("pm%d" % i) for i in range(3)]
        pa = psum("pa")
        pb = psum("pb")
        pr = [psum("pr%d" % i) for i in range(3)]
        PRR = pr + pm
        rr = {"pm": 0, "pr": 0, "slot": 0}

        def next_pm():
            i = rr["pm"]; rr["pm"] = (i + 1) % 3
            return pm[i], "pm%d" % i

        def next_pr():
            i = rr["pr"]; rr["pr"] = (i + 1) % 6
            return PRR[i], ("pr%d" % i if i < 3 else "pm%d" % (i - 3))

        def pc(name, c):
            o = pcol[name] + c
            return PT[:, o:o + 1]

        def dump(name, ap, shape, keys, dt=F32):
            if name not in dbg_names:
                return
            t = nc.dram_tensor("dbg_" + name, list(shape), dt, kind="ExternalOutput").ap()
            dbg_out[name] = shape
            P.dma(lambda e: e.dma_start(out=t, in_=ap), reads=list(keys))

        P.dma(lambda e: e.dma_start(out=PT[:, 0:npar], in_=par[:, :]), writes=["PT"])
        P.op("pool", lambda e: e.memset(ident[:], 1.0), writes=["ident"])
        P.op("pool", lambda e: e.affine_select(out=ident[:], in_=ident[:], pattern=[[-1, 128]],
                                                compare_op=ALU.is_equal, fill=0.0, base=0,
                                                channel_multiplier=1), reads=["ident"], writes=["ident"])
        P.op("pool", lambda e: e.memset(onesln[:], 1.0 / D), writes=["onesln"])
        P.op("pool", lambda e: e.memset(ones_f[:], 1.0), writes=["ones_f"])
        for (t_, v_, k_) in ((bo64, 1.0 / 64, "bo64"), (bo1, 1.0, "bo1")):
            P.op("pool", lambda e, t_=t_: e.memset(t_[:], 0.0), writes=[k_])
            P.op("pool", lambda e, t_=t_, v_=v_: e.memset(t_[0:64, 0:64], v_), reads=[k_], writes=[k_])
            P.op("pool", lambda e, t_=t_, v_=v_: e.memset(t_[64:128, 64:128], v_), reads=[k_], writes=[k_])
        P.op("pool", lambda e: e.memset(mask4[:], 1.0), writes=["mask4"])
        for q in range(4):
            base = -1 if q % 2 == 0 else 0
            P.op("pool", lambda e, q=q, base=base: e.affine_select(
                out=mask4[:, q, :], in_=mask4[:, q, :], pattern=[[1, 128]], compare_op=ALU.is_ge,
                fill=0.0, base=base, channel_multiplier=-1), reads=["mask4"], writes=["mask4"])
        P.op("dve", lambda e: e.tensor_scalar(out=PT[:, pcol["omka"]:pcol["omka"] + 8],
                                              in0=PT[:, pcol["k_a"]:pcol["k_a"] + 8],
                                              scalar1=-1.0, scalar2=1.0, op0=ALU.mult, op1=ALU.add),
             reads=["PT"], writes=["PT"])
        P.op("pool", lambda e: e.memset(Sf[:], 0.0), writes=["Sf"])
        P.op("pool", lambda e: e.memset(Sb[:], 0.0), writes=["Sb"])
        P.op("pool", lambda e: e.memset(hT[:, :, 0:1], 0.0), writes=["hT"])
        P.op("pool", lambda e: e.memset(h1T[:, :, 0:2], 0.0), writes=["h1T"])
        P.op("pool", lambda e: e.memset(cb[:, :, 0:HALO], 0.0), writes=["cb"])
        KBIG0 = [("BIG0", c) for c in range(8)]
        KBIG1 = [("BIG1", c) for c in range(8)]
        KHB0 = [("HB0", c) for c in range(8)]
        KHB1 = [("HB1", c) for c in range(8)]
        for half in range(2):
            P.dma(lambda e, half=half: e.dma_start(out=BIG0[:, 0:2 * D], in_=lora[:, 2 * half:2 * half + 2, :]),
                  writes=KBIG0)
            P.op("dve", lambda e, half=half: e.tensor_copy(out=lorab[:, 2 * half:2 * half + 2, :], in_=BIG0[:, 0:2 * D]),
                 reads=KBIG0, writes=["lorab"])

        stg32 = [(BIG0, KBIG0), (BIG1, KBIG1)]
        stg16 = [(HB0, KHB0), (HB1, KHB1)]
        pp = 0
        cast_engs = ["dve", "pool", "act"]
        for g in groups:
            kcs = g["kc_src"] // 2
            m = g["m"]
            ne = kcs * m
            for half in range(2):
                if g["kind"] == "plain":
                    s32, k32 = stg32[pp % 2]
                    s16, k16 = stg16[pp % 2]
                    ce = cast_engs[pp % 3]
                    pp += 1
                    so = g["off"] + half * ne
                    P.dma(lambda e, s32=s32, so=so, ne=ne: e.dma_start(out=s32[:, 0:ne], in_=wall[:, so:so + ne]),
                          writes=k32)
                    if ce == "act":
                        P.op("act", lambda e, s16=s16, s32=s32, ne=ne: e.activation(out=s16[:, 0:ne], in_=s32[:, 0:ne], func=AF.Copy),
                             reads=k32, writes=k16)
                    else:
                        P.op(ce, lambda e, s16=s16, s32=s32, ne=ne: e.tensor_copy(out=s16[:, 0:ne], in_=s32[:, 0:ne]),
                             reads=k32, writes=k16)
                    P.dma(lambda e, s16=s16, gi=g["idx"], half=half, ne=ne: e.dma_start(
                        out=wsc[gi, :, half * ne:(half + 1) * ne], in_=s16[:, 0:ne]),
                        reads=k16, writes=[("wsc", g["idx"])])
                else:
                    mu_t = mut[pp % 2]; kmu = "mut%d" % (pp % 2)
                    pp += 1
                    so = g["off"] + half * ne
                    c0 = int(g["cols"][0]) - 2048
                    P.dma(lambda e, so=so, ne=ne: e.dma_start(out=BIG0[:, 0:ne], in_=wall[:, so:so + ne]),
                          writes=KBIG0)
                    P.dma(lambda e, mu_t=mu_t, c0=c0, m=m: e.dma_start(out=mu_t[:, 0:m], in_=mub[:, c0:c0 + m]),
                          writes=[kmu])
                    for kk_ in range(kcs):
                        P.op("dve", lambda e, kk_=kk_, m=m, mu_t=mu_t: e.tensor_tensor(
                            out=BIG1[:, kk_ * m:(kk_ + 1) * m], in0=BIG0[:, kk_ * m:(kk_ + 1) * m],
                            in1=mu_t[:, 0:m], op=ALU.mult), reads=KBIG0 + [kmu], writes=KBIG1)
                    P.op("pool", lambda e, ne=ne: e.tensor_copy(out=HB1[:, 0:ne], in_=BIG1[:, 0:ne]),
                         reads=KBIG1, writes=KHB1)
                    P.op("dve", lambda e, ne=ne: e.tensor_tensor(out=HB0[:, 0:ne], in0=BIG0[:, 0:ne],
                                                                in1=BIG1[:, 0:ne], op=ALU.subtract),
                         reads=KBIG0 + KBIG1, writes=KHB0)
                    gi = g["idx"]
                    o1 = half * ne
                    o2 = 8 * m + half * ne
                    P.dma(lambda e, gi=gi, o1=o1, ne=ne: e.dma_start(out=wsc[gi, :, o1:o1 + ne], in_=HB0[:, 0:ne]),
                          reads=KHB0, writes=[("wsc", gi)])
                    P.dma(lambda e, gi=gi, o2=o2, ne=ne: e.dma_start(out=wsc[gi, :, o2:o2 + ne], in_=HB1[:, 0:ne]),
                          reads=KHB1, writes=[("wsc", gi)])

        seq = [g["name"] for _ in range(nt_run) for g in groups]
        issued = []
        st = {"cur": 0, "released": 0}

        def issue_load():
            idx = len(issued)
            g = gidx[seq[idx]]
            i = idx % NSLOT
            slot = ring[i]; key = "ring%d" % i
            ne = g["kc_out"] * g["m"]
            P.dma(lambda e: e.dma_start(out=slot[:, 0:ne], in_=wsc[g["idx"], :, 0:ne]),
                  reads=[("wsc", g["idx"])], writes=[key])
            issued.append((slot, key, g))

        def pump():
            while len(issued) < min(len(seq), st["released"] + NSLOT):
                issue_load()

        def load_group(name):
            cur = st["cur"]
            assert seq[cur] == name, (seq[cur], name)
            pump()
            assert len(issued) > cur, "weight ring too shallow for simultaneously open groups"
            slot, key, g = issued[cur]
            st["cur"] = cur + 1
            m = g["m"]
            def w(kc, c0, c1):
                return slot[:, kc * m + c0:kc * m + c1]
            return w, key, g

        def done_group(n=1):
            st["released"] += n
            pump()

        convo = [BIG0[:, c * N:(c + 1) * N] for c in range(8)]
        yA = [BIG1[:, c * N:(c + 1) * N] for c in range(8)]
        cs_ = [HB0[:, c * N:(c + 1) * N] for c in range(8)]
        oT = cs_
        ybv = [HB1[:, c * N:(c + 1) * N] for c in range(8)]

        def ln_fm(srcs, skeys, ones_t, ones_key, eps, gname, bname, cidx, outs):
            nchunk = len(srcs)
            for i, (s, sk) in enumerate(zip(srcs, skeys)):
                xbt, xbk = Ht[i % 2], "H%d" % (i % 2)
                sqt, sqk = Ht[2 + i % 2], "H%d" % (2 + i % 2)
                P.op("pool", lambda e, s=s, xbt=xbt: e.tensor_copy(out=xbt[:], in_=s), reads=[sk], writes=[xbk])
                P.op("act", lambda e, s=s, sqt=sqt: e.activation(out=sqt[:], in_=s, func=AF.Square), reads=[sk], writes=[sqk])
                P.op("pe", lambda e, xbt=xbt, i=i: e.matmul(pa[:, 0:N], lhsT=ones_t[:], rhs=xbt[:], start=(i == 0), stop=(i == nchunk - 1)),
                     reads=[xbk, ones_key], writes=["pa"])
                P.op("pe", lambda e, sqt=sqt, i=i: e.matmul(pb[:, 0:N], lhsT=ones_t[:], rhs=sqt[:], start=(i == 0), stop=(i == nchunk - 1)),
                     reads=[sqk, ones_key], writes=["pb"])
            m2, var, rstd, nb = Ft[0], Ft[1], Ft[2], Ft[3]
            P.op("act", lambda e: e.activation(out=m2[:], in_=pa[:, 0:N], func=AF.Square), reads=["pa"], writes=["F0"])
            P.op("dve", lambda e: e.tensor_tensor(out=var[:], in0=pb[:, 0:N], in1=m2[:], op=ALU.subtract), reads=["pb", "F0"], writes=["F1"])
            P.op("dve", lambda e: e.tensor_scalar(out=var[:], in0=var[:], scalar1=eps, scalar2=None, op0=ALU.add), reads=["F1"], writes=["F1"])
            P.op("act", lambda e: e.activation(out=var[:], in_=var[:], func=AF.Sqrt), reads=["F1"], writes=["F1"])
            P.op("dve", lambda e: e.reciprocal(out=rstd[:], in_=var[:]), reads=["F1"], writes=["F2"])
            P.op("dve", lambda e: e.scalar_tensor_tensor(out=nb[:], in0=pa[:, 0:N], scalar=-1.0, in1=rstd[:], op0=ALU.mult, op1=ALU.mult),
                 reads=["pa", "F2"], writes=["F3"])
            for i, (s, sk) in enumerate(zip(srcs, skeys)):
                t0, k0 = Ft[4 + i % 2], "F%d" % (4 + i % 2)
                P.op("dve", lambda e, s=s, t0=t0: e.tensor_tensor(out=t0[:], in0=s, in1=rstd[:], op=ALU.mult), reads=[sk, "F2"], writes=[k0])
                P.op("pool", lambda e, t0=t0: e.tensor_tensor(out=t0[:], in0=t0[:], in1=nb[:], op=ALU.add), reads=[k0, "F3"], writes=[k0])
                c = cidx[i]
                for (oap, okey, func) in outs[i]:
                    P.op("act", lambda e, oap=oap, t0=t0, func=func, c=c: e.activation(
                        out=oap, in_=t0[:], func=func, scale=pc(gname, c), bias=pc(bname, c)),
                        reads=[k0, "PT"], writes=[okey])

        def mm_chunk(out_ap, okey, w, wkey, kcn, c0, c1, rhs_fn, rkeys):
            for kc in range(kcn):
                P.op("pe", lambda e, kc=kc: e.matmul(out_ap, lhsT=w(kc, c0, c1), rhs=rhs_fn(kc), start=(kc == 0), stop=(kc == kcn - 1)),
                     reads=[wkey] + list(rkeys), writes=[okey])

        def rhs_h(kc):
            return hT[:, kc, 1:N + 1]

        def rhs_hs(kc):
            if kc < 8:
                return hT[:, kc, 1:N + 1]
            return hT[:, kc - 8, 0:N]

        class StopBuild(Exception):
            pass
        kstop = float(os.environ.get("KSTOP", "99"))

        def ckpt(i):
            if kstop <= i:
                raise StopBuild()

        try:
          ckpt(0)
          for it in range(nt_run):
              t0 = it * N
              P.dma(lambda e, t0=t0: e.dma_start(out=xres[:, :, :], in_=xT[:, t0:t0 + N].rearrange("(c p) n -> p c n", p=128)),
                    writes=["xres"])
              ckpt(1)
              ln_fm([xres[:, c, :] for c in range(8)], ["xres"] * 8, onesln, "onesln", LN_EPS, "ln_in_g", "ln_in_b",
                    list(range(8)),
                    [[(xres[:, c, :], "xres", AF.Identity), (hT[:, c, 1:N + 1], "hT", AF.Identity)] for c in range(8)])
              if it == 0:
                  dump("h", xres[:, :, :], [128, 8, N], ["xres"])
              ckpt(2)
              for half in range(2):
                  wa, ka, _ = load_group("ga%d" % half)
                  wg, kg, _ = load_group("gg%d" % half)
                  for cc in range(4):
                      c = half * 4 + cc
                      pA, kA = next_pm()
                      mm_chunk(pA[:, 0:N], kA, wa, ka, 8, cc * 128, (cc + 1) * 128, rhs_h, ["hT"])
                      pG, kG = next_pm()
                      mm_chunk(pG[:, 0:N], kG, wg, kg, 8, cc * 128, (cc + 1) * 128, rhs_h, ["hT"])
                      sgt, sgk = Ft[6 + c % 2], "F%d" % (6 + c % 2)
                      P.op("act", lambda e, pG=pG, sgt=sgt: e.activation(out=sgt[:], in_=pG[:, 0:N], func=AF.Sigmoid), reads=[kG], writes=[sgk])
                      P.op("dve", lambda e, pA=pA, sgt=sgt, c=c: e.tensor_tensor(out=cb[:, c, HALO:HALO + N], in0=pA[:, 0:N], in1=sgt[:], op=ALU.mult),
                           reads=[kA, sgk], writes=[("cb", c)])
                      if it == 0 and c == 0 and "dbgA" in dbg_names:
                          P.op("act", lambda e, pA=pA: e.activation(out=Ft[10][:], in_=pA[:, 0:N], func=AF.Copy), reads=[kA], writes=["F10"])
                          dump("dbgA", Ft[10][:], [128, N], ["F10"])
                          dump("dbgS", sgt[:], [128, N], [sgk])
                  done_group(2)
              if it == 0:
                  dump("hT", hT[:, :, :], [128, 8, N + 1], ["hT"], BF16)
                  dump("cb", cb[:, :, :], [128, 8, N + HALO], [("cb", c) for c in range(8)], BF16)
              ckpt(3)
              for c in range(8):
                  dg, dk = diag[0], "diag0"
                  for j in range(CONVK):
                      P.op("pool", lambda e, dg=dg, j=j, c=c: e.tensor_scalar(
                          out=dg[:, j, :], in0=ident[:], scalar1=pc("conv_dw", j * 8 + c), scalar2=0.0, op0=ALU.mult, op1=ALU.add),
                          reads=["ident", "PT"], writes=[dk])
                  pC, kC = next_pm()
                  for j in range(CONVK):
                      P.op("pe", lambda e, dg=dg, j=j, c=c, pC=pC: e.matmul(pC[:, 0:N], lhsT=dg[:, j, :], rhs=cb[:, c, j:j + N], start=(j == 0), stop=(j == CONVK - 1)),
                           reads=[dk, ("cb", c), "cb"], writes=[kC])
                  P.op("act", lambda e, c=c, pC=pC: e.activation(out=convo[c], in_=pC[:, 0:N], func=AF.Identity, bias=pc("conv_dw_b", c)),
                       reads=[kC, "PT"], writes=[("BIG0", c)])
                  P.op("pool", lambda e, c=c: e.tensor_copy(out=cb[:, c, 0:HALO], in_=cb[:, c, N:N + HALO]), reads=[("cb", c)], writes=[("cb", c)])
              if it == 0:
                  dump("convo", BIG0[:, :], [128, 8 * N], KBIG0)
              ckpt(4)
              ln_fm(convo, [("BIG0", c) for c in range(8)], onesln, "onesln", LN_EPS, "conv_ln_g", "conv_ln_b", list(range(8)),
                    [[(cs_[c], ("HB0", c), AF.Silu)] for c in range(8)])
              for half in range(2):
                  wc, kc_, _ = load_group("co%d" % half)
                  wga, kga, _ = load_group("gA%d" % half)
                  for cc in range(4):
                      c = half * 4 + cc
                      pA, kA = next_pm()
                      mm_chunk(pA[:, 0:N], kA, wc, kc_, 8, cc * 128, (cc + 1) * 128, lambda kc: cs_[kc], [("HB0", k) for k in range(8)])
                      pG, kG = next_pm()
                      mm_chunk(pG[:, 0:N], kG, wga, kga, 8, cc * 128, (cc + 1) * 128, rhs_h, ["hT"])
                      sgt, sgk = Ft[6 + c % 2], "F%d" % (6 + c % 2)
                      P.op("act", lambda e, pG=pG, sgt=sgt, c=c: e.activation(out=sgt[:], in_=pG[:, 0:N], func=AF.Sigmoid, bias=pc("b_gate", c)),
                           reads=[kG, "PT"], writes=[sgk])
                      P.op("dve", lambda e, pA=pA, sgt=sgt, c=c: e.tensor_tensor(out=yA[c], in0=pA[:, 0:N], in1=sgt[:], op=ALU.mult),
                           reads=[kA, sgk], writes=[("BIG1", c)])
                  done_group(2)
              if it == 0:
                  dump("yA", BIG1[:, :], [128, 8 * N], KBIG1)

              ckpt(5)
              wl, kl, _ = load_group("lr0")
              pW, kW = next_pm()
              mm_chunk(pW[0:64, 0:N], kW, wl, kl, 16, 0, 64, rhs_hs, ["hT"])
              P.op("act", lambda e, pW=pW: e.activation(out=tw[0:64, :], in_=pW[0:64, 0:N], func=AF.Tanh), reads=[kW], writes=["tw"])
              pX, kX = next_pm()
              mm_chunk(pX[0:64, 0:N], kX, wl, kl, 16, 64, 128, rhs_hs, ["hT"])
              P.op("act", lambda e, pX=pX: e.activation(out=xab[0:64, :], in_=pX[0:64, 0:N], func=AF.Copy), reads=[kX], writes=["xab"])
              done_group(1)
              wl, kl, _ = load_group("lr1")
              pW, kW = next_pm()
              mm_chunk(pW[:, 0:N], kW, wl, kl, 16, 0, 128, rhs_hs, ["hT"])
              P.op("act", lambda e, pW=pW: e.activation(out=sg0[:], in_=pW[:, 0:N], func=AF.Sigmoid), reads=[kW], writes=["sg0"])
              pX, kX = next_pm()
              mm_chunk(pX[0:32, 0:N], kX, wl, kl, 16, 128, 160, rhs_hs, ["hT"])
              P.op("act", lambda e, pX=pX: e.activation(out=sg1[0:32, :], in_=pX[0:32, 0:N], func=AF.Sigmoid), reads=[kX], writes=["sg1"])
              done_group(1)

              for grp in range(4):
                  hps = [2 * grp, 2 * grp + 1]
                  wr_, kr_, _ = load_group("r%d" % grp)
                  wk_, kk__, _ = load_group("k%d" % grp)
                  wv_, kv_, _ = load_group("v%d" % grp)
                  for gi_, hp in enumerate(hps):
                      co = gi_ * 128
                      r_, k_, sgw, ai = Ft[0], Ft[1], Ft[2], Ft[3]
                      cs, csm, Ep, En, Em, EC = Ft[4], Ft[5], Ft[6], Ft[7], Ft[8], Ft[9]
                      kkn, b_, kf, kp, nrm = Ft[10], Ft[11], Ft[12], Ft[13], Ft[14]
                      ksq = Ht[4]
                      pR, kR = next_pm()
                      mm_chunk(pR[:, 0:N], kR, wr_, kr_, 16, co, co + 128, rhs_hs, ["hT"])
                      P.op("act", lambda e, pR=pR: e.activation(out=r_[:], in_=pR[:, 0:N], func=AF.Copy), reads=[kR], writes=["F0"])
                      pK, kK = next_pm()
                      mm_chunk(pK[:, 0:N], kK, wk_, kk__, 16, co, co + 128, rhs_hs, ["hT"])
                      P.op("act", lambda e, pK=pK: e.activation(out=k_[:], in_=pK[:, 0:N], func=AF.Copy), reads=[kK], writes=["F1"])
                      pV, kV = next_pm()
                      mm_chunk(pV[:, 0:N], kV, wv_, kv_, 16, co, co + 128, rhs_hs, ["hT"])
                      P.op("act", lambda e, pV=pV, gi_=gi_: e.activation(out=vTf[gi_][:], in_=pV[:, 0:N], func=AF.Copy), reads=[kV], writes=["vTf%d" % gi_])
                      P.op("pool", lambda e, gi_=gi_: e.tensor_copy(out=vb[gi_][:], in_=vTf[gi_][:]), reads=["vTf%d" % gi_], writes=["vb%d" % gi_])
                      pZ, kZ = next_pm()
                      P.op("pe", lambda e, pZ=pZ, hp=hp: e.matmul(pZ[:, 0:N], lhsT=lorab[0:64, 0, hp * 128:(hp + 1) * 128], rhs=tw[0:64, :], start=True, stop=True),
                           reads=["lorab", "tw"], writes=[kZ])
                      P.op("act", lambda e, pZ=pZ, hp=hp: e.activation(out=sgw[:], in_=pZ[:, 0:N], func=AF.Sigmoid, bias=pc("w0", hp)),
                           reads=[kZ, "PT"], writes=["F2"])
                      pZ2, kZ2 = next_pm()
                      P.op("pe", lambda e, pZ2=pZ2, hp=hp: e.matmul(pZ2[:, 0:N], lhsT=lorab[0:64, 1, hp * 128:(hp + 1) * 128], rhs=xab[0:64, :], start=True, stop=True),
                           reads=["lorab", "xab"], writes=[kZ2])
                      P.op("act", lambda e, pZ2=pZ2, hp=hp: e.activation(out=ai[:], in_=pZ2[:, 0:N], func=AF.Sigmoid, bias=pc("a0", hp)),
                           reads=[kZ2, "PT"], writes=["F3"])
                      for ch in range(NCH):
                          sl = slice(ch * CH, (ch + 1) * CH)
                          P.op("dve", lambda e, sl=sl: e.tensor_tensor_scan(out=cs[:, sl], data0=ones_f[:], data1=sgw[:, sl], initial=0.0,
                                                                           op0=ALU.mult, op1=ALU.add),
                               reads=["F2", "ones_f"], writes=["F4"])
                      P.op("pool", lambda e: e.tensor_tensor(out=csm[:], in0=cs[:], in1=sgw[:], op=ALU.subtract), reads=["F4", "F2"], writes=["F5"])
                      P.op("act", lambda e: e.activation(out=Ep[:], in_=cs[:], func=AF.Exp, scale=-C0), reads=["F4"], writes=["F6"])
                      P.op("act", lambda e: e.activation(out=En[:], in_=cs[:], func=AF.Exp, scale=C0), reads=["F4"], writes=["F7"])
                      P.op("act", lambda e: e.activation(out=Em[:], in_=csm[:], func=AF.Exp, scale=-C0), reads=["F5"], writes=["F8"])
                      for ch in range(NCH):
                          ce = (ch + 1) * CH - 1
                          P.op("dve", lambda e, ch=ch, ce=ce: e.tensor_scalar(out=nbC[:, ch:ch + 1], in0=cs[:, ce:ce + 1], scalar1=-C0, scalar2=None, op0=ALU.mult),
                               reads=["F4"], writes=["nbC"])
                          P.op("pool", lambda e, ch=ch, ce=ce, gi_=gi_: e.tensor_copy(out=PCc[gi_][:, ch:ch + 1], in_=Ep[:, ce:ce + 1]),
                               reads=["F6"], writes=["PCc%d" % gi_])
                      for ch in range(NCH):
                          sl = slice(ch * CH, (ch + 1) * CH)
                          P.op("act", lambda e, sl=sl, ch=ch: e.activation(out=EC[:, sl], in_=cs[:, sl], func=AF.Exp, scale=C0, bias=nbC[:, ch:ch + 1]),
                               reads=["F4", "nbC"], writes=["F9"])
                      P.op("act", lambda e, hp=hp: e.activation(out=ksq[:], in_=k_[:], func=AF.Square, scale=pc("k_k", hp)), reads=["F1", "PT"], writes=["H4"])
                      pS, kS = next_pm()
                      P.op("pe", lambda e, pS=pS: e.matmul(pS[:, 0:N], lhsT=bo1[:], rhs=ksq[:], start=True, stop=True), reads=["bo1", "H4"], writes=[kS])
                      P.op("act", lambda e, pS=pS: e.activation(out=nrm[:], in_=pS[:, 0:N], func=AF.Sqrt), reads=[kS], writes=["F14"])
                      P.op("dve", lambda e: e.tensor_scalar(out=nrm[:], in0=nrm[:], scalar1=1e-12, scalar2=None, op0=ALU.max), reads=["F14"], writes=["F14"])
                      P.op("dve", lambda e: e.reciprocal(out=nrm[:], in_=nrm[:]), reads=["F14"], writes=["F14"])
                      P.op("dve", lambda e, hp=hp: e.scalar_tensor_tensor(out=kkn[:], in0=k_[:], scalar=pc("k_k", hp), in1=nrm[:], op0=ALU.mult, op1=ALU.mult),
                           reads=["F1", "F14", "PT"], writes=["F10"])
                      P.op("dve", lambda e, gi_=gi_: e.scalar_tensor_tensor(out=AR[gi_][:, 0, :], in0=kkn[:], scalar=-1.0, in1=Em[:], op0=ALU.mult, op1=ALU.mult),
                           reads=["F10", "F8"], writes=["AR%d" % gi_])
                      P.op("pool", lambda e, gi_=gi_: e.tensor_tensor(out=AR[gi_][:, 1, :], in0=r_[:], in1=Ep[:], op=ALU.mult),
                           reads=["F0", "F6"], writes=["AR%d" % gi_])
                      P.op("pool", lambda e: e.tensor_tensor(out=b_[:], in0=kkn[:], in1=ai[:], op=ALU.mult), reads=["F10", "F3"], writes=["F11"])
                      P.op("dve", lambda e, gi_=gi_: e.tensor_tensor(out=BK[gi_][:, 0, :], in0=b_[:], in1=En[:], op=ALU.mult), reads=["F11", "F7"], writes=["BK%d" % gi_])
                      P.op("pool", lambda e, gi_=gi_: e.tensor_tensor(out=BKh[gi_][:, 0, :], in0=b_[:], in1=EC[:], op=ALU.mult), reads=["F11", "F9"], writes=["BKh%d" % gi_])
                      P.op("dve", lambda e, hp=hp: e.tensor_scalar(out=kf[:], in0=ai[:], scalar1=pc("k_a", hp), scalar2=pc("omka", hp), op0=ALU.mult, op1=ALU.add),
                           reads=["F3", "PT"], writes=["F12"])
                      P.op("pool", lambda e: e.tensor_tensor(out=kp[:], in0=k_[:], in1=kf[:], op=ALU.mult), reads=["F1", "F12"], writes=["F13"])
                      P.op("dve", lambda e, gi_=gi_: e.tensor_tensor(out=BK[gi_][:, 1, :], in0=kp[:], in1=En[:], op=ALU.mult), reads=["F13", "F7"], writes=["BK%d" % gi_])
                      P.op("pool", lambda e, gi_=gi_: e.tensor_tensor(out=BKh[gi_][:, 1, :], in0=kp[:], in1=EC[:], op=ALU.mult), reads=["F13", "F9"], writes=["BKh%d" % gi_])
                      P.op("dve", lambda e, hp=hp, gi_=gi_: e.scalar_tensor_tensor(out=rkb[gi_][:], in0=r_[:], scalar=pc("r_k", hp), in1=kp[:], op0=ALU.mult, op1=ALU.mult),
                           reads=["F0", "F13", "PT"], writes=["rkb%d" % gi_])

                  done_group(3)
                  ckpt(6)
                  for ch in range(NCH):
                      sl = slice(ch * CH, (ch + 1) * CH)
                      for gi_, hp in enumerate(hps):
                          pT, kT = next_pr()
                          srcs = [(AR[gi_][:, 0, sl], "AR%d" % gi_), (BKh[gi_][:, 0, sl], "BKh%d" % gi_),
                                  (BKh[gi_][:, 1, sl], "BKh%d" % gi_), (vb[gi_][:, sl], "vb%d" % gi_)]
                          for q, (sap, skey) in enumerate(srcs):
                              P.op("pe", lambda e, pT=pT, q=q, sap=sap: e.matmul(pT[:, q * 128:(q + 1) * 128], lhsT=sap, rhs=ident[:], start=True, stop=True),
                                   reads=[skey, "ident"], writes=[kT])
                          P.op("act", lambda e, pT=pT, gi_=gi_: e.activation(out=TM[gi_][:, :, :], in_=pT[:, :].rearrange("p (q n) -> p q n", q=4), func=AF.Copy),
                               reads=[kT], writes=["TM%d" % gi_])
                      ckpt(6.1)
                      for gi_, hp in enumerate(hps):
                          for h in range(2):
                              hs = slice(h * 64, (h + 1) * 64)
                              pQ, kQ = next_pr()
                              P.op("pe", lambda e, pQ=pQ, gi_=gi_, hs=hs: e.matmul(pQ[:, 0:256].rearrange("p (q n) -> p q n", q=2), lhsT=BK[gi_][hs, 0, sl], rhs=AR[gi_][hs, :, sl], start=True, stop=True),
                                   reads=["BK%d" % gi_, "AR%d" % gi_], writes=[kQ])
                              P.op("pe", lambda e, pQ=pQ, gi_=gi_, hs=hs: e.matmul(pQ[:, 256:512].rearrange("p (q n) -> p q n", q=2), lhsT=BK[gi_][hs, 1, sl], rhs=AR[gi_][hs, :, sl], start=True, stop=True),
                                   reads=["BK%d" % gi_, "AR%d" % gi_], writes=[kQ])
                              P.op("dve", lambda e, pQ=pQ, gi_=gi_, h=h: e.tensor_tensor(out=QKs[gi_][h][:, :, :], in0=pQ[:, :].rearrange("p (q n) -> p q n", q=4), in1=mask4[:, :, :], op=ALU.mult),
                                   reads=[kQ, "mask4"], writes=["QKs%d_%d" % (gi_, h)])
                          pL, kL = next_pr()
                          for h in range(2):
                              P.op("pe", lambda e, pL=pL, gi_=gi_, h=h: e.matmul(pL[:, h * 128:(h + 1) * 128], lhsT=QKs[gi_][h][:, 0, :], rhs=ident[:], start=True, stop=True),
                                   reads=["QKs%d_%d" % (gi_, h), "ident"], writes=[kL])
                              P.op("pool", lambda e, gi_=gi_, h=h: e.tensor_copy(out=NMb[gi_][0][:, h, :], in_=QKs[gi_][h][:, 0, :]),
                                   reads=["QKs%d_%d" % (gi_, h)], writes=["NMb%d_0" % gi_])
                              P.op("pool", lambda e, gi_=gi_, h=h: e.tensor_tensor(out=Qb[gi_][0][:, h, :], in0=QKs[gi_][h][:, 0, :], in1=ident[:], op=ALU.add),
                                   reads=["QKs%d_%d" % (gi_, h), "ident"], writes=["Qb%d_0" % gi_])
                          P.op("act", lambda e, pL=pL, gi_=gi_: e.activation(out=NMb[gi_][0][:, 2:4, :], in_=pL[:, 0:256].rearrange("p (q n) -> p q n", q=2), func=AF.Copy),
                               reads=[kL], writes=["NMb%d_0" % gi_])
                      ckpt(6.2)
                      NLV = 6
                      for lv in range(NLV):
                          cur, nxt = lv % 2, (lv + 1) % 2
                          last = (lv == NLV - 1)
                          for gi_, hp in enumerate(hps):
                              pN, kN = next_pr()
                              kcur = "NMb%d_%d" % (gi_, cur)
                              knxt = "NMb%d_%d" % (gi_, nxt)
                              for h in range(2):
                                  if not last:
                                      P.op("pe", lambda e, pN=pN, gi_=gi_, h=h, cur=cur: e.matmul(pN[:, h * 128:(h + 1) * 128], lhsT=NMb[gi_][cur][:, 2 + h, :], rhs=NMb[gi_][cur][:, h, :], start=True, stop=True),
                                           reads=[kcur], writes=[kN])
                                  P.op("pe", lambda e, pN=pN, gi_=gi_, h=h, cur=cur: e.matmul(pN[:, (2 + h) * 128:(3 + h) * 128], lhsT=NMb[gi_][cur][:, h, :], rhs=NMb[gi_][cur][:, 2 + h, :], start=True, stop=True),
                                       reads=[kcur], writes=[kN])
                              if not last:
                                  P.op("act", lambda e, pN=pN, gi_=gi_, nxt=nxt: e.activation(out=NMb[gi_][nxt][:, :, :], in_=pN[:, :].rearrange("p (q n) -> p q n", q=4), func=AF.Copy),
                                       reads=[kN], writes=[knxt])
                              else:
                                  P.op("act", lambda e, pN=pN, gi_=gi_, nxt=nxt: e.activation(out=NMb[gi_][nxt][:, 2:4, :], in_=pN[:, 256:512].rearrange("p (q n) -> p q n", q=2), func=AF.Copy),
                                       reads=[kN], writes=[knxt])
                          for gi_, hp in enumerate(hps):
                              pQ2, kQ2 = next_pr()
                              knxt = "NMb%d_%d" % (gi_, nxt)
                              kq0 = "Qb%d_%d" % (gi_, cur)
                              kq1 = "Qb%d_%d" % (gi_, nxt)
                              for h in range(2):
                                  P.op("pe", lambda e, pQ2=pQ2, gi_=gi_, h=h, nxt=nxt, cur=cur: e.matmul(pQ2[:, h * 128:(h + 1) * 128], lhsT=NMb[gi_][nxt][:, 2 + h, :], rhs=Qb[gi_][cur][:, h, :], start=True, stop=True),
                                       reads=[knxt, kq0], writes=[kQ2])
                              P.op("dve", lambda e, pQ2=pQ2, gi_=gi_, nxt=nxt, cur=cur: e.tensor_tensor(out=Qb[gi_][nxt][:, :, :], in0=pQ2[:, 0:256].rearrange("p (q n) -> p q n", q=2), in1=Qb[gi_][cur][:, :, :], op=ALU.add),
                                   reads=[kQ2, kq0], writes=[kq1])
                      qf = NLV % 2
                      ckpt(6.3)
                      for gi_, hp in enumerate(hps):
                          kq = "Qb%d_%d" % (gi_, qf)
                          pG1, kG1 = next_pr()
                          for h in range(2):
                              hs = slice(h * 64, (h + 1) * 64)
                              P.op("pe", lambda e, pG1=pG1, gi_=gi_, h=h, hs=hs: e.matmul(pG1[hs, 0:128], lhsT=TM[gi_][:, 0, hs], rhs=Qb[gi_][qf][:, h, :], start=True, stop=True),
                                   reads=["TM%d" % gi_, kq], writes=[kG1])
                              P.op("pe", lambda e, pG1=pG1, gi_=gi_, h=h, hs=hs: e.matmul(pG1[:, 128 + h * 64:128 + (h + 1) * 64], lhsT=QKs[gi_][h][:, 2, :], rhs=TM[gi_][:, 3, hs], start=True, stop=True),
                                   reads=["TM%d" % gi_, "QKs%d_%d" % (gi_, h)], writes=[kG1])
                          P.op("act", lambda e, pG1=pG1, gi_=gi_: e.activation(out=G1Tb[gi_][:], in_=pG1[:, 0:128], func=AF.Copy), reads=[kG1], writes=["G1Tb%d" % gi_])
                          P.op("act", lambda e, pG1=pG1, gi_=gi_: e.activation(out=Xb[gi_][:], in_=pG1[:, 128:256], func=AF.Copy), reads=[kG1], writes=["Xb%d" % gi_])
                          pG2, kG2 = next_pr()
                          for h in range(2):
                              hs = slice(h * 64, (h + 1) * 64)
                              P.op("pe", lambda e, pG2=pG2, gi_=gi_, h=h, hs=hs: e.matmul(pG2[:, h * 64:(h + 1) * 64], lhsT=Qb[gi_][qf][:, h, :], rhs=Xb[gi_][:, hs], start=True, stop=True),
                                   reads=[kq, "Xb%d" % gi_], writes=[kG2])
                          P.op("act", lambda e, pG2=pG2, gi_=gi_: e.activation(out=G2f[gi_][:], in_=pG2[:, 0:128], func=AF.Copy), reads=[kG2], writes=["G2f%d" % gi_])
                      ckpt(6.4)
                      for gi_, hp in enumerate(hps):
                          pU, kU = next_pr()
                          P.op("pe", lambda e, pU=pU, gi_=gi_, hp=hp: e.matmul(pU[:, 0:128], lhsT=G1Tb[gi_][:, :], rhs=Sb[:, hp, :], start=True, stop=True),
                               reads=["G1Tb%d" % gi_, ("Sb", hp)], writes=[kU])
                          P.op("dve", lambda e, pU=pU, gi_=gi_: e.tensor_tensor(out=Ub[gi_][:], in0=pU[:, 0:128], in1=G2f[gi_][:], op=ALU.add),
                               reads=[kU, "G2f%d" % gi_], writes=["Ub%d" % gi_])
                      ckpt(6.5)
                      for gi_, hp in enumerate(hps):
                          pY, kY = next_pr()
                          P.op("pe", lambda e, pY=pY, gi_=gi_, hp=hp: e.matmul(pY[:, 0:128], lhsT=Sb[:, hp, :], rhs=AR[gi_][:, 1, sl], start=True, stop=False),
                               reads=[("Sb", hp), "AR%d" % gi_], writes=[kY])
                          for h in range(2):
                              hs = slice(h * 64, (h + 1) * 64)
                              P.op("pe", lambda e, pY=pY, gi_=gi_, hs=hs, h=h: e.matmul(pY[hs, 0:128], lhsT=Ub[gi_][:, hs], rhs=QKs[gi_][h][:, 1, :], start=False, stop=False),
                                   reads=["Ub%d" % gi_, "QKs%d_%d" % (gi_, h)], writes=[kY])
                              P.op("pe", lambda e, pY=pY, gi_=gi_, hs=hs, h=h: e.matmul(pY[hs, 0:128], lhsT=TM[gi_][:, 3, hs], rhs=QKs[gi_][h][:, 3, :], start=False, stop=True),
                                   reads=["TM%d" % gi_, "QKs%d_%d" % (gi_, h)], writes=[kY])
                          ckpt(6.6)
                          P.op("pe", lambda e, pY=pY, gi_=gi_: e.matmul(pY[:, 128:256], lhsT=TM[gi_][:, 1, :], rhs=Ub[gi_][:, :], start=True, stop=False),
                               reads=["TM%d" % gi_, "Ub%d" % gi_], writes=[kY])
                          P.op("pe", lambda e, pY=pY, gi_=gi_: e.matmul(pY[:, 128:256], lhsT=TM[gi_][:, 2, :], rhs=TM[gi_][:, 3, :], start=False, stop=True),
                               reads=["TM%d" % gi_], writes=[kY])
                          ckpt(6.7)
                          P.op("act", lambda e, pY=pY, gi_=gi_: e.activation(out=Yf[gi_][:, sl], in_=pY[:, 0:128], func=AF.Copy), reads=[kY], writes=["Yf%d" % gi_])
                          ckpt(6.8)
                          P.op("act", lambda e, pY=pY, gi_=gi_: e.activation(out=Stmp[gi_][:], in_=pY[:, 128:256], func=AF.Copy),
                               reads=[kY], writes=["Stmp%d" % gi_])
                          ckpt(6.85)
                          P.op("pool", lambda e, gi_=gi_: e.tensor_tensor(out=Stmp[gi_][:], in0=Stmp[gi_][:], in1=bo1[:], op=ALU.mult),
                               reads=["Stmp%d" % gi_, "bo1"], writes=["Stmp%d" % gi_])
                          ckpt(6.87)
                          P.op("dve", lambda e, gi_=gi_, hp=hp, ch=ch: e.scalar_tensor_tensor(
                              out=Sf[:, hp, :], in0=Sf[:, hp, :], scalar=PCc[gi_][:, ch:ch + 1], in1=Stmp[gi_][:], op0=ALU.mult, op1=ALU.add),
                               reads=["Stmp%d" % gi_, ("Sf", hp), "PCc%d" % gi_], writes=[("Sf", hp)])
                          ckpt(6.9)
                          P.op("pool", lambda e, hp=hp: e.tensor_copy(out=Sb[:, hp, :], in_=Sf[:, hp, :]), reads=[("Sf", hp)], writes=[("Sb", hp)])

                  ckpt(7)
                  for gi_, hp in enumerate(hps):
                      ybf, ysq = Ht[0], Ht[2]
                      P.op("pool", lambda e, gi_=gi_: e.tensor_copy(out=ybf[:], in_=Yf[gi_][:]), reads=["Yf%d" % gi_], writes=["H0"])
                      P.op("act", lambda e, gi_=gi_: e.activation(out=ysq[:], in_=Yf[gi_][:], func=AF.Square), reads=["Yf%d" % gi_], writes=["H2"])
                      P.op("pe", lambda e: e.matmul(pa[:, 0:N], lhsT=bo64[:], rhs=ybf[:], start=True, stop=True), reads=["bo64", "H0"], writes=["pa"])
                      P.op("pe", lambda e: e.matmul(pb[:, 0:N], lhsT=bo64[:], rhs=ysq[:], start=True, stop=True), reads=["bo64", "H2"], writes=["pb"])
                      m2, var, rstd, t1 = Ft[0], Ft[1], Ft[2], Ft[3]
                      P.op("act", lambda e: e.activation(out=m2[:], in_=pa[:, 0:N], func=AF.Square), reads=["pa"], writes=["F0"])
                      P.op("dve", lambda e: e.tensor_tensor(out=var[:], in0=pb[:, 0:N], in1=m2[:], op=ALU.subtract), reads=["pb", "F0"], writes=["F1"])
                      P.op("dve", lambda e: e.tensor_scalar(out=var[:], in0=var[:], scalar1=GN_EPS, scalar2=None, op0=ALU.add), reads=["F1"], writes=["F1"])
                      P.op("act", lambda e: e.activation(out=var[:], in_=var[:], func=AF.Sqrt), reads=["F1"], writes=["F1"])
                      P.op("dve", lambda e: e.reciprocal(out=rstd[:], in_=var[:]), reads=["F1"], writes=["F2"])
                      P.op("dve", lambda e, gi_=gi_: e.tensor_tensor(out=t1[:], in0=Yf[gi_][:], in1=pa[:, 0:N], op=ALU.subtract), reads=["Yf%d" % gi_, "pa"], writes=["F3"])
                      P.op("pool", lambda e: e.tensor_tensor(out=t1[:], in0=t1[:], in1=rstd[:], op=ALU.mult), reads=["F3", "F2"], writes=["F3"])
                      P.op("dve", lambda e, hp=hp: e.tensor_scalar(out=t1[:], in0=t1[:], scalar1=pc("lnx_g", hp), scalar2=pc("lnx_b", hp), op0=ALU.mult, op1=ALU.add),
                           reads=["F3", "PT"], writes=["F3"])
                      pBn, kBn = next_pm()
                      P.op("pe", lambda e, pBn=pBn, gi_=gi_: e.matmul(pBn[:, 0:N], lhsT=bo1[:], rhs=rkb[gi_][:], start=True, stop=True), reads=["bo1", "rkb%d" % gi_], writes=[kBn])
                      t2 = Ft[4]
                      P.op("dve", lambda e, pBn=pBn, gi_=gi_: e.tensor_tensor(out=t2[:], in0=pBn[:, 0:N], in1=vTf[gi_][:], op=ALU.mult), reads=[kBn, "vTf%d" % gi_], writes=["F4"])
                      P.op("pool", lambda e: e.tensor_tensor(out=t1[:], in0=t1[:], in1=t2[:], op=ALU.add), reads=["F3", "F4"], writes=["F3"])
                      pGt, kGt = next_pm()
                      P.op("pe", lambda e, pGt=pGt, hp=hp: e.matmul(pGt[:, 0:N], lhsT=lorab[:, 2, hp * 128:(hp + 1) * 128], rhs=sg0[:], start=True, stop=False), reads=["lorab", "sg0"], writes=[kGt])
                      P.op("pe", lambda e, pGt=pGt, hp=hp: e.matmul(pGt[:, 0:N], lhsT=lorab[0:32, 3, hp * 128:(hp + 1) * 128], rhs=sg1[0:32, :], start=False, stop=True), reads=["lorab", "sg1"], writes=[kGt])
                      P.op("dve", lambda e, pGt=pGt, hp=hp: e.tensor_tensor(out=oT[hp], in0=pGt[:, 0:N], in1=t1[:], op=ALU.mult), reads=[kGt, "F3"], writes=[("HB0", hp)])
              if it == 0:
                  dump("oT", HB0[:, :], [128, 8 * N], KHB0, BF16)

              ckpt(8)
              for half in range(2):
                  wro, kro, _ = load_group("ro%d" % half)
                  wgb, kgb, _ = load_group("gB%d" % half)
                  for cc in range(4):
                      c = half * 4 + cc
                      pA, kA = next_pm()
                      mm_chunk(pA[:, 0:N], kA, wro, kro, 8, cc * 128, (cc + 1) * 128, lambda kc: oT[kc], [("HB0", k) for k in range(8)])
                      pG, kG = next_pm()
                      mm_chunk(pG[:, 0:N], kG, wgb, kgb, 8, cc * 128, (cc + 1) * 128, rhs_h, ["hT"])
                      sgt, sgk = Ft[6 + c % 2], "F%d" % (6 + c % 2)
                      tt, tk = Ft[8 + c % 2], "F%d" % (8 + c % 2)
                      P.op("act", lambda e, pG=pG, sgt=sgt, c=c: e.activation(out=sgt[:], in_=pG[:, 0:N], func=AF.Sigmoid, bias=pc("b_gate", 8 + c)),
                           reads=[kG, "PT"], writes=[sgk])
                      P.op("dve", lambda e, pA=pA, sgt=sgt, tt=tt: e.tensor_tensor(out=tt[:], in0=pA[:, 0:N], in1=sgt[:], op=ALU.mult),
                           reads=[kA, sgk], writes=[tk])
                      P.op("pool", lambda e, tt=tt, c=c: e.tensor_tensor(out=ybv[c], in0=tt[:], in1=yA[c], op=ALU.add),
                           reads=[tk, ("BIG1", c)], writes=[("HB1", c)])
                  done_group(2)
              P.op("pool", lambda e: e.tensor_copy(out=hT[:, :, 0:1], in_=hT[:, :, N:N + 1]), reads=["hT"], writes=["hT"])

              for half in range(2):
                  wwo, kwo, _ = load_group("wo%d" % half)
                  for cc in range(4):
                      c = half * 4 + cc
                      pA, kA = next_pm()
                      mm_chunk(pA[:, 0:N], kA, wwo, kwo, 8, cc * 128, (cc + 1) * 128, lambda kc: ybv[kc], [("HB1", k) for k in range(8)])
                      P.op("dve", lambda e, pA=pA, c=c: e.scalar_tensor_tensor(out=xres[:, c, :], in0=xres[:, c, :], scalar=ALPHA, in1=pA[:, 0:N], op0=ALU.mult, op1=ALU.add),
                           reads=[kA, "xres"], writes=["xres"])
                  done_group(1)
              ln_fm([xres[:, c, :] for c in range(8)], ["xres"] * 8, onesln, "onesln", LN_EPS, "ln1_g", "ln1_b", list(range(8)),
                    [[(xres[:, c, :], "xres", AF.Identity), (h1T[:, c, 2:N + 2], "h1T", AF.Identity)] for c in range(8)])
              if it == 0:
                  dump("h1", xres[:, :, :], [128, 8, N], ["xres"])

              ckpt(9)
              NW = N + 2
              for j in range(11):
                  wfu, kfu, _ = load_group("fu%d" % j)
                  for ff in range(2):
                      f = 2 * j + ff
                      outs = []
                      for part in range(2):
                          ci = part * 22 + f
                          pU, kU = next_pm()
                          mm_chunk(pU[:, 0:NW], kU, wfu, kfu, 8, part * 256 + ff * 128, part * 256 + (ff + 1) * 128,
                                   lambda kc: h1T[:, kc, 0:NW], ["h1T"])
                          ta, tka = Ft[(4 * f + 2 * part) % 12], "F%d" % ((4 * f + 2 * part) % 12)
                          tb, tkb = Ft[(4 * f + 2 * part + 1) % 12], "F%d" % ((4 * f + 2 * part + 1) % 12)
                          P.op("act", lambda e, pU=pU, ta=ta, ci=ci: e.activation(out=ta[:], in_=pU[:, 2:NW], func=AF.Identity, scale=pc("ffn_dw", 2 * 44 + ci)),
                               reads=[kU, "PT"], writes=[tka])
                          P.op("dve", lambda e, pU=pU, ta=ta, tb=tb, ci=ci: e.scalar_tensor_tensor(out=tb[:], in0=pU[:, 1:NW - 1], scalar=pc("ffn_dw", 1 * 44 + ci), in1=ta[:], op0=ALU.mult, op1=ALU.add),
                               reads=[kU, tka, "PT"], writes=[tkb])
                          P.op("dve", lambda e, pU=pU, ta=ta, tb=tb, ci=ci: e.scalar_tensor_tensor(out=ta[:], in0=pU[:, 0:NW - 2], scalar=pc("ffn_dw", 0 * 44 + ci), in1=tb[:], op0=ALU.mult, op1=ALU.add),
                               reads=[kU, tkb, "PT"], writes=[tka])
                          outs.append((ta, tka, tb, tkb))
                      (la, lka, _, _), (ga_, gka, gb_, gkb) = outs
                      P.op("act", lambda e, ga_=ga_, gb_=gb_: e.activation(out=gb_[:], in_=ga_[:], func=AF.Silu), reads=[gka], writes=[gkb])
                      P.op("pool", lambda e, la=la, gb_=gb_, f=f: e.tensor_tensor(out=act[:, f, :], in0=la[:], in1=gb_[:], op=ALU.mult),
                           reads=[lka, gkb], writes=[("act", f)])
                  done_group(1)
              P.op("pool", lambda e: e.tensor_copy(out=h1T[:, :, 0:2], in_=h1T[:, :, N:N + 2]), reads=["h1T"], writes=["h1T"])
              for m in range(8):
                  wfd, kfd, _ = load_group("fd%d" % m)
                  pA, kA = next_pm()
                  mm_chunk(pA[:, 0:N], kA, wfd, kfd, 22, 0, 128, lambda kc: act[:, kc, :], [("act", k) for k in range(22)])
                  P.op("dve", lambda e, pA=pA, m=m: e.scalar_tensor_tensor(out=xres[:, m, :], in0=xres[:, m, :], scalar=ALPHA, in1=pA[:, 0:N], op0=ALU.mult, op1=ALU.add),
                       reads=[kA, "xres"], writes=["xres"])
                  done_group(1)
              ln_fm([xres[:, c, :] for c in range(8)], ["xres"] * 8, onesln, "onesln", LN_EPS, "ln2_g", "ln2_b", list(range(8)),
                    [[(xres[:, c, :], "xres", AF.Identity)] for c in range(8)])
              lo = max(t0, NMETA)
              hi = min(t0 + N, T_REAL)
              if hi > lo:
                  P.dma(lambda e, lo=lo, hi=hi, t0=t0: e.dma_start(
                      out=outT[:, lo - NMETA:hi - NMETA].rearrange("(c p) n -> p c n", p=128),
                      in_=xres[:, :, lo - t0:hi - t0]), reads=["xres"])

        except StopBuild:
            pass
        P.final_wait()
        P.emit()
        nops = P.nops
    return nc, dbg_out, nops


def host_layout(inputs):
    groups, totw = weight_groups()
    pcol, npar = par_layout()
    W = {k: np.asarray(inputs[k], np.float32)[0] for k in ("w_in", "w_conv_out", "w_rwkv_out", "w_o", "ffn_up", "ffn_down")}
    wall = np.empty((128, totw), np.float32)
    for g in groups:
        src = W[g["src"]][:, g["cols"]]
        kc = g["kc_src"]
        blk = src.reshape(kc, 128, g["m"]).transpose(1, 0, 2)
        wall[:, g["off"]:g["off"] + kc * g["m"]] = blk.reshape(128, kc * g["m"])
    par = np.zeros((128, npar), np.float32)
    def put(name, vec):
        cv = colvec(vec)
        par[:, pcol[name]:pcol[name] + cv.shape[1]] = cv
    put("ln_in_g", inputs["ln_in_g"]); put("ln_in_b", inputs["ln_in_b"])
    cdw = np.asarray(inputs["conv_dw"], np.float32)[0]
    par[:, pcol["conv_dw"]:pcol["conv_dw"] + CONVK * 8] = cdw.reshape(CONVK, 8, 128).transpose(2, 0, 1).reshape(128, CONVK * 8)
    for nm in ("conv_dw_b", "conv_ln_g", "conv_ln_b", "b_gate", "w0", "a0", "k_k", "k_a", "r_k", "lnx_g", "lnx_b",
               "ln1_g", "ln1_b", "ln2_g", "ln2_b"):
        put(nm, np.asarray(inputs[nm], np.float32)[0].reshape(-1))
    fdw = np.asarray(inputs["ffn_dw"], np.float32)[0]
    par[:, pcol["ffn_dw"]:pcol["ffn_dw"] + 3 * 44] = fdw.reshape(3, 44, 128).transpose(2, 0, 1).reshape(128, 3 * 44)
    mub = np.ascontiguousarray(np.broadcast_to(np.asarray(inputs["rwkv_mu"], np.float32)[0][None, :], (128, NRW)))
    lora = np.zeros((128, 4, D), np.float32)
    lora[0:64, 0] = np.asarray(inputs["w_decay_up"], np.float32)[0]
    lora[0:64, 1] = np.asarray(inputs["a_up"], np.float32)[0]
    gup = np.asarray(inputs["g_up"], np.float32)[0]
    lora[:, 2] = gup[0:128]
    lora[0:32, 3] = gup[128:160]
    x = np.asarray(inputs["x"], np.float32)
    meta = np.asarray(inputs["meta_tokens"], np.float32)
    xTs = []
    for b in range(x.shape[0]):
        xt = np.zeros((D, TP), np.float32)
        xt[:, 0:NMETA] = meta.T
        xt[:, NMETA:T_REAL] = x[b].T
        xTs.append(xt)
    return wall, par, mub, lora, xTs


_CACHE = {}


def kernel(**inputs):
    nt_run = int(os.environ.get("KNT", NT_FULL))
    dbg_names = tuple(x for x in os.environ.get("KDBG", "").split(",") if x)
    ncores = int(os.environ.get("KCORES", 8))
    key = (nt_run, dbg_names)
    if key not in _CACHE:
        _CACHE[key] = build_program(nt_run, dbg_names)
    nc, dbg_out, nops = _CACHE[key]
    wall, par, mub, lora, xTs = host_layout(inputs)
    in_maps = [{"xT": xTs[b], "wall": wall, "par": par, "mub": mub, "lora": lora} for b in range(ncores)]
    res = run_bass_kernel_spmd(nc, in_maps, core_ids=list(range(ncores)))
    out = np.zeros((8, SEQ, D), np.float32)
    for b in range(ncores):
        out[b] = np.asarray(res.results[b]["outT"], np.float32).T
    if dbg_names:
        kernel.dbg = [{n: np.asarray(res.results[b]["dbg_" + n]) for n in dbg_out} for b in range(ncores)]
    return out
```

```python
import os
import math
import types
import numpy as np
from contextlib import ExitStack
import concourse.bass as bass
import concourse.mybir as mybir
from concourse.bass_utils import run_bass_kernel_spmd

F32 = mybir.dt.float32
BF16 = mybir.dt.bfloat16
AF = mybir.ActivationFunctionType
ALU = mybir.AluOpType

D = 1024
DC = 8
N = 384
NCH = 3
CH = 128
NMETA = 16
SEQ = 4096
T_REAL = SEQ + NMETA
TP = 4224
NT_FULL = TP // N
DFF = 2816
FC = 22
ALPHA = 2.0 ** 0.25
LN_EPS = 1e-5
GN_EPS = 64e-5
C0 = math.exp(-0.5)
CONVK = 31
HALO = CONVK - 1
NRW = 3360
GSZ = 4096
NSLOT = 3
SEM_ROLL = 6000
ENGS = ("pe", "act", "dve", "pool", "sp")


def freeze(fn):
    if fn is None or fn.__closure__ is None:
        return fn
    cells = []
    for c in fn.__closure__:
        try:
            cells.append(types.CellType(c.cell_contents))
        except ValueError:
            cells.append(c)
    return types.FunctionType(fn.__code__, fn.__globals__, fn.__name__, fn.__defaults__, tuple(cells))


class Prog:
    def __init__(self, nc, es, n_dma_sems=12):
        self.nc = nc
        self.es = es
        self.streams = {e: [] for e in ENGS}
        self.cnt = {e: 0 for e in ENGS}
        self.cur_sem = {}
        self.nsem = 0
        for e in ENGS:
            self._new_eng_sem(e)
        self.dma_sems = [self._sem("dma%d" % i) for i in range(n_dma_sems)]
        self.dma_val = [0] * n_dma_sems
        self.dma_rr = 0
        self.lastw = {}
        self.readers = {}
        self.waited = {e: {} for e in ENGS}
        self.nops = 0

    def _sem(self, name):
        self.nsem += 1
        return self.es.enter_context(self.nc.semaphore(name))

    def _new_eng_sem(self, e):
        self.cur_sem[e] = self._sem("s_%s_%d" % (e, self.nsem))
        self.cnt[e] = 0

    def _deps(self, eng, reads, writes):
        toks = []
        for k in reads:
            t = self.lastw.get(k)
            if t is not None:
                toks.append(t)
        for k in writes:
            t = self.lastw.get(k)
            if t is not None:
                toks.append(t)
            toks.extend(self.readers.get(k, ()))
        waits = {}
        for (sem, val, src) in toks:
            if src == "pe" and eng == "pe":
                continue
            key = id(sem)
            if self.waited[eng].get(key, 0) >= val:
                continue
            if key not in waits or waits[key][1] < val:
                waits[key] = (sem, val)
        for key, (sem, val) in waits.items():
            self.waited[eng][key] = val
        return list(waits.values())

    def _commit(self, tok, reads, writes):
        for k in reads:
            if k in writes:
                continue
            self.readers.setdefault(k, []).append(tok)
        for k in writes:
            self.lastw[k] = tok
            self.readers[k] = []

    def op(self, eng, fn, reads=(), writes=()):
        fn = freeze(fn)
        reads = tuple(reads)
        writes = tuple(writes)
        waits = self._deps(eng, reads, writes)
        if self.cnt[eng] >= SEM_ROLL:
            self._new_eng_sem(eng)
        self.cnt[eng] += 1
        sem = self.cur_sem[eng]
        tok = (sem, self.cnt[eng], eng)
        self.streams[eng].append((waits, fn, sem, 1))
        self._commit(tok, reads, writes)
        self.nops += 1

    def dma(self, fn, reads=(), writes=(), eng="sp"):
        fn = freeze(fn)
        reads = tuple(reads)
        writes = tuple(writes)
        i = self.dma_rr
        self.dma_rr = (self.dma_rr + 1) % len(self.dma_sems)
        sem = self.dma_sems[i]
        waits = self._deps(eng, reads, writes)
        if self.dma_val[i] > 0 and self.waited[eng].get(id(sem), 0) < self.dma_val[i]:
            waits = [w for w in waits if w[0] is not sem] + [(sem, self.dma_val[i])]
            self.waited[eng][id(sem)] = self.dma_val[i]
        self.dma_val[i] += 16
        tok = (sem, self.dma_val[i], "dma")
        self.streams[eng].append((waits, fn, sem, 16))
        self._commit(tok, reads, writes)
        self.nops += 1

    def final_wait(self, eng="sp"):
        waits = []
        for i, sem in enumerate(self.dma_sems):
            if self.dma_val[i] > 0:
                waits.append((sem, self.dma_val[i]))
        self.streams[eng].append((waits, None, None, 0))

    def emit(self):
        nc = self.nc
        with nc.Block() as block:
            def run(engname):
                def body(e):
                    for (waits, fn, sem, inc) in self.streams[engname]:
                        for (s, v) in waits:
                            e.wait_ge(s, v)
                        if fn is not None:
                            ins = fn(e)
                            ins.then_inc(sem, inc)
                return body
            block.tensor(run("pe"))
            block.scalar(run("act"))
            block.vector(run("dve"))
            block.gpsimd(run("pool"))
            block.sync(run("sp"))


def weight_groups():
    g = []
    def add(name, kind, kc_src, m, src, cols):
        g.append(dict(name=name, kind=kind, kc_src=kc_src, m=m, src=src, cols=np.asarray(cols)))
    for i in range(2):
        add("ga%d" % i, "plain", 8, 512, "w_in", np.arange(i * 512, (i + 1) * 512))
        add("gg%d" % i, "plain", 8, 512, "w_in", 1024 + np.arange(i * 512, (i + 1) * 512))
    for i in range(2):
        add("co%d" % i, "plain", 8, 512, "w_conv_out", np.arange(i * 512, (i + 1) * 512))
        add("gA%d" % i, "plain", 8, 512, "w_in", 5408 + np.arange(i * 512, (i + 1) * 512))
    add("lr0", "rwkv", 8, 128, "w_in", 2048 + 3072 + np.arange(0, 128))
    add("lr1", "rwkv", 8, 160, "w_in", 2048 + 3072 + 128 + np.arange(0, 160))
    for i in range(4):
        for j, nm in enumerate("rkv"):
            add("%s%d" % (nm, i), "rwkv", 8, 256, "w_in", 2048 + j * 1024 + np.arange(i * 256, (i + 1) * 256))
    for i in range(2):
        add("ro%d" % i, "plain", 8, 512, "w_rwkv_out", np.arange(i * 512, (i + 1) * 512))
        add("gB%d" % i, "plain", 8, 512, "w_in", 6432 + np.arange(i * 512, (i + 1) * 512))
    for i in range(2):
        add("wo%d" % i, "plain", 8, 512, "w_o", np.arange(i * 512, (i + 1) * 512))
    for j in range(11):
        cols = np.concatenate([np.arange(2 * j * 128, (2 * j + 2) * 128),
                               DFF + np.arange(2 * j * 128, (2 * j + 2) * 128)])
        add("fu%d" % j, "plain", 8, 512, "ffn_up", cols)
    for m in range(8):
        add("fd%d" % m, "plain", 22, 128, "ffn_down", np.arange(m * 128, (m + 1) * 128))
    off = 0
    for i, x in enumerate(g):
        x["idx"] = i
        x["off"] = off
        x["kc_out"] = x["kc_src"] * (2 if x["kind"] == "rwkv" else 1)
        off += x["kc_src"] * x["m"]
    return g, off


def par_layout():
    cols = {}
    off = 0
    def add(name, n):
        nonlocal off
        cols[name] = off
        off += n
    add("ln_in_g", 8); add("ln_in_b", 8)
    add("conv_dw", CONVK * 8)
    add("conv_dw_b", 8); add("conv_ln_g", 8); add("conv_ln_b", 8)
    add("b_gate", 16)
    add("w0", 8); add("a0", 8); add("k_k", 8); add("k_a", 8); add("r_k", 8)
    add("lnx_g", 8); add("lnx_b", 8)
    add("ln1_g", 8); add("ln1_b", 8)
    add("ffn_dw", 3 * 44)
    add("ln2_g", 8); add("ln2_b", 8)
    add("omka", 8)
    return cols, off


def colvec(v):
    v = np.asarray(v, np.float32).reshape(-1, 128)
    return np.ascontiguousarray(v.T)


def build_program(nt_run, dbg_names=()):
    groups, totw = weight_groups()
    gidx = {x["name"]: x for x in groups}
    NG = len(groups)
    pcol, npar = par_layout()

    nc = bass.Bass("TRN2", target_bir_lowering=False)
    xT = nc.dram_tensor("xT", [D, TP], F32, kind="ExternalInput").ap()
    wall = nc.dram_tensor("wall", [128, totw], F32, kind="ExternalInput").ap()
    par = nc.dram_tensor("par", [128, npar], F32, kind="ExternalInput").ap()
    mub = nc.dram_tensor("mub", [128, NRW], F32, kind="ExternalInput").ap()
    lora = nc.dram_tensor("lora", [128, 4, D], F32, kind="ExternalInput").ap()
    outT = nc.dram_tensor("outT", [D, SEQ], F32, kind="ExternalOutput").ap()
    wsc = nc.dram_tensor("wsc", [NG, 128, GSZ], BF16, kind="Internal").ap()
    dbg_out = {}

    with ExitStack() as es:
        P = Prog(nc, es)

        def sb(name, shape, dt):
            return es.enter_context(nc.sbuf_tensor(name, shape, dt))

        def psum(name):
            return es.enter_context(nc.psum_tensor(name, [128, 512], F32))

        PT = sb("PT", [128, npar], F32)
        ident = sb("ident", [128, 128], BF16)
        onesln = sb("onesln", [128, 128], BF16)
        bo64 = sb("bo64", [128, 128], BF16)
        bo1 = sb("bo1", [128, 128], BF16)
        mask4 = sb("mask4", [128, 4, 128], F32)
        ones_f = sb("ones_f", [128, 128], F32)
        lorab = sb("lorab", [128, 4, D], BF16)
        ring = [sb("ring%d" % i, [128, GSZ], BF16) for i in range(NSLOT)]
        BIG0 = sb("BIG0", [128, 8 * N], F32)
        BIG1 = sb("BIG1", [128, 8 * N], F32)
        HB0 = sb("HB0", [128, 8 * N], BF16)
        HB1 = sb("HB1", [128, 8 * N], BF16)
        mut = [sb("mut%d" % i, [128, 256], F32) for i in range(2)]
        xres = sb("xres", [128, 8, N], F32)
        hT = sb("hT", [128, 8, N + 1], BF16)
        h1T = sb("h1T", [128, 8, N + 2], BF16)
        cb = sb("cb", [128, 8, N + HALO], BF16)
        diag = [sb("diag%d" % i, [128, 16, 128], BF16) for i in range(2)]
        NF = 16
        NH = 6
        Ft = [sb("F%d" % i, [128, N], F32) for i in range(NF)]
        Ht = [sb("H%d" % i, [128, N], BF16) for i in range(NH)]
        act = sb("act", [128, FC, N], BF16)
        tw = sb("tw", [128, N], BF16)
        xab = sb("xab", [128, N], BF16)
        sg0 = sb("sg0", [128, N], BF16)
        sg1 = sb("sg1", [128, N], BF16)
        GP = 2
        AR = [sb("AR%d" % i, [128, 2, N], BF16) for i in range(GP)]
        BK = [sb("BK%d" % i, [128, 2, N], BF16) for i in range(GP)]
        BKh = [sb("BKh%d" % i, [128, 2, N], BF16) for i in range(GP)]
        vb = [sb("vb%d" % i, [128, N], BF16) for i in range(GP)]
        vTf = [sb("vTf%d" % i, [128, N], F32) for i in range(GP)]
        rkb = [sb("rkb%d" % i, [128, N], BF16) for i in range(GP)]
        Yf = [sb("Yf%d" % i, [128, N], F32) for i in range(GP)]
        PCc = [sb("PCc%d" % i, [128, NCH], F32) for i in range(GP)]
        TM = [sb("TM%d" % i, [128, 4, 128], BF16) for i in range(GP)]
        QKs = [[sb("QKs%d_%d" % (i, h), [128, 4, 128], BF16) for h in range(2)] for i in range(GP)]
        NMb = [[sb("NMb%d_%d" % (i, k), [128, 4, 128], BF16) for k in range(2)] for i in range(GP)]
        Qb = [[sb("Qb%d_%d" % (i, k), [128, 2, 128], BF16) for k in range(2)] for i in range(GP)]
        G1Tb = [sb("G1Tb%d" % i, [128, 128], BF16) for i in range(GP)]
        Xb = [sb("Xb%d" % i, [128, 128], BF16) for i in range(GP)]
        G2f = [sb("G2f%d" % i, [128, 128], F32) for i in range(GP)]
        Ub = [sb("Ub%d" % i, [128, 128], BF16) for i in range(GP)]
        Stmp = [sb("Stmp%d" % i, [128, 128], F32) for i in range(GP)]
        Sf = sb("Sf", [128, 8, 128], F32)
        Sb = sb("Sb", [128, 8, 128], BF16)
        nbC = sb("nbC", [128, NCH], F32)

        pm = [psum("pm%d" % i) for i in range(3)]
        pa = psum("pa")
        pb = psum("pb")
        pr = [psum("pr%d" % i) for i in range(3)]
        PRR = pr + pm
        rr = {"pm": 0, "pr": 0, "slot": 0}

        def next_pm():
            i = rr["pm"]; rr["pm"] = (i + 1) % 3
            return pm[i], "pm%d" % i

        def next_pr():
            i = rr["pr"]; rr["pr"] = (i + 1) % 6
            return PRR[i], ("pr%d" % i if i < 3 else "pm%d" % (i - 3))

        def pc(name, c):
            o = pcol[name] + c
            return PT[:, o:o + 1]

        def dump(name, ap, shape, keys, dt=F32):
            if name not in dbg_names:
                return
            t = nc.dram_tensor("dbg_" + name, list(shape), dt, kind="ExternalOutput").ap()
            dbg_out[name] = shape
            P.dma(lambda e: e.dma_start(out=t, in_=ap), reads=list(keys))

        P.dma(lambda e: e.dma_start(out=PT[:, 0:npar], in_=par[:, :]), writes=["PT"])
        P.op("pool", lambda e: e.memset(ident[:], 1.0), writes=["ident"])
        P.op("pool", lambda e: e.affine_select(out=ident[:], in_=ident[:], pattern=[[-1, 128]],
                                                compare_op=ALU.is_equal, fill=0.0, base=0,
                                                channel_multiplier=1), reads=["ident"], writes=["ident"])
        P.op("pool", lambda e: e.memset(onesln[:], 1.0 / D), writes=["onesln"])
        P.op("pool", lambda e: e.memset(ones_f[:], 1.0), writes=["ones_f"])
        for (t_, v_, k_) in ((bo64, 1.0 / 64, "bo64"), (bo1, 1.0, "bo1")):
            P.op("pool", lambda e, t_=t_: e.memset(t_[:], 0.0), writes=[k_])
            P.op("pool", lambda e, t_=t_, v_=v_: e.memset(t_[0:64, 0:64], v_), reads=[k_], writes=[k_])
            P.op("pool", lambda e, t_=t_, v_=v_: e.memset(t_[64:128, 64:128], v_), reads=[k_], writes=[k_])
        P.op("pool", lambda e: e.memset(mask4[:], 1.0), writes=["mask4"])
        for q in range(4):
            base = -1 if q % 2 == 0 else 0
            P.op("pool", lambda e, q=q, base=base: e.affine_select(
                out=mask4[:, q, :], in_=mask4[:, q, :], pattern=[[1, 128]], compare_op=ALU.is_ge,
                fill=0.0, base=base, channel_multiplier=-1), reads=["mask4"], writes=["mask4"])
        P.op("dve", lambda e: e.tensor_scalar(out=PT[:, pcol["omka"]:pcol["omka"] + 8],
                                              in0=PT[:, pcol["k_a"]:pcol["k_a"] + 8],
                                              scalar1=-1.0, scalar2=1.0, op0=ALU.mult, op1=ALU.add),
             reads=["PT"], writes=["PT"])
        P.op("pool", lambda e: e.memset(Sf[:], 0.0), writes=["Sf"])
        P.op("pool", lambda e: e.memset(Sb[:], 0.0), writes=["Sb"])
        P.op("pool", lambda e: e.memset(hT[:, :, 0:1], 0.0), writes=["hT"])
        P.op("pool", lambda e: e.memset(h1T[:, :, 0:2], 0.0), writes=["h1T"])
        P.op("pool", lambda e: e.memset(cb[:, :, 0:HALO], 0.0), writes=["cb"])
        KBIG0 = [("BIG0", c) for c in range(8)]
        KBIG1 = [("BIG1", c) for c in range(8)]
        KHB0 = [("HB0", c) for c in range(8)]
        KHB1 = [("HB1", c) for c in range(8)]
        for half in range(2):
            P.dma(lambda e, half=half: e.dma_start(out=BIG0[:, 0:2 * D], in_=lora[:, 2 * half:2 * half + 2, :]),
                  writes=KBIG0)
            P.op("dve", lambda e, half=half: e.tensor_copy(out=lorab[:, 2 * half:2 * half + 2, :], in_=BIG0[:, 0:2 * D]),
                 reads=KBIG0, writes=["lorab"])

        stg32 = [(BIG0, KBIG0), (BIG1, KBIG1)]
        stg16 = [(HB0, KHB0), (HB1, KHB1)]
        pp = 0
        cast_engs = ["dve", "pool"]
        for g in groups:
            kcs = g["kc_src"] // 2
            m = g["m"]
            ne = kcs * m
            for half in range(2):
                if g["kind"] == "plain":
                    s32, k32 = stg32[pp % 2]
                    s16, k16 = stg16[pp % 2]
                    ce = cast_engs[pp % 2]
                    pp += 1
                    so = g["off"] + half * ne
                    P.dma(lambda e, s32=s32, so=so, ne=ne: e.dma_start(out=s32[:, 0:ne], in_=wall[:, so:so + ne]),
                          writes=k32)
                    if ce == "act":
                        P.op("act", lambda e, s16=s16, s32=s32, ne=ne: e.activation(out=s16[:, 0:ne], in_=s32[:, 0:ne], func=AF.Copy),
                             reads=k32, writes=k16)
                    else:
                        P.op(ce, lambda e, s16=s16, s32=s32, ne=ne: e.tensor_copy(out=s16[:, 0:ne], in_=s32[:, 0:ne]),
                             reads=k32, writes=k16)
                    P.dma(lambda e, s16=s16, gi=g["idx"], half=half, ne=ne: e.dma_start(
                        out=wsc[gi, :, half * ne:(half + 1) * ne], in_=s16[:, 0:ne]),
                        reads=k16, writes=[("wsc", g["idx"])], eng="act")
                else:
                    mu_t = mut[pp % 2]; kmu = "mut%d" % (pp % 2)
                    pp += 1
                    so = g["off"] + half * ne
                    c0 = int(g["cols"][0]) - 2048
                    P.dma(lambda e, so=so, ne=ne: e.dma_start(out=BIG0[:, 0:ne], in_=wall[:, so:so + ne]),
                          writes=KBIG0)
                    P.dma(lambda e, mu_t=mu_t, c0=c0, m=m: e.dma_start(out=mu_t[:, 0:m], in_=mub[:, c0:c0 + m]),
                          writes=[kmu])
                    for kk_ in range(kcs):
                        P.op("dve", lambda e, kk_=kk_, m=m, mu_t=mu_t: e.tensor_tensor(
                            out=BIG1[:, kk_ * m:(kk_ + 1) * m], in0=BIG0[:, kk_ * m:(kk_ + 1) * m],
                            in1=mu_t[:, 0:m], op=ALU.mult), reads=KBIG0 + [kmu], writes=KBIG1)
                    P.op("pool", lambda e, ne=ne: e.tensor_copy(out=HB1[:, 0:ne], in_=BIG1[:, 0:ne]),
                         reads=KBIG1, writes=KHB1)
                    P.op("dve", lambda e, ne=ne: e.tensor_tensor(out=HB0[:, 0:ne], in0=BIG0[:, 0:ne],
                                                                in1=BIG1[:, 0:ne], op=ALU.subtract),
                         reads=KBIG0 + KBIG1, writes=KHB0)
                    gi = g["idx"]
                    o1 = half * ne
                    o2 = 8 * m + half * ne
                    P.dma(lambda e, gi=gi, o1=o1, ne=ne: e.dma_start(out=wsc[gi, :, o1:o1 + ne], in_=HB0[:, 0:ne]),
                          reads=KHB0, writes=[("wsc", gi)], eng="act")
                    P.dma(lambda e, gi=gi, o2=o2, ne=ne: e.dma_start(out=wsc[gi, :, o2:o2 + ne], in_=HB1[:, 0:ne]),
                          reads=KHB1, writes=[("wsc", gi)], eng="act")

        seq = [g["name"] for _ in range(nt_run) for g in groups]
        issued = []
        st = {"cur": 0, "released": 0}

        def issue_load():
            idx = len(issued)
            g = gidx[seq[idx]]
            i = idx % NSLOT
            slot = ring[i]; key = "ring%d" % i
            ne = g["kc_out"] * g["m"]
            P.dma(lambda e: e.dma_start(out=slot[:, 0:ne], in_=wsc[g["idx"], :, 0:ne]),
                  reads=[("wsc", g["idx"])], writes=[key])
            issued.append((slot, key, g))

        def pump():
            while len(issued) < min(len(seq), st["released"] + NSLOT):
                issue_load()

        def load_group(name):
            cur = st["cur"]
            assert seq[cur] == name, (seq[cur], name)
            pump()
            assert len(issued) > cur, "weight ring too shallow for simultaneously open groups"
            slot, key, g = issued[cur]
            st["cur"] = cur + 1
            m = g["m"]
            def w(kc, c0, c1):
                return slot[:, kc * m + c0:kc * m + c1]
            return w, key, g

        def done_group(n=1):
            st["released"] += n
            pump()

        convo = [BIG0[:, c * N:(c + 1) * N] for c in range(8)]
        yA = [BIG1[:, c * N:(c + 1) * N] for c in range(8)]
        cs_ = [HB0[:, c * N:(c + 1) * N] for c in range(8)]
        oT = cs_
        ybv = [HB1[:, c * N:(c + 1) * N] for c in range(8)]

        def ln_fm(srcs, skeys, ones_t, ones_key, eps, gname, bname, cidx, outs):
            nchunk = len(srcs)
            for i, (s, sk) in enumerate(zip(srcs, skeys)):
                xbt, xbk = Ht[i % 2], "H%d" % (i % 2)
                sqt, sqk = Ht[2 + i % 2], "H%d" % (2 + i % 2)
                P.op("pool", lambda e, s=s, xbt=xbt: e.tensor_copy(out=xbt[:], in_=s), reads=[sk], writes=[xbk])
                P.op("act", lambda e, s=s, sqt=sqt: e.activation(out=sqt[:], in_=s, func=AF.Square), reads=[sk], writes=[sqk])
                P.op("pe", lambda e, xbt=xbt, i=i: e.matmul(pa[:, 0:N], lhsT=ones_t[:], rhs=xbt[:], start=(i == 0), stop=(i == nchunk - 1)),
                     reads=[xbk, ones_key], writes=["pa"])
                P.op("pe", lambda e, sqt=sqt, i=i: e.matmul(pb[:, 0:N], lhsT=ones_t[:], rhs=sqt[:], start=(i == 0), stop=(i == nchunk - 1)),
                     reads=[sqk, ones_key], writes=["pb"])
            m2, var, rstd, nb = Ft[0], Ft[1], Ft[2], Ft[3]
            P.op("act", lambda e: e.activation(out=m2[:], in_=pa[:, 0:N], func=AF.Square), reads=["pa"], writes=["F0"])
            P.op("dve", lambda e: e.tensor_tensor(out=var[:], in0=pb[:, 0:N], in1=m2[:], op=ALU.subtract), reads=["pb", "F0"], writes=["F1"])
            P.op("dve", lambda e: e.tensor_scalar(out=var[:], in0=var[:], scalar1=eps, scalar2=None, op0=ALU.add), reads=["F1"], writes=["F1"])
            P.op("act", lambda e: e.activation(out=var[:], in_=var[:], func=AF.Sqrt), reads=["F1"], writes=["F1"])
            P.op("dve", lambda e: e.reciprocal(out=rstd[:], in_=var[:]), reads=["F1"], writes=["F2"])
            P.op("dve", lambda e: e.scalar_tensor_tensor(out=nb[:], in0=pa[:, 0:N], scalar=-1.0, in1=rstd[:], op0=ALU.mult, op1=ALU.mult),
                 reads=["pa", "F2"], writes=["F3"])
            for i, (s, sk) in enumerate(zip(srcs, skeys)):
                t0, k0 = Ft[4 + i % 2], "F%d" % (4 + i % 2)
                P.op("dve", lambda e, s=s, t0=t0: e.tensor_tensor(out=t0[:], in0=s, in1=rstd[:], op=ALU.mult), reads=[sk, "F2"], writes=[k0])
                P.op("pool", lambda e, t0=t0: e.tensor_tensor(out=t0[:], in0=t0[:], in1=nb[:], op=ALU.add), reads=[k0, "F3"], writes=[k0])
                c = cidx[i]
                for (oap, okey, func) in outs[i]:
                    P.op("act", lambda e, oap=oap, t0=t0, func=func, c=c: e.activation(
                        out=oap, in_=t0[:], func=func, scale=pc(gname, c), bias=pc(bname, c)),
                        reads=[k0, "PT"], writes=[okey])

        def mm_chunk(out_ap, okey, w, wkey, kcn, c0, c1, rhs_fn, rkeys):
            for kc in range(kcn):
                P.op("pe", lambda e, kc=kc: e.matmul(out_ap, lhsT=w(kc, c0, c1), rhs=rhs_fn(kc), start=(kc == 0), stop=(kc == kcn - 1)),
                     reads=[wkey] + list(rkeys), writes=[okey])

        def rhs_h(kc):
            return hT[:, kc, 1:N + 1]

        def rhs_hs(kc):
            if kc < 8:
                return hT[:, kc, 1:N + 1]
            return hT[:, kc - 8, 0:N]

        class StopBuild(Exception):
            pass
        kstop = float(os.environ.get("KSTOP", "99"))

        def ckpt(i):
            if kstop <= i:
                raise StopBuild()

        try:
          ckpt(0)
          for it in range(nt_run):
              t0 = it * N
              P.dma(lambda e, t0=t0: e.dma_start(out=xres[:, :, :], in_=xT[:, t0:t0 + N].rearrange("(c p) n -> p c n", p=128)),
                    writes=["xres"])
              ckpt(1)
              ln_fm([xres[:, c, :] for c in range(8)], ["xres"] * 8, onesln, "onesln", LN_EPS, "ln_in_g", "ln_in_b",
                    list(range(8)),
                    [[(xres[:, c, :], "xres", AF.Identity), (hT[:, c, 1:N + 1], "hT", AF.Identity)] for c in range(8)])
              if it == 0:
                  dump("h", xres[:, :, :], [128, 8, N], ["xres"])
              ckpt(2)
              for half in range(2):
                  wa, ka, _ = load_group("ga%d" % half)
                  wg, kg, _ = load_group("gg%d" % half)
                  for cc in range(4):
                      c = half * 4 + cc
                      pA, kA = next_pm()
                      mm_chunk(pA[:, 0:N], kA, wa, ka, 8, cc * 128, (cc + 1) * 128, rhs_h, ["hT"])
                      pG, kG = next_pm()
                      mm_chunk(pG[:, 0:N], kG, wg, kg, 8, cc * 128, (cc + 1) * 128, rhs_h, ["hT"])
                      sgt, sgk = Ft[6 + c % 2], "F%d" % (6 + c % 2)
                      P.op("act", lambda e, pG=pG, sgt=sgt: e.activation(out=sgt[:], in_=pG[:, 0:N], func=AF.Sigmoid), reads=[kG], writes=[sgk])
                      P.op("dve", lambda e, pA=pA, sgt=sgt, c=c: e.tensor_tensor(out=cb[:, c, HALO:HALO + N], in0=pA[:, 0:N], in1=sgt[:], op=ALU.mult),
                           reads=[kA, sgk], writes=[("cb", c)])
                      if it == 0 and c == 0 and "dbgA" in dbg_names:
                          P.op("act", lambda e, pA=pA: e.activation(out=Ft[10][:], in_=pA[:, 0:N], func=AF.Copy), reads=[kA], writes=["F10"])
                          dump("dbgA", Ft[10][:], [128, N], ["F10"])
                          dump("dbgS", sgt[:], [128, N], [sgk])
                  done_group(2)
              if it == 0:
                  dump("hT", hT[:, :, :], [128, 8, N + 1], ["hT"], BF16)
                  dump("cb", cb[:, :, :], [128, 8, N + HALO], [("cb", c) for c in range(8)], BF16)
              ckpt(3)
              for c in range(8):
                  for j in range(CONVK):
                      dg, dk, jj = diag[j // 16], "diag%d" % (j // 16), j % 16
                      P.op("pool", lambda e, dg=dg, jj=jj, j=j, c=c: e.tensor_scalar(
                          out=dg[:, jj, :], in0=ident[:], scalar1=pc("conv_dw", j * 8 + c), scalar2=0.0, op0=ALU.mult, op1=ALU.add),
                          reads=["ident", "PT"], writes=[dk])
                  pC, kC = next_pm()
                  for j in range(CONVK):
                      dg, dk, jj = diag[j // 16], "diag%d" % (j // 16), j % 16
                      P.op("pe", lambda e, dg=dg, jj=jj, j=j, c=c, pC=pC: e.matmul(pC[:, 0:N], lhsT=dg[:, jj, :], rhs=cb[:, c, j:j + N], start=(j == 0), stop=(j == CONVK - 1)),
                           reads=[dk, ("cb", c), "cb"], writes=[kC])
                  P.op("act", lambda e, c=c, pC=pC: e.activation(out=convo[c], in_=pC[:, 0:N], func=AF.Identity, bias=pc("conv_dw_b", c)),
                       reads=[kC, "PT"], writes=[("BIG0", c)])
                  P.op("pool", lambda e, c=c: e.tensor_copy(out=cb[:, c, 0:HALO], in_=cb[:, c, N:N + HALO]), reads=[("cb", c)], writes=[("cb", c)])
              if it == 0:
                  dump("convo", BIG0[:, :], [128, 8 * N], KBIG0)
              ckpt(4)
              ln_fm(convo, [("BIG0", c) for c in range(8)], onesln, "onesln", LN_EPS, "conv_ln_g", "conv_ln_b", list(range(8)),
                    [[(cs_[c], ("HB0", c), AF.Silu)] for c in range(8)])
              for half in range(2):
                  wc, kc_, _ = load_group("co%d" % half)
                  wga, kga, _ = load_group("gA%d" % half)
                  for cc in range(4):
                      c = half * 4 + cc
                      pA, kA = next_pm()
                      mm_chunk(pA[:, 0:N], kA, wc, kc_, 8, cc * 128, (cc + 1) * 128, lambda kc: cs_[kc], [("HB0", k) for k in range(8)])
                      pG, kG = next_pm()
                      mm_chunk(pG[:, 0:N], kG, wga, kga, 8, cc * 128, (cc + 1) * 128, rhs_h, ["hT"])
                      sgt, sgk = Ft[6 + c % 2], "F%d" % (6 + c % 2)
                      P.op("act", lambda e, pG=pG, sgt=sgt, c=c: e.activation(out=sgt[:], in_=pG[:, 0:N], func=AF.Sigmoid, bias=pc("b_gate", c)),
                           reads=[kG, "PT"], writes=[sgk])
                      P.op("dve", lambda e, pA=pA, sgt=sgt, c=c: e.tensor_tensor(out=yA[c], in0=pA[:, 0:N], in1=sgt[:], op=ALU.mult),
                           reads=[kA, sgk], writes=[("BIG1", c)])
                  done_group(2)
              if it == 0:
                  dump("yA", BIG1[:, :], [128, 8 * N], KBIG1)

              ckpt(5)
              wl, kl, _ = load_group("lr0")
              pW, kW = next_pm()
              mm_chunk(pW[0:64, 0:N], kW, wl, kl, 16, 0, 64, rhs_hs, ["hT"])
              P.op("act", lambda e, pW=pW: e.activation(out=tw[0:64, :], in_=pW[0:64, 0:N], func=AF.Tanh), reads=[kW], writes=["tw"])
              pX, kX = next_pm()
              mm_chunk(pX[0:64, 0:N], kX, wl, kl, 16, 64, 128, rhs_hs, ["hT"])
              P.op("act", lambda e, pX=pX: e.activation(out=xab[0:64, :], in_=pX[0:64, 0:N], func=AF.Copy), reads=[kX], writes=["xab"])
              done_group(1)
              wl, kl, _ = load_group("lr1")
              pW, kW = next_pm()
              mm_chunk(pW[:, 0:N], kW, wl, kl, 16, 0, 128, rhs_hs, ["hT"])
              P.op("act", lambda e, pW=pW: e.activation(out=sg0[:], in_=pW[:, 0:N], func=AF.Sigmoid), reads=[kW], writes=["sg0"])
              pX, kX = next_pm()
              mm_chunk(pX[0:32, 0:N], kX, wl, kl, 16, 128, 160, rhs_hs, ["hT"])
              P.op("act", lambda e, pX=pX: e.activation(out=sg1[0:32, :], in_=pX[0:32, 0:N], func=AF.Sigmoid), reads=[kX], writes=["sg1"])
              done_group(1)

              for grp in range(4):
                  hps = [2 * grp, 2 * grp + 1]
                  wr_, kr_, _ = load_group("r%d" % grp)
                  wk_, kk__, _ = load_group("k%d" % grp)
                  wv_, kv_, _ = load_group("v%d" % grp)
                  for gi_, hp in enumerate(hps):
                      co = gi_ * 128
                      r_, k_, sgw, ai = Ft[0], Ft[1], Ft[2], Ft[3]
                      cs, csm, Ep, En, Em, EC = Ft[4], Ft[5], Ft[6], Ft[7], Ft[8], Ft[9]
                      kkn, b_, kf, kp, nrm = Ft[10], Ft[11], Ft[12], Ft[13], Ft[14]
                      ksq = Ht[4]
                      pR, kR = next_pm()
                      mm_chunk(pR[:, 0:N], kR, wr_, kr_, 16, co, co + 128, rhs_hs, ["hT"])
                      P.op("act", lambda e, pR=pR: e.activation(out=r_[:], in_=pR[:, 0:N], func=AF.Copy), reads=[kR], writes=["F0"])
                      pK, kK = next_pm()
                      mm_chunk(pK[:, 0:N], kK, wk_, kk__, 16, co, co + 128, rhs_hs, ["hT"])
                      P.op("act", lambda e, pK=pK: e.activation(out=k_[:], in_=pK[:, 0:N], func=AF.Copy), reads=[kK], writes=["F1"])
                      pV, kV = next_pm()
                      mm_chunk(pV[:, 0:N], kV, wv_, kv_, 16, co, co + 128, rhs_hs, ["hT"])
                      P.op("act", lambda e, pV=pV, gi_=gi_: e.activation(out=vTf[gi_][:], in_=pV[:, 0:N], func=AF.Copy), reads=[kV], writes=["vTf%d" % gi_])
                      P.op("pool", lambda e, gi_=gi_: e.tensor_copy(out=vb[gi_][:], in_=vTf[gi_][:]), reads=["vTf%d" % gi_], writes=["vb%d" % gi_])
                      pZ, kZ = next_pm()
                      P.op("pe", lambda e, pZ=pZ, hp=hp: e.matmul(pZ[:, 0:N], lhsT=lorab[0:64, 0, hp * 128:(hp + 1) * 128], rhs=tw[0:64, :], start=True, stop=True),
                           reads=["lorab", "tw"], writes=[kZ])
                      P.op("act", lambda e, pZ=pZ, hp=hp: e.activation(out=sgw[:], in_=pZ[:, 0:N], func=AF.Sigmoid, bias=pc("w0", hp)),
                           reads=[kZ, "PT"], writes=["F2"])
                      pZ2, kZ2 = next_pm()
                      P.op("pe", lambda e, pZ2=pZ2, hp=hp: e.matmul(pZ2[:, 0:N], lhsT=lorab[0:64, 1, hp * 128:(hp + 1) * 128], rhs=xab[0:64, :], start=True, stop=True),
                           reads=["lorab", "xab"], writes=[kZ2])
                      P.op("act", lambda e, pZ2=pZ2, hp=hp: e.activation(out=ai[:], in_=pZ2[:, 0:N], func=AF.Sigmoid, bias=pc("a0", hp)),
                           reads=[kZ2, "PT"], writes=["F3"])
                      for ch in range(NCH):
                          sl = slice(ch * CH, (ch + 1) * CH)
                          P.op("dve", lambda e, sl=sl: e.tensor_tensor_scan(out=cs[:, sl], data0=ones_f[:], data1=sgw[:, sl], initial=0.0,
                                                                           op0=ALU.mult, op1=ALU.add),
                               reads=["F2", "ones_f"], writes=["F4"])
                      P.op("pool", lambda e: e.tensor_tensor(out=csm[:], in0=cs[:], in1=sgw[:], op=ALU.subtract), reads=["F4", "F2"], writes=["F5"])
                      P.op("act", lambda e: e.activation(out=Ep[:], in_=cs[:], func=AF.Exp, scale=-C0), reads=["F4"], writes=["F6"])
                      P.op("act", lambda e: e.activation(out=En[:], in_=cs[:], func=AF.Exp, scale=C0), reads=["F4"], writes=["F7"])
                      P.op("act", lambda e: e.activation(out=Em[:], in_=csm[:], func=AF.Exp, scale=-C0), reads=["F5"], writes=["F8"])
                      for ch in range(NCH):
                          ce = (ch + 1) * CH - 1
                          P.op("dve", lambda e, ch=ch, ce=ce: e.tensor_scalar(out=nbC[:, ch:ch + 1], in0=cs[:, ce:ce + 1], scalar1=-C0, scalar2=None, op0=ALU.mult),
                               reads=["F4"], writes=["nbC"])
                          P.op("pool", lambda e, ch=ch, ce=ce, gi_=gi_: e.tensor_copy(out=PCc[gi_][:, ch:ch + 1], in_=Ep[:, ce:ce + 1]),
                               reads=["F6"], writes=["PCc%d" % gi_])
                      for ch in range(NCH):
                          sl = slice(ch * CH, (ch + 1) * CH)
                          P.op("act", lambda e, sl=sl, ch=ch: e.activation(out=EC[:, sl], in_=cs[:, sl], func=AF.Exp, scale=C0, bias=nbC[:, ch:ch + 1]),
                               reads=["F4", "nbC"], writes=["F9"])
                      P.op("act", lambda e, hp=hp: e.activation(out=ksq[:], in_=k_[:], func=AF.Square, scale=pc("k_k", hp)), reads=["F1", "PT"], writes=["H4"])
                      pS, kS = next_pm()
                      P.op("pe", lambda e, pS=pS: e.matmul(pS[:, 0:N], lhsT=bo1[:], rhs=ksq[:], start=True, stop=True), reads=["bo1", "H4"], writes=[kS])
                      P.op("act", lambda e, pS=pS: e.activation(out=nrm[:], in_=pS[:, 0:N], func=AF.Sqrt), reads=[kS], writes=["F14"])
                      P.op("dve", lambda e: e.tensor_scalar(out=nrm[:], in0=nrm[:], scalar1=1e-12, scalar2=None, op0=ALU.max), reads=["F14"], writes=["F14"])
                      P.op("dve", lambda e: e.reciprocal(out=nrm[:], in_=nrm[:]), reads=["F14"], writes=["F14"])
                      P.op("dve", lambda e, hp=hp: e.scalar_tensor_tensor(out=kkn[:], in0=k_[:], scalar=pc("k_k", hp), in1=nrm[:], op0=ALU.mult, op1=ALU.mult),
                           reads=["F1", "F14", "PT"], writes=["F10"])
                      P.op("dve", lambda e, gi_=gi_: e.scalar_tensor_tensor(out=AR[gi_][:, 0, :], in0=kkn[:], scalar=-1.0, in1=Em[:], op0=ALU.mult, op1=ALU.mult),
                           reads=["F10", "F8"], writes=["AR%d" % gi_])
                      P.op("pool", lambda e, gi_=gi_: e.tensor_tensor(out=AR[gi_][:, 1, :], in0=r_[:], in1=Ep[:], op=ALU.mult),
                           reads=["F0", "F6"], writes=["AR%d" % gi_])
                      P.op("pool", lambda e: e.tensor_tensor(out=b_[:], in0=kkn[:], in1=ai[:], op=ALU.mult), reads=["F10", "F3"], writes=["F11"])
                      P.op("dve", lambda e, gi_=gi_: e.tensor_tensor(out=BK[gi_][:, 0, :], in0=b_[:], in1=En[:], op=ALU.mult), reads=["F11", "F7"], writes=["BK%d" % gi_])
                      P.op("pool", lambda e, gi_=gi_: e.tensor_tensor(out=BKh[gi_][:, 0, :], in0=b_[:], in1=EC[:], op=ALU.mult), reads=["F11", "F9"], writes=["BKh%d" % gi_])
                      P.op("dve", lambda e, hp=hp: e.tensor_scalar(out=kf[:], in0=ai[:], scalar1=pc("k_a", hp), scalar2=pc("omka", hp), op0=ALU.mult, op1=ALU.add),
                           reads=["F3", "PT"], writes=["F12"])
                      P.op("pool", lambda e: e.tensor_tensor(out=kp[:], in0=k_[:], in1=kf[:], op=ALU.mult), reads=["F1", "F12"], writes=["F13"])
                      P.op("dve", lambda e, gi_=gi_: e.tensor_tensor(out=BK[gi_][:, 1, :], in0=kp[:], in1=En[:], op=ALU.mult), reads=["F13", "F7"], writes=["BK%d" % gi_])
                      P.op("pool", lambda e, gi_=gi_: e.tensor_tensor(out=BKh[gi_][:, 1, :], in0=kp[:], in1=EC[:], op=ALU.mult), reads=["F13", "F9"], writes=["BKh%d" % gi_])
                      P.op("dve", lambda e, hp=hp, gi_=gi_: e.scalar_tensor_tensor(out=rkb[gi_][:], in0=r_[:], scalar=pc("r_k", hp), in1=kp[:], op0=ALU.mult, op1=ALU.mult),
                           reads=["F0", "F13", "PT"], writes=["rkb%d" % gi_])

                  done_group(3)
                  ckpt(6)
                  for ch in range(NCH):
                      sl = slice(ch * CH, (ch + 1) * CH)
                      for gi_, hp in enumerate(hps):
                          pT, kT = next_pr()
                          srcs = [(AR[gi_][:, 0, sl], "AR%d" % gi_), (BKh[gi_][:, 0, sl], "BKh%d" % gi_),
                                  (BKh[gi_][:, 1, sl], "BKh%d" % gi_), (vb[gi_][:, sl], "vb%d" % gi_)]
                          for q, (sap, skey) in enumerate(srcs):
                              P.op("pe", lambda e, pT=pT, q=q, sap=sap: e.matmul(pT[:, q * 128:(q + 1) * 128], lhsT=sap, rhs=ident[:], start=True, stop=True),
                                   reads=[skey, "ident"], writes=[kT])
                          P.op("act", lambda e, pT=pT, gi_=gi_: e.activation(out=TM[gi_][:, :, :], in_=pT[:, :].rearrange("p (q n) -> p q n", q=4), func=AF.Copy),
                               reads=[kT], writes=["TM%d" % gi_])
                      ckpt(6.1)
                      for gi_, hp in enumerate(hps):
                          for h in range(2):
                              hs = slice(h * 64, (h + 1) * 64)
                              pQ, kQ = next_pr()
                              P.op("pe", lambda e, pQ=pQ, gi_=gi_, hs=hs: e.matmul(pQ[:, 0:256].rearrange("p (q n) -> p q n", q=2), lhsT=BK[gi_][hs, 0, sl], rhs=AR[gi_][hs, :, sl], start=True, stop=True),
                                   reads=["BK%d" % gi_, "AR%d" % gi_], writes=[kQ])
                              P.op("pe", lambda e, pQ=pQ, gi_=gi_, hs=hs: e.matmul(pQ[:, 256:512].rearrange("p (q n) -> p q n", q=2), lhsT=BK[gi_][hs, 1, sl], rhs=AR[gi_][hs, :, sl], start=True, stop=True),
                                   reads=["BK%d" % gi_, "AR%d" % gi_], writes=[kQ])
                              P.op("dve", lambda e, pQ=pQ, gi_=gi_, h=h: e.tensor_tensor(out=QKs[gi_][h][:, :, :], in0=pQ[:, :].rearrange("p (q n) -> p q n", q=4), in1=mask4[:, :, :], op=ALU.mult),
                                   reads=[kQ, "mask4"], writes=["QKs%d_%d" % (gi_, h)])
                          pL, kL = next_pr()
                          for h in range(2):
                              P.op("pe", lambda e, pL=pL, gi_=gi_, h=h: e.matmul(pL[:, h * 128:(h + 1) * 128], lhsT=QKs[gi_][h][:, 0, :], rhs=ident[:], start=True, stop=True),
                                   reads=["QKs%d_%d" % (gi_, h), "ident"], writes=[kL])
                              P.op("pool", lambda e, gi_=gi_, h=h: e.tensor_copy(out=NMb[gi_][0][:, h, :], in_=QKs[gi_][h][:, 0, :]),
                                   reads=["QKs%d_%d" % (gi_, h)], writes=["NMb%d_0" % gi_])
                              P.op("pool", lambda e, gi_=gi_, h=h: e.tensor_tensor(out=Qb[gi_][0][:, h, :], in0=QKs[gi_][h][:, 0, :], in1=ident[:], op=ALU.add),
                                   reads=["QKs%d_%d" % (gi_, h), "ident"], writes=["Qb%d_0" % gi_])
                          P.op("act", lambda e, pL=pL, gi_=gi_: e.activation(out=NMb[gi_][0][:, 2:4, :], in_=pL[:, 0:256].rearrange("p (q n) -> p q n", q=2), func=AF.Copy),
                               reads=[kL], writes=["NMb%d_0" % gi_])
                      ckpt(6.2)
                      NLV = 6
                      for lv in range(NLV):
                          cur, nxt = lv % 2, (lv + 1) % 2
                          last = (lv == NLV - 1)
                          for gi_, hp in enumerate(hps):
                              pN, kN = next_pr()
                              kcur = "NMb%d_%d" % (gi_, cur)
                              knxt = "NMb%d_%d" % (gi_, nxt)
                              for h in range(2):
                                  if not last:
                                      P.op("pe", lambda e, pN=pN, gi_=gi_, h=h, cur=cur: e.matmul(pN[:, h * 128:(h + 1) * 128], lhsT=NMb[gi_][cur][:, 2 + h, :], rhs=NMb[gi_][cur][:, h, :], start=True, stop=True),
                                           reads=[kcur], writes=[kN])
                                  P.op("pe", lambda e, pN=pN, gi_=gi_, h=h, cur=cur: e.matmul(pN[:, (2 + h) * 128:(3 + h) * 128], lhsT=NMb[gi_][cur][:, h, :], rhs=NMb[gi_][cur][:, 2 + h, :], start=True, stop=True),
                                       reads=[kcur], writes=[kN])
                              if not last:
                                  P.op("act", lambda e, pN=pN, gi_=gi_, nxt=nxt: e.activation(out=NMb[gi_][nxt][:, :, :], in_=pN[:, :].rearrange("p (q n) -> p q n", q=4), func=AF.Copy),
                                       reads=[kN], writes=[knxt])
                              else:
                                  P.op("act", lambda e, pN=pN, gi_=gi_, nxt=nxt: e.activation(out=NMb[gi_][nxt][:, 2:4, :], in_=pN[:, 256:512].rearrange("p (q n) -> p q n", q=2), func=AF.Copy),
                                       reads=[kN], writes=[knxt])
                          for gi_, hp in enumerate(hps):
                              pQ2, kQ2 = next_pr()
                              knxt = "NMb%d_%d" % (gi_, nxt)
                              kq0 = "Qb%d_%d" % (gi_, cur)
                              kq1 = "Qb%d_%d" % (gi_, nxt)
                              for h in range(2):
                                  P.op("pe", lambda e, pQ2=pQ2, gi_=gi_, h=h, nxt=nxt, cur=cur: e.matmul(pQ2[:, h * 128:(h + 1) * 128], lhsT=NMb[gi_][nxt][:, 2 + h, :], rhs=Qb[gi_][cur][:, h, :], start=True, stop=True),
                                       reads=[knxt, kq0], writes=[kQ2])
                              P.op("dve", lambda e, pQ2=pQ2, gi_=gi_, nxt=nxt, cur=cur: e.tensor_tensor(out=Qb[gi_][nxt][:, :, :], in0=pQ2[:, 0:256].rearrange("p (q n) -> p q n", q=2), in1=Qb[gi_][cur][:, :, :], op=ALU.add),
                                   reads=[kQ2, kq0], writes=[kq1])
                      qf = NLV % 2
                      ckpt(6.3)
                      for gi_, hp in enumerate(hps):
                          kq = "Qb%d_%d" % (gi_, qf)
                          pG1, kG1 = next_pr()
                          for h in range(2):
                              hs = slice(h * 64, (h + 1) * 64)
                              P.op("pe", lambda e, pG1=pG1, gi_=gi_, h=h, hs=hs: e.matmul(pG1[hs, 0:128], lhsT=TM[gi_][:, 0, hs], rhs=Qb[gi_][qf][:, h, :], start=True, stop=True),
                                   reads=["TM%d" % gi_, kq], writes=[kG1])
                              P.op("pe", lambda e, pG1=pG1, gi_=gi_, h=h, hs=hs: e.matmul(pG1[:, 128 + h * 64:128 + (h + 1) * 64], lhsT=QKs[gi_][h][:, 2, :], rhs=TM[gi_][:, 3, hs], start=True, stop=True),
                                   reads=["TM%d" % gi_, "QKs%d_%d" % (gi_, h)], writes=[kG1])
                          P.op("act", lambda e, pG1=pG1, gi_=gi_: e.activation(out=G1Tb[gi_][:], in_=pG1[:, 0:128], func=AF.Copy), reads=[kG1], writes=["G1Tb%d" % gi_])
                          P.op("act", lambda e, pG1=pG1, gi_=gi_: e.activation(out=Xb[gi_][:], in_=pG1[:, 128:256], func=AF.Copy), reads=[kG1], writes=["Xb%d" % gi_])
                          pG2, kG2 = next_pr()
                          for h in range(2):
                              hs = slice(h * 64, (h + 1) * 64)
                              P.op("pe", lambda e, pG2=pG2, gi_=gi_, h=h, hs=hs: e.matmul(pG2[:, h * 64:(h + 1) * 64], lhsT=Qb[gi_][qf][:, h, :], rhs=Xb[gi_][:, hs], start=True, stop=True),
                                   reads=[kq, "Xb%d" % gi_], writes=[kG2])
                          P.op("act", lambda e, pG2=pG2, gi_=gi_: e.activation(out=G2f[gi_][:], in_=pG2[:, 0:128], func=AF.Copy), reads=[kG2], writes=["G2f%d" % gi_])
                      ckpt(6.4)
                      for gi_, hp in enumerate(hps):
                          pU, kU = next_pr()
                          P.op("pe", lambda e, pU=pU, gi_=gi_, hp=hp: e.matmul(pU[:, 0:128], lhsT=G1Tb[gi_][:, :], rhs=Sb[:, hp, :], start=True, stop=True),
                               reads=["G1Tb%d" % gi_, ("Sb", hp)], writes=[kU])
                          P.op("dve", lambda e, pU=pU, gi_=gi_: e.tensor_tensor(out=Ub[gi_][:], in0=pU[:, 0:128], in1=G2f[gi_][:], op=ALU.add),
                               reads=[kU, "G2f%d" % gi_], writes=["Ub%d" % gi_])
                      ckpt(6.5)
                      for gi_, hp in enumerate(hps):
                          pY, kY = next_pr()
                          P.op("pe", lambda e, pY=pY, gi_=gi_, hp=hp: e.matmul(pY[:, 0:128], lhsT=Sb[:, hp, :], rhs=AR[gi_][:, 1, sl], start=True, stop=False),
                               reads=[("Sb", hp), "AR%d" % gi_], writes=[kY])
                          for h in range(2):
                              hs = slice(h * 64, (h + 1) * 64)
                              P.op("pe", lambda e, pY=pY, gi_=gi_, hs=hs, h=h: e.matmul(pY[hs, 0:128], lhsT=Ub[gi_][:, hs], rhs=QKs[gi_][h][:, 1, :], start=False, stop=False),
                                   reads=["Ub%d" % gi_, "QKs%d_%d" % (gi_, h)], writes=[kY])
                              P.op("pe", lambda e, pY=pY, gi_=gi_, hs=hs, h=h: e.matmul(pY[hs, 0:128], lhsT=TM[gi_][:, 3, hs], rhs=QKs[gi_][h][:, 3, :], start=False, stop=True),
                                   reads=["TM%d" % gi_, "QKs%d_%d" % (gi_, h)], writes=[kY])
                          ckpt(6.6)
                          P.op("pe", lambda e, pY=pY, gi_=gi_: e.matmul(pY[:, 128:256], lhsT=TM[gi_][:, 1, :], rhs=Ub[gi_][:, :], start=True, stop=False),
                               reads=["TM%d" % gi_, "Ub%d" % gi_], writes=[kY])
                          P.op("pe", lambda e, pY=pY, gi_=gi_: e.matmul(pY[:, 128:256], lhsT=TM[gi_][:, 2, :], rhs=TM[gi_][:, 3, :], start=False, stop=True),
                               reads=["TM%d" % gi_], writes=[kY])
                          ckpt(6.7)
                          P.op("act", lambda e, pY=pY, gi_=gi_: e.activation(out=Yf[gi_][:, sl], in_=pY[:, 0:128], func=AF.Copy), reads=[kY], writes=["Yf%d" % gi_])
                          ckpt(6.8)
                          P.op("act", lambda e, pY=pY, gi_=gi_: e.activation(out=Stmp[gi_][:], in_=pY[:, 128:256], func=AF.Copy),
                               reads=[kY], writes=["Stmp%d" % gi_])
                          ckpt(6.85)
                          P.op("pool", lambda e, gi_=gi_: e.tensor_tensor(out=Stmp[gi_][:], in0=Stmp[gi_][:], in1=bo1[:], op=ALU.mult),
                               reads=["Stmp%d" % gi_, "bo1"], writes=["Stmp%d" % gi_])
                          ckpt(6.87)
                          P.op("dve", lambda e, gi_=gi_, hp=hp, ch=ch: e.scalar_tensor_tensor(
                              out=Sf[:, hp, :], in0=Sf[:, hp, :], scalar=PCc[gi_][:, ch:ch + 1], in1=Stmp[gi_][:], op0=ALU.mult, op1=ALU.add),
                               reads=["Stmp%d" % gi_, ("Sf", hp), "PCc%d" % gi_], writes=[("Sf", hp)])
                          ckpt(6.9)
                          P.op("pool", lambda e, hp=hp: e.tensor_copy(out=Sb[:, hp, :], in_=Sf[:, hp, :]), reads=[("Sf", hp)], writes=[("Sb", hp)])

                  ckpt(7)
                  for gi_, hp in enumerate(hps):
                      ybf, ysq = Ht[0], Ht[2]
                      P.op("pool", lambda e, gi_=gi_: e.tensor_copy(out=ybf[:], in_=Yf[gi_][:]), reads=["Yf%d" % gi_], writes=["H0"])
                      P.op("act", lambda e, gi_=gi_: e.activation(out=ysq[:], in_=Yf[gi_][:], func=AF.Square), reads=["Yf%d" % gi_], writes=["H2"])
                      P.op("pe", lambda e: e.matmul(pa[:, 0:N], lhsT=bo64[:], rhs=ybf[:], start=True, stop=True), reads=["bo64", "H0"], writes=["pa"])
                      P.op("pe", lambda e: e.matmul(pb[:, 0:N], lhsT=bo64[:], rhs=ysq[:], start=True, stop=True), reads=["bo64", "H2"], writes=["pb"])
                      m2, var, rstd, t1 = Ft[0], Ft[1], Ft[2], Ft[3]
                      P.op("act", lambda e: e.activation(out=m2[:], in_=pa[:, 0:N], func=AF.Square), reads=["pa"], writes=["F0"])
                      P.op("dve", lambda e: e.tensor_tensor(out=var[:], in0=pb[:, 0:N], in1=m2[:], op=ALU.subtract), reads=["pb", "F0"], writes=["F1"])
                      P.op("dve", lambda e: e.tensor_scalar(out=var[:], in0=var[:], scalar1=GN_EPS, scalar2=None, op0=ALU.add), reads=["F1"], writes=["F1"])
                      P.op("act", lambda e: e.activation(out=var[:], in_=var[:], func=AF.Sqrt), reads=["F1"], writes=["F1"])
                      P.op("dve", lambda e: e.reciprocal(out=rstd[:], in_=var[:]), reads=["F1"], writes=["F2"])
                      P.op("dve", lambda e, gi_=gi_: e.tensor_tensor(out=t1[:], in0=Yf[gi_][:], in1=pa[:, 0:N], op=ALU.subtract), reads=["Yf%d" % gi_, "pa"], writes=["F3"])
                      P.op("pool", lambda e: e.tensor_tensor(out=t1[:], in0=t1[:], in1=rstd[:], op=ALU.mult), reads=["F3", "F2"], writes=["F3"])
                      P.op("dve", lambda e, hp=hp: e.tensor_scalar(out=t1[:], in0=t1[:], scalar1=pc("lnx_g", hp), scalar2=pc("lnx_b", hp), op0=ALU.mult, op1=ALU.add),
                           reads=["F3", "PT"], writes=["F3"])
                      pBn, kBn = next_pm()
                      P.op("pe", lambda e, pBn=pBn, gi_=gi_: e.matmul(pBn[:, 0:N], lhsT=bo1[:], rhs=rkb[gi_][:], start=True, stop=True), reads=["bo1", "rkb%d" % gi_], writes=[kBn])
                      t2 = Ft[4]
                      P.op("dve", lambda e, pBn=pBn, gi_=gi_: e.tensor_tensor(out=t2[:], in0=pBn[:, 0:N], in1=vTf[gi_][:], op=ALU.mult), reads=[kBn, "vTf%d" % gi_], writes=["F4"])
                      P.op("pool", lambda e: e.tensor_tensor(out=t1[:], in0=t1[:], in1=t2[:], op=ALU.add), reads=["F3", "F4"], writes=["F3"])
                      pGt, kGt = next_pm()
                      P.op("pe", lambda e, pGt=pGt, hp=hp: e.matmul(pGt[:, 0:N], lhsT=lorab[:, 2, hp * 128:(hp + 1) * 128], rhs=sg0[:], start=True, stop=False), reads=["lorab", "sg0"], writes=[kGt])
                      P.op("pe", lambda e, pGt=pGt, hp=hp: e.matmul(pGt[:, 0:N], lhsT=lorab[0:32, 3, hp * 128:(hp + 1) * 128], rhs=sg1[0:32, :], start=False, stop=True), reads=["lorab", "sg1"], writes=[kGt])
                      P.op("dve", lambda e, pGt=pGt, hp=hp: e.tensor_tensor(out=oT[hp], in0=pGt[:, 0:N], in1=t1[:], op=ALU.mult), reads=[kGt, "F3"], writes=[("HB0", hp)])
              if it == 0:
                  dump("oT", HB0[:, :], [128, 8 * N], KHB0, BF16)

              ckpt(8)
              for half in range(2):
                  wro, kro, _ = load_group("ro%d" % half)
                  wgb, kgb, _ = load_group("gB%d" % half)
                  for cc in range(4):
                      c = half * 4 + cc
                      pA, kA = next_pm()
                      mm_chunk(pA[:, 0:N], kA, wro, kro, 8, cc * 128, (cc + 1) * 128, lambda kc: oT[kc], [("HB0", k) for k in range(8)])
                      pG, kG = next_pm()
                      mm_chunk(pG[:, 0:N], kG, wgb, kgb, 8, cc * 128, (cc + 1) * 128, rhs_h, ["hT"])
                      sgt, sgk = Ft[6 + c % 2], "F%d" % (6 + c % 2)
                      tt, tk = Ft[8 + c % 2], "F%d" % (8 + c % 2)
                      P.op("act", lambda e, pG=pG, sgt=sgt, c=c: e.activation(out=sgt[:], in_=pG[:, 0:N], func=AF.Sigmoid, bias=pc("b_gate", 8 + c)),
                           reads=[kG, "PT"], writes=[sgk])
                      P.op("dve", lambda e, pA=pA, sgt=sgt, tt=tt: e.tensor_tensor(out=tt[:], in0=pA[:, 0:N], in1=sgt[:], op=ALU.mult),
                           reads=[kA, sgk], writes=[tk])
                      P.op("pool", lambda e, tt=tt, c=c: e.tensor_tensor(out=ybv[c], in0=tt[:], in1=yA[c], op=ALU.add),
                           reads=[tk, ("BIG1", c)], writes=[("HB1", c)])
                  done_group(2)
              P.op("pool", lambda e: e.tensor_copy(out=hT[:, :, 0:1], in_=hT[:, :, N:N + 1]), reads=["hT"], writes=["hT"])

              for half in range(2):
                  wwo, kwo, _ = load_group("wo%d" % half)
                  for cc in range(4):
                      c = half * 4 + cc
                      pA, kA = next_pm()
                      mm_chunk(pA[:, 0:N], kA, wwo, kwo, 8, cc * 128, (cc + 1) * 128, lambda kc: ybv[kc], [("HB1", k) for k in range(8)])
                      P.op("dve", lambda e, pA=pA, c=c: e.scalar_tensor_tensor(out=xres[:, c, :], in0=xres[:, c, :], scalar=ALPHA, in1=pA[:, 0:N], op0=ALU.mult, op1=ALU.add),
                           reads=[kA, "xres"], writes=["xres"])
                  done_group(1)
              ln_fm([xres[:, c, :] for c in range(8)], ["xres"] * 8, onesln, "onesln", LN_EPS, "ln1_g", "ln1_b", list(range(8)),
                    [[(xres[:, c, :], "xres", AF.Identity), (h1T[:, c, 2:N + 2], "h1T", AF.Identity)] for c in range(8)])
              if it == 0:
                  dump("h1", xres[:, :, :], [128, 8, N], ["xres"])

              ckpt(9)
              NW = N + 2
              for j in range(11):
                  wfu, kfu, _ = load_group("fu%d" % j)
                  for ff in range(2):
                      f = 2 * j + ff
                      outs = []
                      for part in range(2):
                          ci = part * 22 + f
                          pU, kU = next_pm()
                          mm_chunk(pU[:, 0:NW], kU, wfu, kfu, 8, part * 256 + ff * 128, part * 256 + (ff + 1) * 128,
                                   lambda kc: h1T[:, kc, 0:NW], ["h1T"])
                          ta, tka = Ft[(4 * f + 2 * part) % 12], "F%d" % ((4 * f + 2 * part) % 12)
                          tb, tkb = Ft[(4 * f + 2 * part + 1) % 12], "F%d" % ((4 * f + 2 * part + 1) % 12)
                          P.op("act", lambda e, pU=pU, ta=ta, ci=ci: e.activation(out=ta[:], in_=pU[:, 2:NW], func=AF.Identity, scale=pc("ffn_dw", 2 * 44 + ci)),
                               reads=[kU, "PT"], writes=[tka])
                          P.op("dve", lambda e, pU=pU, ta=ta, tb=tb, ci=ci: e.scalar_tensor_tensor(out=tb[:], in0=pU[:, 1:NW - 1], scalar=pc("ffn_dw", 1 * 44 + ci), in1=ta[:], op0=ALU.mult, op1=ALU.add),
                               reads=[kU, tka, "PT"], writes=[tkb])
                          P.op("dve", lambda e, pU=pU, ta=ta, tb=tb, ci=ci: e.scalar_tensor_tensor(out=ta[:], in0=pU[:, 0:NW - 2], scalar=pc("ffn_dw", 0 * 44 + ci), in1=tb[:], op0=ALU.mult, op1=ALU.add),
                               reads=[kU, tkb, "PT"], writes=[tka])
                          outs.append((ta, tka, tb, tkb))
                      (la, lka, _, _), (ga_, gka, gb_, gkb) = outs
                      P.op("act", lambda e, ga_=ga_, gb_=gb_: e.activation(out=gb_[:], in_=ga_[:], func=AF.Silu), reads=[gka], writes=[gkb])
                      P.op("pool", lambda e, la=la, gb_=gb_, f=f: e.tensor_tensor(out=act[:, f, :], in0=la[:], in1=gb_[:], op=ALU.mult),
                           reads=[lka, gkb], writes=[("act", f)])
                  done_group(1)
              P.op("pool", lambda e: e.tensor_copy(out=h1T[:, :, 0:2], in_=h1T[:, :, N:N + 2]), reads=["h1T"], writes=["h1T"])
              for m in range(8):
                  wfd, kfd, _ = load_group("fd%d" % m)
                  pA, kA = next_pm()
                  mm_chunk(pA[:, 0:N], kA, wfd, kfd, 22, 0, 128, lambda kc: act[:, kc, :], [("act", k) for k in range(22)])
                  P.op("dve", lambda e, pA=pA, m=m: e.scalar_tensor_tensor(out=xres[:, m, :], in0=xres[:, m, :], scalar=ALPHA, in1=pA[:, 0:N], op0=ALU.mult, op1=ALU.add),
                       reads=[kA, "xres"], writes=["xres"])
                  done_group(1)
              ln_fm([xres[:, c, :] for c in range(8)], ["xres"] * 8, onesln, "onesln", LN_EPS, "ln2_g", "ln2_b", list(range(8)),
                    [[(xres[:, c, :], "xres", AF.Identity)] for c in range(8)])
              lo = max(t0, NMETA)
              hi = min(t0 + N, T_REAL)
              if hi > lo:
                  P.dma(lambda e, lo=lo, hi=hi, t0=t0: e.dma_start(
                      out=outT[:, lo - NMETA:hi - NMETA].rearrange("(c p) n -> p c n", p=128),
                      in_=xres[:, :, lo - t0:hi - t0]), reads=["xres"])

        except StopBuild:
            pass
        P.final_wait()
        P.emit()
        nops = P.nops
    return nc, dbg_out, nops


def host_layout(inputs):
    groups, totw = weight_groups()
    pcol, npar = par_layout()
    W = {k: np.asarray(inputs[k], np.float32)[0] for k in ("w_in", "w_conv_out", "w_rwkv_out", "w_o", "ffn_up", "ffn_down")}
    wall = np.empty((128, totw), np.float32)
    for g in groups:
        src = W[g["src"]][:, g["cols"]]
        kc = g["kc_src"]
        blk = src.reshape(kc, 128, g["m"]).transpose(1, 0, 2)
        wall[:, g["off"]:g["off"] + kc * g["m"]] = blk.reshape(128, kc * g["m"])
    par = np.zeros((128, npar), np.float32)
    def put(name, vec):
        cv = colvec(vec)
        par[:, pcol[name]:pcol[name] + cv.shape[1]] = cv
    put("ln_in_g", inputs["ln_in_g"]); put("ln_in_b", inputs["ln_in_b"])
    cdw = np.asarray(inputs["conv_dw"], np.float32)[0]
    par[:, pcol["conv_dw"]:pcol["conv_dw"] + CONVK * 8] = cdw.reshape(CONVK, 8, 128).transpose(2, 0, 1).reshape(128, CONVK * 8)
    for nm in ("conv_dw_b", "conv_ln_g", "conv_ln_b", "b_gate", "w0", "a0", "k_k", "k_a", "r_k", "lnx_g", "lnx_b",
               "ln1_g", "ln1_b", "ln2_g", "ln2_b"):
        put(nm, np.asarray(inputs[nm], np.float32)[0].reshape(-1))
    fdw = np.asarray(inputs["ffn_dw"], np.float32)[0]
    par[:, pcol["ffn_dw"]:pcol["ffn_dw"] + 3 * 44] = fdw.reshape(3, 44, 128).transpose(2, 0, 1).reshape(128, 3 * 44)
    mub = np.ascontiguousarray(np.broadcast_to(np.asarray(inputs["rwkv_mu"], np.float32)[0][None, :], (128, NRW)))
    lora = np.zeros((128, 4, D), np.float32)
    lora[0:64, 0] = np.asarray(inputs["w_decay_up"], np.float32)[0]
    lora[0:64, 1] = np.asarray(inputs["a_up"], np.float32)[0]
    gup = np.asarray(inputs["g_up"], np.float32)[0]
    lora[:, 2] = gup[0:128]
    lora[0:32, 3] = gup[128:160]
    x = np.asarray(inputs["x"], np.float32)
    meta = np.asarray(inputs["meta_tokens"], np.float32)
    xTs = []
    for b in range(x.shape[0]):
        xt = np.zeros((D, TP), np.float32)
        xt[:, 0:NMETA] = meta.T
        xt[:, NMETA:T_REAL] = x[b].T
        xTs.append(xt)
    return wall, par, mub, lora, xTs


_CACHE = {}


def kernel(**inputs):
    nt_run = int(os.environ.get("KNT", NT_FULL))
    dbg_names = tuple(x for x in os.environ.get("KDBG", "").split(",") if x)
    ncores = int(os.environ.get("KCORES", 8))
    key = (nt_run, dbg_names)
    if key not in _CACHE:
        _CACHE[key] = build_program(nt_run, dbg_names)
    nc, dbg_out, nops = _CACHE[key]
    wall, par, mub, lora, xTs = host_layout(inputs)
    in_maps = [{"xT": xTs[b], "wall": wall, "par": par, "mub": mub, "lora": lora} for b in range(ncores)]
    res = run_bass_kernel_spmd(nc, in_maps, core_ids=list(range(ncores)))
    out = np.zeros((8, SEQ, D), np.float32)
    for b in range(ncores):
        out[b] = np.asarray(res.results[b]["outT"], np.float32).T
    if dbg_names:
        kernel.dbg = [{n: np.asarray(res.results[b]["dbg_" + n]) for n in dbg_out} for b in range(ncores)]
    return out
```

```python
import os
import math
import types
import numpy as np
from contextlib import ExitStack
import concourse.bass as bass
import concourse.mybir as mybir
from concourse.bass_utils import run_bass_kernel_spmd

F32 = mybir.dt.float32
BF16 = mybir.dt.bfloat16
AF = mybir.ActivationFunctionType
ALU = mybir.AluOpType

D = 1024
DC = 8
N = 384
NCH = 3
CH = 128
NMETA = 16
SEQ = 4096
T_REAL = SEQ + NMETA
TP = 4224
NT_FULL = TP // N
DFF = 2816
FC = 22
ALPHA = 2.0 ** 0.25
LN_EPS = 1e-5
GN_EPS = 64e-5
C0 = math.exp(-0.5)
CONVK = 31
HALO = CONVK - 1
NRW = 3360
GSZ = 4096
NSLOT = 3
SEM_ROLL = 6000
ENGS = ("pe", "act", "dve", "pool", "sp")


def freeze(fn):
    if fn is None or fn.__closure__ is None:
        return fn
    cells = []
    for c in fn.__closure__:
        try:
            cells.append(types.CellType(c.cell_contents))
        except ValueError:
            cells.append(c)
    return types.FunctionType(fn.__code__, fn.__globals__, fn.__name__, fn.__defaults__, tuple(cells))


class Prog:
    def __init__(self, nc, es, n_dma_sems=12):
        self.nc = nc
        self.es = es
        self.streams = {e: [] for e in ENGS}
        self.cnt = {e: 0 for e in ENGS}
        self.cur_sem = {}
        self.nsem = 0
        for e in ENGS:
            self._new_eng_sem(e)
        self.dma_sems = [self._sem("dma%d" % i) for i in range(n_dma_sems)]
        self.dma_val = [0] * n_dma_sems
        self.dma_rr = 0
        self.lastw = {}
        self.readers = {}
        self.waited = {e: {} for e in ENGS}
        self.nops = 0

    def _sem(self, name):
        self.nsem += 1
        return self.es.enter_context(self.nc.semaphore(name))

    def _new_eng_sem(self, e):
        self.cur_sem[e] = self._sem("s_%s_%d" % (e, self.nsem))
        self.cnt[e] = 0

    def _deps(self, eng, reads, writes):
        toks = []
        for k in reads:
            t = self.lastw.get(k)
            if t is not None:
                toks.append(t)
        for k in writes:
            t = self.lastw.get(k)
            if t is not None:
                toks.append(t)
            toks.extend(self.readers.get(k, ()))
        waits = {}
        for (sem, val, src) in toks:
            if src == "pe" and eng == "pe":
                continue
            key = id(sem)
            if self.waited[eng].get(key, 0) >= val:
                continue
            if key not in waits or waits[key][1] < val:
                waits[key] = (sem, val)
        for key, (sem, val) in waits.items():
            self.waited[eng][key] = val
        return list(waits.values())

    def _commit(self, tok, reads, writes):
        for k in reads:
            if k in writes:
                continue
            self.readers.setdefault(k, []).append(tok)
        for k in writes:
            self.lastw[k] = tok
            self.readers[k] = []

    def op(self, eng, fn, reads=(), writes=()):
        fn = freeze(fn)
        reads = tuple(reads)
        writes = tuple(writes)
        waits = self._deps(eng, reads, writes)
        if self.cnt[eng] >= SEM_ROLL:
            self._new_eng_sem(eng)
        self.cnt[eng] += 1
        sem = self.cur_sem[eng]
        tok = (sem, self.cnt[eng], eng)
        self.streams[eng].append((waits, fn, sem, 1))
        self._commit(tok, reads, writes)
        self.nops += 1

    def dma(self, fn, reads=(), writes=(), eng="sp"):
        fn = freeze(fn)
        reads = tuple(reads)
        writes = tuple(writes)
        i = self.dma_rr
        self.dma_rr = (self.dma_rr + 1) % len(self.dma_sems)
        sem = self.dma_sems[i]
        waits = self._deps(eng, reads, writes)
        if self.dma_val[i] > 0 and self.waited[eng].get(id(sem), 0) < self.dma_val[i]:
            waits = [w for w in waits if w[0] is not sem] + [(sem, self.dma_val[i])]
            self.waited[eng][id(sem)] = self.dma_val[i]
        self.dma_val[i] += 16
        tok = (sem, self.dma_val[i], "dma")
        self.streams[eng].append((waits, fn, sem, 16))
        self._commit(tok, reads, writes)
        self.nops += 1

    def final_wait(self, eng="sp"):
        waits = []
        for i, sem in enumerate(self.dma_sems):
            if self.dma_val[i] > 0:
                waits.append((sem, self.dma_val[i]))
        self.streams[eng].append((waits, None, None, 0))

    def emit(self):
        nc = self.nc
        with nc.Block() as block:
            def run(engname):
                def body(e):
                    for (waits, fn, sem, inc) in self.streams[engname]:
                        for (s, v) in waits:
                            e.wait_ge(s, v)
                        if fn is not None:
                            ins = fn(e)
                            ins.then_inc(sem, inc)
                return body
            block.tensor(run("pe"))
            block.scalar(run("act"))
            block.vector(run("dve"))
            block.gpsimd(run("pool"))
            block.sync(run("sp"))


def weight_groups():
    g = []
    def add(name, kind, kc_src, m, src, cols):
        g.append(dict(name=name, kind=kind, kc_src=kc_src, m=m, src=src, cols=np.asarray(cols)))
    for i in range(2):
        add("ga%d" % i, "plain", 8, 512, "w_in", np.arange(i * 512, (i + 1) * 512))
        add("gg%d" % i, "plain", 8, 512, "w_in", 1024 + np.arange(i * 512, (i + 1) * 512))
    for i in range(2):
        add("co%d" % i, "plain", 8, 512, "w_conv_out", np.arange(i * 512, (i + 1) * 512))
        add("gA%d" % i, "plain", 8, 512, "w_in", 5408 + np.arange(i * 512, (i + 1) * 512))
    add("lr0", "rwkv", 8, 128, "w_in", 2048 + 3072 + np.arange(0, 128))
    add("lr1", "rwkv", 8, 160, "w_in", 2048 + 3072 + 128 + np.arange(0, 160))
    for i in range(4):
        for j, nm in enumerate("rkv"):
            add("%s%d" % (nm, i), "rwkv", 8, 256, "w_in", 2048 + j * 1024 + np.arange(i * 256, (i + 1) * 256))
    for i in range(2):
        add("ro%d" % i, "plain", 8, 512, "w_rwkv_out", np.arange(i * 512, (i + 1) * 512))
        add("gB%d" % i, "plain", 8, 512, "w_in", 6432 + np.arange(i * 512, (i + 1) * 512))
    for i in range(2):
        add("wo%d" % i, "plain", 8, 512, "w_o", np.arange(i * 512, (i + 1) * 512))
    for j in range(11):
        cols = np.concatenate([np.arange(2 * j * 128, (2 * j + 2) * 128),
                               DFF + np.arange(2 * j * 128, (2 * j + 2) * 128)])
        add("fu%d" % j, "plain", 8, 512, "ffn_up", cols)
    for m in range(8):
        add("fd%d" % m, "plain", 22, 128, "ffn_down", np.arange(m * 128, (m + 1) * 128))
    off = 0
    for i, x in enumerate(g):
        x["idx"] = i
        x["off"] = off
        x["kc_out"] = x["kc_src"] * (2 if x["kind"] == "rwkv" else 1)
        off += x["kc_src"] * x["m"]
    return g, off


def par_layout():
    cols = {}
    off = 0
    def add(name, n):
        nonlocal off
        cols[name] = off
        off += n
    add("ln_in_g", 8); add("ln_in_b", 8)
    add("conv_dw", CONVK * 8)
    add("conv_dw_b", 8); add("conv_ln_g", 8); add("conv_ln_b", 8)
    add("b_gate", 16)
    add("w0", 8); add("a0", 8); add("k_k", 8); add("k_a", 8); add("r_k", 8)
    add("lnx_g", 8); add("lnx_b", 8)
    add("ln1_g", 8); add("ln1_b", 8)
    add("ffn_dw", 3 * 44)
    add("ln2_g", 8); add("ln2_b", 8)
    add("omka", 8)
    return cols, off


def colvec(v):
    v = np.asarray(v, np.float32).reshape(-1, 128)
    return np.ascontiguousarray(v.T)


def build_program(nt_run, dbg_names=()):
    groups, totw = weight_groups()
    gidx = {x["name"]: x for x in groups}
    NG = len(groups)
    pcol, npar = par_layout()

    nc = bass.Bass("TRN2", target_bir_lowering=False)
    xT = nc.dram_tensor("xT", [D, TP], F32, kind="ExternalInput").ap()
    wall = nc.dram_tensor("wall", [128, totw], F32, kind="ExternalInput").ap()
    par = nc.dram_tensor("par", [128, npar], F32, kind="ExternalInput").ap()
    mub = nc.dram_tensor("mub", [128, NRW], F32, kind="ExternalInput").ap()
    lora = nc.dram_tensor("lora", [128, 4, D], F32, kind="ExternalInput").ap()
    outT = nc.dram_tensor("outT", [D, SEQ], F32, kind="ExternalOutput").ap()
    wsc = nc.dram_tensor("wsc", [NG, 128, GSZ], BF16, kind="Internal").ap()
    dbg_out = {}

    with ExitStack() as es:
        P = Prog(nc, es)

        def sb(name, shape, dt):
            return es.enter_context(nc.sbuf_tensor(name, shape, dt))

        def psum(name):
            return es.enter_context(nc.psum_tensor(name, [128, 512], F32))

        PT = sb("PT", [128, npar], F32)
        ident = sb("ident", [128, 128], BF16)
        onesln = sb("onesln", [128, 128], BF16)
        bo64 = sb("bo64", [128, 128], BF16)
        bo1 = sb("bo1", [128, 128], BF16)
        mask4 = sb("mask4", [128, 4, 128], F32)
        ones_f = sb("ones_f", [128, 128], F32)
        lorab = sb("lorab", [128, 4, D], BF16)
        ring = [sb("ring%d" % i, [128, GSZ], BF16) for i in range(NSLOT)]
        BIG0 = sb("BIG0", [128, 8 * N], F32)
        BIG1 = sb("BIG1", [128, 8 * N], F32)
        HB0 = sb("HB0", [128, 8 * N], BF16)
        HB1 = sb("HB1", [128, 8 * N], BF16)
        mut = [sb("mut%d" % i, [128, 256], F32) for i in range(2)]
        xres = sb("xres", [128, 8, N], F32)
        hT = sb("hT", [128, 8, N + 1], BF16)
        h1T = sb("h1T", [128, 8, N + 2], BF16)
        cb = sb("cb", [128, 8, N + HALO], BF16)
        diag = [sb("diag%d" % i, [128, 16, 128], BF16) for i in range(2)]
        NF = 16
        NH = 6
        Ft = [sb("F%d" % i, [128, N], F32) for i in range(NF)]
        Ht = [sb("H%d" % i, [128, N], BF16) for i in range(NH)]
        act = sb("act", [128, FC, N], BF16)
        tw = sb("tw", [128, N], BF16)
        xab = sb("xab", [128, N], BF16)
        sg0 = sb("sg0", [128, N], BF16)
        sg1 = sb("sg1", [128, N], BF16)
        GP = 2
        AR = [sb("AR%d" % i, [128, 2, N], BF16) for i in range(GP)]
        BK = [sb("BK%d" % i, [128, 2, N], BF16) for i in range(GP)]
        BKh = [sb("BKh%d" % i, [128, 2, N], BF16) for i in range(GP)]
        vb = [sb("vb%d" % i, [128, N], BF16) for i in range(GP)]
        vTf = [sb("vTf%d" % i, [128, N], F32) for i in range(GP)]
        rkb = [sb("rkb%d" % i, [128, N], BF16) for i in range(GP)]
        Yf = [sb("Yf%d" % i, [128, N], F32) for i in range(GP)]
        PCc = [sb("PCc%d" % i, [128, NCH], F32) for i in range(GP)]
        TM = [sb("TM%d" % i, [128, 4, 128], BF16) for i in range(GP)]
        QKs = [[sb("QKs%d_%d" % (i, h), [128, 4, 128], BF16) for h in range(2)] for i in range(GP)]
        NMb = [[sb("NMb%d_%d" % (i, k), [128, 4, 128], BF16) for k in range(2)] for i in range(GP)]
        Qb = [[sb("Qb%d_%d" % (i, k), [128, 2, 128], BF16) for k in range(2)] for i in range(GP)]
        G1Tb = [sb("G1Tb%d" % i, [128, 128], BF16) for i in range(GP)]
        Xb = [sb("Xb%d" % i, [128, 128], BF16) for i in range(GP)]
        G2f = [sb("G2f%d" % i, [128, 128], F32) for i in range(GP)]
        Ub = [sb("Ub%d" % i, [128, 128], BF16) for i in range(GP)]
        Stmp = [sb("Stmp%d" % i, [128, 128], F32) for i in range(GP)]
        Sf = sb("Sf", [128, 8, 128], F32)
        Sb = sb("Sb", [128, 8, 128], BF16)
        nbC = sb("nbC", [128, NCH], F32)

        pm = [psum("pm%d" % i) for i in range(3)]
        pa = psum("pa")
        pb = psum("pb")
        pr = [psum("pr%d" % i) for i in range(3)]
        PRR = pr + pm
        rr = {"pm": 0, "pr": 0, "slot": 0}

        PMR = pm + pr

        def next_pm():
            i = rr["pm"]; rr["pm"] = (i + 1) % 6
            return PMR[i], ("pm%d" % i if i < 3 else "pr%d" % (i - 3))

        def next_pr():
            i = rr["pr"]; rr["pr"] = (i + 1) % 6
            return PRR[i], ("pr%d" % i if i < 3 else "pm%d" % (i - 3))

        def pc(name, c):
            o = pcol[name] + c
            return PT[:, o:o + 1]

        def dump(name, ap, shape, keys, dt=F32):
            if name not in dbg_names:
                return
            t = nc.dram_tensor("dbg_" + name, list(shape), dt, kind="ExternalOutput").ap()
            dbg_out[name] = shape
            P.dma(lambda e: e.dma_start(out=t, in_=ap), reads=list(keys))

        P.dma(lambda e: e.dma_start(out=PT[:, 0:npar], in_=par[:, :]), writes=["PT"])
        P.op("pool", lambda e: e.memset(ident[:], 1.0), writes=["ident"])
        P.op("pool", lambda e: e.affine_select(out=ident[:], in_=ident[:], pattern=[[-1, 128]],
                                                compare_op=ALU.is_equal, fill=0.0, base=0,
                                                channel_multiplier=1), reads=["ident"], writes=["ident"])
        P.op("pool", lambda e: e.memset(onesln[:], 1.0 / D), writes=["onesln"])
        P.op("pool", lambda e: e.memset(ones_f[:], 1.0), writes=["ones_f"])
        for (t_, v_, k_) in ((bo64, 1.0 / 64, "bo64"), (bo1, 1.0, "bo1")):
            P.op("pool", lambda e, t_=t_: e.memset(t_[:], 0.0), writes=[k_])
            P.op("pool", lambda e, t_=t_, v_=v_: e.memset(t_[0:64, 0:64], v_), reads=[k_], writes=[k_])
            P.op("pool", lambda e, t_=t_, v_=v_: e.memset(t_[64:128, 64:128], v_), reads=[k_], writes=[k_])
        P.op("pool", lambda e: e.memset(mask4[:], 1.0), writes=["mask4"])
        for q in range(4):
            base = -1 if q % 2 == 0 else 0
            P.op("pool", lambda e, q=q, base=base: e.affine_select(
                out=mask4[:, q, :], in_=mask4[:, q, :], pattern=[[1, 128]], compare_op=ALU.is_ge,
                fill=0.0, base=base, channel_multiplier=-1), reads=["mask4"], writes=["mask4"])
        P.op("dve", lambda e: e.tensor_scalar(out=PT[:, pcol["omka"]:pcol["omka"] + 8],
                                              in0=PT[:, pcol["k_a"]:pcol["k_a"] + 8],
                                              scalar1=-1.0, scalar2=1.0, op0=ALU.mult, op1=ALU.add),
             reads=["PT"], writes=["PT"])
        P.op("pool", lambda e: e.memset(Sf[:], 0.0), writes=["Sf"])
        P.op("pool", lambda e: e.memset(Sb[:], 0.0), writes=["Sb"])
        P.op("pool", lambda e: e.memset(hT[:, :, 0:1], 0.0), writes=["hT"])
        P.op("pool", lambda e: e.memset(h1T[:, :, 0:2], 0.0), writes=["h1T"])
        P.op("pool", lambda e: e.memset(cb[:, :, 0:HALO], 0.0), writes=["cb"])
        KBIG0 = [("BIG0", c) for c in range(8)]
        KBIG1 = [("BIG1", c) for c in range(8)]
        KHB0 = [("HB0", c) for c in range(8)]
        KHB1 = [("HB1", c) for c in range(8)]
        for half in range(2):
            P.dma(lambda e, half=half: e.dma_start(out=BIG0[:, 0:2 * D], in_=lora[:, 2 * half:2 * half + 2, :]),
                  writes=KBIG0)
            P.op("dve", lambda e, half=half: e.tensor_copy(out=lorab[:, 2 * half:2 * half + 2, :], in_=BIG0[:, 0:2 * D]),
                 reads=KBIG0, writes=["lorab"])

        stg32 = [(BIG0, KBIG0), (BIG1, KBIG1)]
        stg16 = [(HB0, KHB0), (HB1, KHB1)]
        pp = 0
        cast_engs = ["dve", "pool"]
        for g in groups:
            kcs = g["kc_src"] // 2
            m = g["m"]
            ne = kcs * m
            for half in range(2):
                if g["kind"] == "plain":
                    s32, k32 = stg32[pp % 2]
                    s16, k16 = stg16[pp % 2]
                    ce = cast_engs[pp % 2]
                    pp += 1
                    so = g["off"] + half * ne
                    P.dma(lambda e, s32=s32, so=so, ne=ne: e.dma_start(out=s32[:, 0:ne], in_=wall[:, so:so + ne]),
                          writes=k32)
                    if ce == "act":
                        P.op("act", lambda e, s16=s16, s32=s32, ne=ne: e.activation(out=s16[:, 0:ne], in_=s32[:, 0:ne], func=AF.Copy),
                             reads=k32, writes=k16)
                    else:
                        P.op(ce, lambda e, s16=s16, s32=s32, ne=ne: e.tensor_copy(out=s16[:, 0:ne], in_=s32[:, 0:ne]),
                             reads=k32, writes=k16)
                    P.dma(lambda e, s16=s16, gi=g["idx"], half=half, ne=ne: e.dma_start(
                        out=wsc[gi, :, half * ne:(half + 1) * ne], in_=s16[:, 0:ne]),
                        reads=k16, writes=[("wsc", g["idx"])], eng="act")
                else:
                    mu_t = mut[pp % 2]; kmu = "mut%d" % (pp % 2)
                    pp += 1
                    so = g["off"] + half * ne
                    c0 = int(g["cols"][0]) - 2048
                    P.dma(lambda e, so=so, ne=ne: e.dma_start(out=BIG0[:, 0:ne], in_=wall[:, so:so + ne]),
                          writes=KBIG0)
                    P.dma(lambda e, mu_t=mu_t, c0=c0, m=m: e.dma_start(out=mu_t[:, 0:m], in_=mub[:, c0:c0 + m]),
                          writes=[kmu])
                    for kk_ in range(kcs):
                        P.op("dve", lambda e, kk_=kk_, m=m, mu_t=mu_t: e.tensor_tensor(
                            out=BIG1[:, kk_ * m:(kk_ + 1) * m], in0=BIG0[:, kk_ * m:(kk_ + 1) * m],
                            in1=mu_t[:, 0:m], op=ALU.mult), reads=KBIG0 + [kmu], writes=KBIG1)
                    P.op("pool", lambda e, ne=ne: e.tensor_copy(out=HB1[:, 0:ne], in_=BIG1[:, 0:ne]),
                         reads=KBIG1, writes=KHB1)
                    P.op("dve", lambda e, ne=ne: e.tensor_tensor(out=HB0[:, 0:ne], in0=BIG0[:, 0:ne],
                                                                in1=BIG1[:, 0:ne], op=ALU.subtract),
                         reads=KBIG0 + KBIG1, writes=KHB0)
                    gi = g["idx"]
                    o1 = half * ne
                    o2 = 8 * m + half * ne
                    P.dma(lambda e, gi=gi, o1=o1, ne=ne: e.dma_start(out=wsc[gi, :, o1:o1 + ne], in_=HB0[:, 0:ne]),
                          reads=KHB0, writes=[("wsc", gi)], eng="act")
                    P.dma(lambda e, gi=gi, o2=o2, ne=ne: e.dma_start(out=wsc[gi, :, o2:o2 + ne], in_=HB1[:, 0:ne]),
                          reads=KHB1, writes=[("wsc", gi)], eng="act")

        seq = [g["name"] for _ in range(nt_run) for g in groups]
        issued = []
        st = {"cur": 0, "released": 0}

        def issue_load():
            idx = len(issued)
            g = gidx[seq[idx]]
            i = idx % NSLOT
            slot = ring[i]; key = "ring%d" % i
            ne = g["kc_out"] * g["m"]
            P.dma(lambda e: e.dma_start(out=slot[:, 0:ne], in_=wsc[g["idx"], :, 0:ne]),
                  reads=[("wsc", g["idx"])], writes=[key])
            issued.append((slot, key, g))

        def pump():
            while len(issued) < min(len(seq), st["released"] + NSLOT):
                issue_load()

        def load_group(name):
            cur = st["cur"]
            assert seq[cur] == name, (seq[cur], name)
            pump()
            assert len(issued) > cur, "weight ring too shallow for simultaneously open groups"
            slot, key, g = issued[cur]
            st["cur"] = cur + 1
            m = g["m"]
            def w(kc, c0, c1):
                return slot[:, kc * m + c0:kc * m + c1]
            return w, key, g

        def done_group(n=1):
            st["released"] += n
            pump()

        convo = [BIG0[:, c * N:(c + 1) * N] for c in range(8)]
        yA = [BIG1[:, c * N:(c + 1) * N] for c in range(8)]
        cs_ = [HB0[:, c * N:(c + 1) * N] for c in range(8)]
        oT = cs_
        ybv = [HB1[:, c * N:(c + 1) * N] for c in range(8)]

        def ln_fm(srcs, skeys, ones_t, ones_key, eps, gname, bname, cidx, outs):
            nchunk = len(srcs)
            for i, (s, sk) in enumerate(zip(srcs, skeys)):
                xbt, xbk = Ht[i % 2], "H%d" % (i % 2)
                sqt, sqk = Ht[2 + i % 2], "H%d" % (2 + i % 2)
                P.op("pool", lambda e, s=s, xbt=xbt: e.tensor_copy(out=xbt[:], in_=s), reads=[sk], writes=[xbk])
                P.op("act", lambda e, s=s, sqt=sqt: e.activation(out=sqt[:], in_=s, func=AF.Square), reads=[sk], writes=[sqk])
                P.op("pe", lambda e, xbt=xbt, i=i: e.matmul(pa[:, 0:N], lhsT=ones_t[:], rhs=xbt[:], start=(i == 0), stop=(i == nchunk - 1)),
                     reads=[xbk, ones_key], writes=["pa"])
                P.op("pe", lambda e, sqt=sqt, i=i: e.matmul(pb[:, 0:N], lhsT=ones_t[:], rhs=sqt[:], start=(i == 0), stop=(i == nchunk - 1)),
                     reads=[sqk, ones_key], writes=["pb"])
            m2, var, rstd, nb = Ft[0], Ft[1], Ft[2], Ft[3]
            P.op("act", lambda e: e.activation(out=m2[:], in_=pa[:, 0:N], func=AF.Square), reads=["pa"], writes=["F0"])
            P.op("dve", lambda e: e.tensor_tensor(out=var[:], in0=pb[:, 0:N], in1=m2[:], op=ALU.subtract), reads=["pb", "F0"], writes=["F1"])
            P.op("dve", lambda e: e.tensor_scalar(out=var[:], in0=var[:], scalar1=eps, scalar2=None, op0=ALU.add), reads=["F1"], writes=["F1"])
            P.op("act", lambda e: e.activation(out=var[:], in_=var[:], func=AF.Sqrt), reads=["F1"], writes=["F1"])
            P.op("dve", lambda e: e.reciprocal(out=rstd[:], in_=var[:]), reads=["F1"], writes=["F2"])
            P.op("dve", lambda e: e.scalar_tensor_tensor(out=nb[:], in0=pa[:, 0:N], scalar=-1.0, in1=rstd[:], op0=ALU.mult, op1=ALU.mult),
                 reads=["pa", "F2"], writes=["F3"])
            for i, (s, sk) in enumerate(zip(srcs, skeys)):
                t0, k0 = Ft[4 + i % 2], "F%d" % (4 + i % 2)
                P.op("dve", lambda e, s=s, t0=t0: e.tensor_tensor(out=t0[:], in0=s, in1=rstd[:], op=ALU.mult), reads=[sk, "F2"], writes=[k0])
                P.op("pool", lambda e, t0=t0: e.tensor_tensor(out=t0[:], in0=t0[:], in1=nb[:], op=ALU.add), reads=[k0, "F3"], writes=[k0])
                c = cidx[i]
                for (oap, okey, func) in outs[i]:
                    P.op("act", lambda e, oap=oap, t0=t0, func=func, c=c: e.activation(
                        out=oap, in_=t0[:], func=func, scale=pc(gname, c), bias=pc(bname, c)),
                        reads=[k0, "PT"], writes=[okey])

        def mm_chunk(out_ap, okey, w, wkey, kcn, c0, c1, rhs_fn, rkeys):
            for kc in range(kcn):
                P.op("pe", lambda e, kc=kc: e.matmul(out_ap, lhsT=w(kc, c0, c1), rhs=rhs_fn(kc), start=(kc == 0), stop=(kc == kcn - 1)),
                     reads=[wkey] + list(rkeys), writes=[okey])

        def rhs_h(kc):
            return hT[:, kc, 1:N + 1]

        def rhs_hs(kc):
            if kc < 8:
                return hT[:, kc, 1:N + 1]
            return hT[:, kc - 8, 0:N]

        class StopBuild(Exception):
            pass
        kstop = float(os.environ.get("KSTOP", "99"))

        def ckpt(i):
            if kstop <= i:
                raise StopBuild()

        try:
          ckpt(0)
          for it in range(nt_run):
              t0 = it * N
              P.dma(lambda e, t0=t0: e.dma_start(out=xres[:, :, :], in_=xT[:, t0:t0 + N].rearrange("(c p) n -> p c n", p=128)),
                    writes=["xres"])
              ckpt(1)
              ln_fm([xres[:, c, :] for c in range(8)], ["xres"] * 8, onesln, "onesln", LN_EPS, "ln_in_g", "ln_in_b",
                    list(range(8)),
                    [[(xres[:, c, :], "xres", AF.Identity), (hT[:, c, 1:N + 1], "hT", AF.Identity)] for c in range(8)])
              if it == 0:
                  dump("h", xres[:, :, :], [128, 8, N], ["xres"])
              ckpt(2)
              for half in range(2):
                  wa, ka, _ = load_group("ga%d" % half)
                  wg, kg, _ = load_group("gg%d" % half)
                  for cc in range(4):
                      c = half * 4 + cc
                      pA, kA = next_pm()
                      mm_chunk(pA[:, 0:N], kA, wa, ka, 8, cc * 128, (cc + 1) * 128, rhs_h, ["hT"])
                      pG, kG = next_pm()
                      mm_chunk(pG[:, 0:N], kG, wg, kg, 8, cc * 128, (cc + 1) * 128, rhs_h, ["hT"])
                      sgt, sgk = Ft[6 + c % 2], "F%d" % (6 + c % 2)
                      P.op("act", lambda e, pG=pG, sgt=sgt: e.activation(out=sgt[:], in_=pG[:, 0:N], func=AF.Sigmoid), reads=[kG], writes=[sgk])
                      P.op("dve", lambda e, pA=pA, sgt=sgt, c=c: e.tensor_tensor(out=cb[:, c, HALO:HALO + N], in0=pA[:, 0:N], in1=sgt[:], op=ALU.mult),
                           reads=[kA, sgk], writes=[("cb", c)])
                      if it == 0 and c == 0 and "dbgA" in dbg_names:
                          P.op("act", lambda e, pA=pA: e.activation(out=Ft[10][:], in_=pA[:, 0:N], func=AF.Copy), reads=[kA], writes=["F10"])
                          dump("dbgA", Ft[10][:], [128, N], ["F10"])
                          dump("dbgS", sgt[:], [128, N], [sgk])
                  done_group(2)
              if it == 0:
                  dump("hT", hT[:, :, :], [128, 8, N + 1], ["hT"], BF16)
                  dump("cb", cb[:, :, :], [128, 8, N + HALO], [("cb", c) for c in range(8)], BF16)
              ckpt(3)
              for c in range(8):
                  for j in range(CONVK):
                      dg, dk, jj = diag[j // 16], "diag%d" % (j // 16), j % 16
                      P.op("pool", lambda e, dg=dg, jj=jj, j=j, c=c: e.tensor_scalar(
                          out=dg[:, jj, :], in0=ident[:], scalar1=pc("conv_dw", j * 8 + c), scalar2=0.0, op0=ALU.mult, op1=ALU.add),
                          reads=["ident", "PT"], writes=[dk])
                  pC, kC = next_pm()
                  for j in range(CONVK):
                      dg, dk, jj = diag[j // 16], "diag%d" % (j // 16), j % 16
                      P.op("pe", lambda e, dg=dg, jj=jj, j=j, c=c, pC=pC: e.matmul(pC[:, 0:N], lhsT=dg[:, jj, :], rhs=cb[:, c, j:j + N], start=(j == 0), stop=(j == CONVK - 1)),
                           reads=[dk, ("cb", c), "cb"], writes=[kC])
                  P.op("act", lambda e, c=c, pC=pC: e.activation(out=convo[c], in_=pC[:, 0:N], func=AF.Identity, bias=pc("conv_dw_b", c)),
                       reads=[kC, "PT"], writes=[("BIG0", c)])
                  P.op("pool", lambda e, c=c: e.tensor_copy(out=cb[:, c, 0:HALO], in_=cb[:, c, N:N + HALO]), reads=[("cb", c)], writes=[("cb", c)])
              if it == 0:
                  dump("convo", BIG0[:, :], [128, 8 * N], KBIG0)
              ckpt(4)
              ln_fm(convo, [("BIG0", c) for c in range(8)], onesln, "onesln", LN_EPS, "conv_ln_g", "conv_ln_b", list(range(8)),
                    [[(cs_[c], ("HB0", c), AF.Silu)] for c in range(8)])
              for half in range(2):
                  wc, kc_, _ = load_group("co%d" % half)
                  wga, kga, _ = load_group("gA%d" % half)
                  for cc in range(4):
                      c = half * 4 + cc
                      pA, kA = next_pm()
                      mm_chunk(pA[:, 0:N], kA, wc, kc_, 8, cc * 128, (cc + 1) * 128, lambda kc: cs_[kc], [("HB0", k) for k in range(8)])
                      pG, kG = next_pm()
                      mm_chunk(pG[:, 0:N], kG, wga, kga, 8, cc * 128, (cc + 1) * 128, rhs_h, ["hT"])
                      sgt, sgk = Ft[6 + c % 2], "F%d" % (6 + c % 2)
                      P.op("act", lambda e, pG=pG, sgt=sgt, c=c: e.activation(out=sgt[:], in_=pG[:, 0:N], func=AF.Sigmoid, bias=pc("b_gate", c)),
                           reads=[kG, "PT"], writes=[sgk])
                      P.op("dve", lambda e, pA=pA, sgt=sgt, c=c: e.tensor_tensor(out=yA[c], in0=pA[:, 0:N], in1=sgt[:], op=ALU.mult),
                           reads=[kA, sgk], writes=[("BIG1", c)])
                  done_group(2)
              if it == 0:
                  dump("yA", BIG1[:, :], [128, 8 * N], KBIG1)

              ckpt(5)
              wl, kl, _ = load_group("lr0")
              pW, kW = next_pm()
              mm_chunk(pW[0:64, 0:N], kW, wl, kl, 16, 0, 64, rhs_hs, ["hT"])
              P.op("act", lambda e, pW=pW: e.activation(out=tw[0:64, :], in_=pW[0:64, 0:N], func=AF.Tanh), reads=[kW], writes=["tw"])
              pX, kX = next_pm()
              mm_chunk(pX[0:64, 0:N], kX, wl, kl, 16, 64, 128, rhs_hs, ["hT"])
              P.op("act", lambda e, pX=pX: e.activation(out=xab[0:64, :], in_=pX[0:64, 0:N], func=AF.Copy), reads=[kX], writes=["xab"])
              done_group(1)
              wl, kl, _ = load_group("lr1")
              pW, kW = next_pm()
              mm_chunk(pW[:, 0:N], kW, wl, kl, 16, 0, 128, rhs_hs, ["hT"])
              P.op("act", lambda e, pW=pW: e.activation(out=sg0[:], in_=pW[:, 0:N], func=AF.Sigmoid), reads=[kW], writes=["sg0"])
              pX, kX = next_pm()
              mm_chunk(pX[0:32, 0:N], kX, wl, kl, 16, 128, 160, rhs_hs, ["hT"])
              P.op("act", lambda e, pX=pX: e.activation(out=sg1[0:32, :], in_=pX[0:32, 0:N], func=AF.Sigmoid), reads=[kX], writes=["sg1"])
              done_group(1)

              for grp in range(4):
                  hps = [2 * grp, 2 * grp + 1]
                  wr_, kr_, _ = load_group("r%d" % grp)
                  wk_, kk__, _ = load_group("k%d" % grp)
                  wv_, kv_, _ = load_group("v%d" % grp)
                  for gi_, hp in enumerate(hps):
                      co = gi_ * 128
                      r_, k_, sgw, ai = Ft[0], Ft[1], Ft[2], Ft[3]
                      cs, csm, Ep, En, Em, EC = Ft[4], Ft[5], Ft[6], Ft[7], Ft[8], Ft[9]
                      kkn, b_, kf, kp, nrm = Ft[10], Ft[11], Ft[12], Ft[13], Ft[14]
                      ksq = Ht[4]
                      pR, kR = next_pm()
                      mm_chunk(pR[:, 0:N], kR, wr_, kr_, 16, co, co + 128, rhs_hs, ["hT"])
                      P.op("act", lambda e, pR=pR: e.activation(out=r_[:], in_=pR[:, 0:N], func=AF.Copy), reads=[kR], writes=["F0"])
                      pK, kK = next_pm()
                      mm_chunk(pK[:, 0:N], kK, wk_, kk__, 16, co, co + 128, rhs_hs, ["hT"])
                      P.op("act", lambda e, pK=pK: e.activation(out=k_[:], in_=pK[:, 0:N], func=AF.Copy), reads=[kK], writes=["F1"])
                      pV, kV = next_pm()
                      mm_chunk(pV[:, 0:N], kV, wv_, kv_, 16, co, co + 128, rhs_hs, ["hT"])
                      P.op("act", lambda e, pV=pV, gi_=gi_: e.activation(out=vTf[gi_][:], in_=pV[:, 0:N], func=AF.Copy), reads=[kV], writes=["vTf%d" % gi_])
                      P.op("pool", lambda e, gi_=gi_: e.tensor_copy(out=vb[gi_][:], in_=vTf[gi_][:]), reads=["vTf%d" % gi_], writes=["vb%d" % gi_])
                      pZ, kZ = next_pm()
                      P.op("pe", lambda e, pZ=pZ, hp=hp: e.matmul(pZ[:, 0:N], lhsT=lorab[0:64, 0, hp * 128:(hp + 1) * 128], rhs=tw[0:64, :], start=True, stop=True),
                           reads=["lorab", "tw"], writes=[kZ])
                      P.op("act", lambda e, pZ=pZ, hp=hp: e.activation(out=sgw[:], in_=pZ[:, 0:N], func=AF.Sigmoid, bias=pc("w0", hp)),
                           reads=[kZ, "PT"], writes=["F2"])
                      pZ2, kZ2 = next_pm()
                      P.op("pe", lambda e, pZ2=pZ2, hp=hp: e.matmul(pZ2[:, 0:N], lhsT=lorab[0:64, 1, hp * 128:(hp + 1) * 128], rhs=xab[0:64, :], start=True, stop=True),
                           reads=["lorab", "xab"], writes=[kZ2])
                      P.op("act", lambda e, pZ2=pZ2, hp=hp: e.activation(out=ai[:], in_=pZ2[:, 0:N], func=AF.Sigmoid, bias=pc("a0", hp)),
                           reads=[kZ2, "PT"], writes=["F3"])
                      for ch in range(NCH):
                          sl = slice(ch * CH, (ch + 1) * CH)
                          P.op("dve", lambda e, sl=sl: e.tensor_tensor_scan(out=cs[:, sl], data0=ones_f[:], data1=sgw[:, sl], initial=0.0,
                                                                           op0=ALU.mult, op1=ALU.add),
                               reads=["F2", "ones_f"], writes=["F4"])
                      P.op("pool", lambda e: e.tensor_tensor(out=csm[:], in0=cs[:], in1=sgw[:], op=ALU.subtract), reads=["F4", "F2"], writes=["F5"])
                      P.op("act", lambda e: e.activation(out=Ep[:], in_=cs[:], func=AF.Exp, scale=-C0), reads=["F4"], writes=["F6"])
                      P.op("act", lambda e: e.activation(out=En[:], in_=cs[:], func=AF.Exp, scale=C0), reads=["F4"], writes=["F7"])
                      P.op("act", lambda e: e.activation(out=Em[:], in_=csm[:], func=AF.Exp, scale=-C0), reads=["F5"], writes=["F8"])
                      for ch in range(NCH):
                          ce = (ch + 1) * CH - 1
                          P.op("dve", lambda e, ch=ch, ce=ce: e.tensor_scalar(out=nbC[:, ch:ch + 1], in0=cs[:, ce:ce + 1], scalar1=-C0, scalar2=None, op0=ALU.mult),
                               reads=["F4"], writes=["nbC"])
                          P.op("pool", lambda e, ch=ch, ce=ce, gi_=gi_: e.tensor_copy(out=PCc[gi_][:, ch:ch + 1], in_=Ep[:, ce:ce + 1]),
                               reads=["F6"], writes=["PCc%d" % gi_])
                      for ch in range(NCH):
                          sl = slice(ch * CH, (ch + 1) * CH)
                          P.op("act", lambda e, sl=sl, ch=ch: e.activation(out=EC[:, sl], in_=cs[:, sl], func=AF.Exp, scale=C0, bias=nbC[:, ch:ch + 1]),
                               reads=["F4", "nbC"], writes=["F9"])
                      P.op("act", lambda e, hp=hp: e.activation(out=ksq[:], in_=k_[:], func=AF.Square, scale=pc("k_k", hp)), reads=["F1", "PT"], writes=["H4"])
                      pS, kS = next_pm()
                      P.op("pe", lambda e, pS=pS: e.matmul(pS[:, 0:N], lhsT=bo1[:], rhs=ksq[:], start=True, stop=True), reads=["bo1", "H4"], writes=[kS])
                      P.op("act", lambda e, pS=pS: e.activation(out=nrm[:], in_=pS[:, 0:N], func=AF.Sqrt), reads=[kS], writes=["F14"])
                      P.op("dve", lambda e: e.tensor_scalar(out=nrm[:], in0=nrm[:], scalar1=1e-12, scalar2=None, op0=ALU.max), reads=["F14"], writes=["F14"])
                      P.op("dve", lambda e: e.reciprocal(out=nrm[:], in_=nrm[:]), reads=["F14"], writes=["F14"])
                      P.op("dve", lambda e, hp=hp: e.scalar_tensor_tensor(out=kkn[:], in0=k_[:], scalar=pc("k_k", hp), in1=nrm[:], op0=ALU.mult, op1=ALU.mult),
                           reads=["F1", "F14", "PT"], writes=["F10"])
                      P.op("dve", lambda e, gi_=gi_: e.scalar_tensor_tensor(out=AR[gi_][:, 0, :], in0=kkn[:], scalar=-1.0, in1=Em[:], op0=ALU.mult, op1=ALU.mult),
                           reads=["F10", "F8"], writes=["AR%d" % gi_])
                      P.op("pool", lambda e, gi_=gi_: e.tensor_tensor(out=AR[gi_][:, 1, :], in0=r_[:], in1=Ep[:], op=ALU.mult),
                           reads=["F0", "F6"], writes=["AR%d" % gi_])
                      P.op("pool", lambda e: e.tensor_tensor(out=b_[:], in0=kkn[:], in1=ai[:], op=ALU.mult), reads=["F10", "F3"], writes=["F11"])
                      P.op("dve", lambda e, gi_=gi_: e.tensor_tensor(out=BK[gi_][:, 0, :], in0=b_[:], in1=En[:], op=ALU.mult), reads=["F11", "F7"], writes=["BK%d" % gi_])
                      P.op("pool", lambda e, gi_=gi_: e.tensor_tensor(out=BKh[gi_][:, 0, :], in0=b_[:], in1=EC[:], op=ALU.mult), reads=["F11", "F9"], writes=["BKh%d" % gi_])
                      P.op("dve", lambda e, hp=hp: e.tensor_scalar(out=kf[:], in0=ai[:], scalar1=pc("k_a", hp), scalar2=pc("omka", hp), op0=ALU.mult, op1=ALU.add),
                           reads=["F3", "PT"], writes=["F12"])
                      P.op("pool", lambda e: e.tensor_tensor(out=kp[:], in0=k_[:], in1=kf[:], op=ALU.mult), reads=["F1", "F12"], writes=["F13"])
                      P.op("dve", lambda e, gi_=gi_: e.tensor_tensor(out=BK[gi_][:, 1, :], in0=kp[:], in1=En[:], op=ALU.mult), reads=["F13", "F7"], writes=["BK%d" % gi_])
                      P.op("pool", lambda e, gi_=gi_: e.tensor_tensor(out=BKh[gi_][:, 1, :], in0=kp[:], in1=EC[:], op=ALU.mult), reads=["F13", "F9"], writes=["BKh%d" % gi_])
                      P.op("dve", lambda e, hp=hp, gi_=gi_: e.scalar_tensor_tensor(out=rkb[gi_][:], in0=r_[:], scalar=pc("r_k", hp), in1=kp[:], op0=ALU.mult, op1=ALU.mult),
                           reads=["F0", "F13", "PT"], writes=["rkb%d" % gi_])

                  done_group(3)
                  ckpt(6)
                  for ch in range(NCH):
                      sl = slice(ch * CH, (ch + 1) * CH)
                      for gi_, hp in enumerate(hps):
                          pT, kT = next_pr()
                          srcs = [(AR[gi_][:, 0, sl], "AR%d" % gi_), (BKh[gi_][:, 0, sl], "BKh%d" % gi_),
                                  (BKh[gi_][:, 1, sl], "BKh%d" % gi_), (vb[gi_][:, sl], "vb%d" % gi_)]
                          for q, (sap, skey) in enumerate(srcs):
                              P.op("pe", lambda e, pT=pT, q=q, sap=sap: e.matmul(pT[:, q * 128:(q + 1) * 128], lhsT=sap, rhs=ident[:], start=True, stop=True),
                                   reads=[skey, "ident"], writes=[kT])
                          P.op("act", lambda e, pT=pT, gi_=gi_: e.activation(out=TM[gi_][:, :, :], in_=pT[:, :].rearrange("p (q n) -> p q n", q=4), func=AF.Copy),
                               reads=[kT], writes=["TM%d" % gi_])
                      ckpt(6.1)
                      for gi_, hp in enumerate(hps):
                          for h in range(2):
                              hs = slice(h * 64, (h + 1) * 64)
                              pQ, kQ = next_pr()
                              P.op("pe", lambda e, pQ=pQ, gi_=gi_, hs=hs: e.matmul(pQ[:, 0:256].rearrange("p (q n) -> p q n", q=2), lhsT=BK[gi_][hs, 0, sl], rhs=AR[gi_][hs, :, sl], start=True, stop=True),
                                   reads=["BK%d" % gi_, "AR%d" % gi_], writes=[kQ])
                              P.op("pe", lambda e, pQ=pQ, gi_=gi_, hs=hs: e.matmul(pQ[:, 256:512].rearrange("p (q n) -> p q n", q=2), lhsT=BK[gi_][hs, 1, sl], rhs=AR[gi_][hs, :, sl], start=True, stop=True),
                                   reads=["BK%d" % gi_, "AR%d" % gi_], writes=[kQ])
                              P.op("dve", lambda e, pQ=pQ, gi_=gi_, h=h: e.tensor_tensor(out=QKs[gi_][h][:, :, :], in0=pQ[:, :].rearrange("p (q n) -> p q n", q=4), in1=mask4[:, :, :], op=ALU.mult),
                                   reads=[kQ, "mask4"], writes=["QKs%d_%d" % (gi_, h)])
                          pL, kL = next_pr()
                          for h in range(2):
                              P.op("pe", lambda e, pL=pL, gi_=gi_, h=h: e.matmul(pL[:, h * 128:(h + 1) * 128], lhsT=QKs[gi_][h][:, 0, :], rhs=ident[:], start=True, stop=True),
                                   reads=["QKs%d_%d" % (gi_, h), "ident"], writes=[kL])
                              P.op("pool", lambda e, gi_=gi_, h=h: e.tensor_copy(out=NMb[gi_][0][:, h, :], in_=QKs[gi_][h][:, 0, :]),
                                   reads=["QKs%d_%d" % (gi_, h)], writes=["NMb%d_0" % gi_])
                              P.op("pool", lambda e, gi_=gi_, h=h: e.tensor_tensor(out=Qb[gi_][0][:, h, :], in0=QKs[gi_][h][:, 0, :], in1=ident[:], op=ALU.add),
                                   reads=["QKs%d_%d" % (gi_, h), "ident"], writes=["Qb%d_0" % gi_])
                          P.op("act", lambda e, pL=pL, gi_=gi_: e.activation(out=NMb[gi_][0][:, 2:4, :], in_=pL[:, 0:256].rearrange("p (q n) -> p q n", q=2), func=AF.Copy),
                               reads=[kL], writes=["NMb%d_0" % gi_])
                      ckpt(6.2)
                      NLV = 6
                      for lv in range(NLV):
                          cur, nxt = lv % 2, (lv + 1) % 2
                          last = (lv == NLV - 1)
                          for gi_, hp in enumerate(hps):
                              pN, kN = next_pr()
                              kcur = "NMb%d_%d" % (gi_, cur)
                              knxt = "NMb%d_%d" % (gi_, nxt)
                              for h in range(2):
                                  if not last:
                                      P.op("pe", lambda e, pN=pN, gi_=gi_, h=h, cur=cur: e.matmul(pN[:, h * 128:(h + 1) * 128], lhsT=NMb[gi_][cur][:, 2 + h, :], rhs=NMb[gi_][cur][:, h, :], start=True, stop=True),
                                           reads=[kcur], writes=[kN])
                                  P.op("pe", lambda e, pN=pN, gi_=gi_, h=h, cur=cur: e.matmul(pN[:, (2 + h) * 128:(3 + h) * 128], lhsT=NMb[gi_][cur][:, h, :], rhs=NMb[gi_][cur][:, 2 + h, :], start=True, stop=True),
                                       reads=[kcur], writes=[kN])
                              if not last:
                                  P.op("act", lambda e, pN=pN, gi_=gi_, nxt=nxt: e.activation(out=NMb[gi_][nxt][:, :, :], in_=pN[:, :].rearrange("p (q n) -> p q n", q=4), func=AF.Copy),
                                       reads=[kN], writes=[knxt])
                              else:
                                  P.op("act", lambda e, pN=pN, gi_=gi_, nxt=nxt: e.activation(out=NMb[gi_][nxt][:, 2:4, :], in_=pN[:, 256:512].rearrange("p (q n) -> p q n", q=2), func=AF.Copy),
                                       reads=[kN], writes=[knxt])
                          for gi_, hp in enumerate(hps):
                              pQ2, kQ2 = next_pr()
                              knxt = "NMb%d_%d" % (gi_, nxt)
                              kq0 = "Qb%d_%d" % (gi_, cur)
                              kq1 = "Qb%d_%d" % (gi_, nxt)
                              for h in range(2):
                                  P.op("pe", lambda e, pQ2=pQ2, gi_=gi_, h=h, nxt=nxt, cur=cur: e.matmul(pQ2[:, h * 128:(h + 1) * 128], lhsT=NMb[gi_][nxt][:, 2 + h, :], rhs=Qb[gi_][cur][:, h, :], start=True, stop=True),
                                       reads=[knxt, kq0], writes=[kQ2])
                              P.op("dve", lambda e, pQ2=pQ2, gi_=gi_, nxt=nxt, cur=cur: e.tensor_tensor(out=Qb[gi_][nxt][:, :, :], in0=pQ2[:, 0:256].rearrange("p (q n) -> p q n", q=2), in1=Qb[gi_][cur][:, :, :], op=ALU.add),
                                   reads=[kQ2, kq0], writes=[kq1])
                      qf = NLV % 2
                      ckpt(6.3)
                      for gi_, hp in enumerate(hps):
                          kq = "Qb%d_%d" % (gi_, qf)
                          pG1, kG1 = next_pr()
                          for h in range(2):
                              hs = slice(h * 64, (h + 1) * 64)
                              P.op("pe", lambda e, pG1=pG1, gi_=gi_, h=h, hs=hs: e.matmul(pG1[hs, 0:128], lhsT=TM[gi_][:, 0, hs], rhs=Qb[gi_][qf][:, h, :], start=True, stop=True),
                                   reads=["TM%d" % gi_, kq], writes=[kG1])
                              P.op("pe", lambda e, pG1=pG1, gi_=gi_, h=h, hs=hs: e.matmul(pG1[:, 128 + h * 64:128 + (h + 1) * 64], lhsT=QKs[gi_][h][:, 2, :], rhs=TM[gi_][:, 3, hs], start=True, stop=True),
                                   reads=["TM%d" % gi_, "QKs%d_%d" % (gi_, h)], writes=[kG1])
                          P.op("act", lambda e, pG1=pG1, gi_=gi_: e.activation(out=G1Tb[gi_][:], in_=pG1[:, 0:128], func=AF.Copy), reads=[kG1], writes=["G1Tb%d" % gi_])
                          P.op("act", lambda e, pG1=pG1, gi_=gi_: e.activation(out=Xb[gi_][:], in_=pG1[:, 128:256], func=AF.Copy), reads=[kG1], writes=["Xb%d" % gi_])
                          pG2, kG2 = next_pr()
                          for h in range(2):
                              hs = slice(h * 64, (h + 1) * 64)
                              P.op("pe", lambda e, pG2=pG2, gi_=gi_, h=h, hs=hs: e.matmul(pG2[:, h * 64:(h + 1) * 64], lhsT=Qb[gi_][qf][:, h, :], rhs=Xb[gi_][:, hs], start=True, stop=True),
                                   reads=[kq, "Xb%d" % gi_], writes=[kG2])
                          P.op("act", lambda e, pG2=pG2, gi_=gi_: e.activation(out=G2f[gi_][:], in_=pG2[:, 0:128], func=AF.Copy), reads=[kG2], writes=["G2f%d" % gi_])
                      ckpt(6.4)
                      for gi_, hp in enumerate(hps):
                          pU, kU = next_pr()
                          P.op("pe", lambda e, pU=pU, gi_=gi_, hp=hp: e.matmul(pU[:, 0:128], lhsT=G1Tb[gi_][:, :], rhs=Sb[:, hp, :], start=True, stop=True),
                               reads=["G1Tb%d" % gi_, ("Sb", hp)], writes=[kU])
                          P.op("dve", lambda e, pU=pU, gi_=gi_: e.tensor_tensor(out=Ub[gi_][:], in0=pU[:, 0:128], in1=G2f[gi_][:], op=ALU.add),
                               reads=[kU, "G2f%d" % gi_], writes=["Ub%d" % gi_])
                      ckpt(6.5)
                      for gi_, hp in enumerate(hps):
                          pY, kY = next_pr()
                          P.op("pe", lambda e, pY=pY, gi_=gi_, hp=hp: e.matmul(pY[:, 0:128], lhsT=Sb[:, hp, :], rhs=AR[gi_][:, 1, sl], start=True, stop=False),
                               reads=[("Sb", hp), "AR%d" % gi_], writes=[kY])
                          for h in range(2):
                              hs = slice(h * 64, (h + 1) * 64)
                              P.op("pe", lambda e, pY=pY, gi_=gi_, hs=hs, h=h: e.matmul(pY[hs, 0:128], lhsT=Ub[gi_][:, hs], rhs=QKs[gi_][h][:, 1, :], start=False, stop=False),
                                   reads=["Ub%d" % gi_, "QKs%d_%d" % (gi_, h)], writes=[kY])
                              P.op("pe", lambda e, pY=pY, gi_=gi_, hs=hs, h=h: e.matmul(pY[hs, 0:128], lhsT=TM[gi_][:, 3, hs], rhs=QKs[gi_][h][:, 3, :], start=False, stop=True),
                                   reads=["TM%d" % gi_, "QKs%d_%d" % (gi_, h)], writes=[kY])
                          ckpt(6.6)
                          P.op("pe", lambda e, pY=pY, gi_=gi_: e.matmul(pY[:, 128:256], lhsT=TM[gi_][:, 1, :], rhs=Ub[gi_][:, :], start=True, stop=False),
                               reads=["TM%d" % gi_, "Ub%d" % gi_], writes=[kY])
                          P.op("pe", lambda e, pY=pY, gi_=gi_: e.matmul(pY[:, 128:256], lhsT=TM[gi_][:, 2, :], rhs=TM[gi_][:, 3, :], start=False, stop=True),
                               reads=["TM%d" % gi_], writes=[kY])
                          ckpt(6.7)
                          P.op("act", lambda e, pY=pY, gi_=gi_: e.activation(out=Yf[gi_][:, sl], in_=pY[:, 0:128], func=AF.Copy), reads=[kY], writes=["Yf%d" % gi_])
                          ckpt(6.8)
                          P.op("act", lambda e, pY=pY, gi_=gi_: e.activation(out=Stmp[gi_][:], in_=pY[:, 128:256], func=AF.Copy),
                               reads=[kY], writes=["Stmp%d" % gi_])
                          ckpt(6.85)
                          P.op("pool", lambda e, gi_=gi_: e.tensor_tensor(out=Stmp[gi_][:], in0=Stmp[gi_][:], in1=bo1[:], op=ALU.mult),
                               reads=["Stmp%d" % gi_, "bo1"], writes=["Stmp%d" % gi_])
                          ckpt(6.87)
                          P.op("dve", lambda e, gi_=gi_, hp=hp, ch=ch: e.scalar_tensor_tensor(
                              out=Sf[:, hp, :], in0=Sf[:, hp, :], scalar=PCc[gi_][:, ch:ch + 1], in1=Stmp[gi_][:], op0=ALU.mult, op1=ALU.add),
                               reads=["Stmp%d" % gi_, ("Sf", hp), "PCc%d" % gi_], writes=[("Sf", hp)])
                          ckpt(6.9)
                          P.op("pool", lambda e, hp=hp: e.tensor_copy(out=Sb[:, hp, :], in_=Sf[:, hp, :]), reads=[("Sf", hp)], writes=[("Sb", hp)])

                  ckpt(7)
                  for gi_, hp in enumerate(hps):
                      ybf, ysq = Ht[0], Ht[2]
                      P.op("pool", lambda e, gi_=gi_: e.tensor_copy(out=ybf[:], in_=Yf[gi_][:]), reads=["Yf%d" % gi_], writes=["H0"])
                      P.op("act", lambda e, gi_=gi_: e.activation(out=ysq[:], in_=Yf[gi_][:], func=AF.Square), reads=["Yf%d" % gi_], writes=["H2"])
                      P.op("pe", lambda e: e.matmul(pa[:, 0:N], lhsT=bo64[:], rhs=ybf[:], start=True, stop=True), reads=["bo64", "H0"], writes=["pa"])
                      P.op("pe", lambda e: e.matmul(pb[:, 0:N], lhsT=bo64[:], rhs=ysq[:], start=True, stop=True), reads=["bo64", "H2"], writes=["pb"])
                      m2, var, rstd, t1 = Ft[0], Ft[1], Ft[2], Ft[3]
                      P.op("act", lambda e: e.activation(out=m2[:], in_=pa[:, 0:N], func=AF.Square), reads=["pa"], writes=["F0"])
                      P.op("dve", lambda e: e.tensor_tensor(out=var[:], in0=pb[:, 0:N], in1=m2[:], op=ALU.subtract), reads=["pb", "F0"], writes=["F1"])
                      P.op("dve", lambda e: e.tensor_scalar(out=var[:], in0=var[:], scalar1=GN_EPS, scalar2=None, op0=ALU.add), reads=["F1"], writes=["F1"])
                      P.op("act", lambda e: e.activation(out=var[:], in_=var[:], func=AF.Sqrt), reads=["F1"], writes=["F1"])
                      P.op("dve", lambda e: e.reciprocal(out=rstd[:], in_=var[:]), reads=["F1"], writes=["F2"])
                      P.op("dve", lambda e, gi_=gi_: e.tensor_tensor(out=t1[:], in0=Yf[gi_][:], in1=pa[:, 0:N], op=ALU.subtract), reads=["Yf%d" % gi_, "pa"], writes=["F3"])
                      P.op("pool", lambda e: e.tensor_tensor(out=t1[:], in0=t1[:], in1=rstd[:], op=ALU.mult), reads=["F3", "F2"], writes=["F3"])
                      P.op("dve", lambda e, hp=hp: e.tensor_scalar(out=t1[:], in0=t1[:], scalar1=pc("lnx_g", hp), scalar2=pc("lnx_b", hp), op0=ALU.mult, op1=ALU.add),
                           reads=["F3", "PT"], writes=["F3"])
                      pBn, kBn = next_pm()
                      P.op("pe", lambda e, pBn=pBn, gi_=gi_: e.matmul(pBn[:, 0:N], lhsT=bo1[:], rhs=rkb[gi_][:], start=True, stop=True), reads=["bo1", "rkb%d" % gi_], writes=[kBn])
                      t2 = Ft[4]
                      P.op("dve", lambda e, pBn=pBn, gi_=gi_: e.tensor_tensor(out=t2[:], in0=pBn[:, 0:N], in1=vTf[gi_][:], op=ALU.mult), reads=[kBn, "vTf%d" % gi_], writes=["F4"])
                      P.op("pool", lambda e: e.tensor_tensor(out=t1[:], in0=t1[:], in1=t2[:], op=ALU.add), reads=["F3", "F4"], writes=["F3"])
                      pGt, kGt = next_pm()
                      P.op("pe", lambda e, pGt=pGt, hp=hp: e.matmul(pGt[:, 0:N], lhsT=lorab[:, 2, hp * 128:(hp + 1) * 128], rhs=sg0[:], start=True, stop=False), reads=["lorab", "sg0"], writes=[kGt])
                      P.op("pe", lambda e, pGt=pGt, hp=hp: e.matmul(pGt[:, 0:N], lhsT=lorab[0:32, 3, hp * 128:(hp + 1) * 128], rhs=sg1[0:32, :], start=False, stop=True), reads=["lorab", "sg1"], writes=[kGt])
                      P.op("dve", lambda e, pGt=pGt, hp=hp: e.tensor_tensor(out=oT[hp], in0=pGt[:, 0:N], in1=t1[:], op=ALU.mult), reads=[kGt, "F3"], writes=[("HB0", hp)])
              if it == 0:
                  dump("oT", HB0[:, :], [128, 8 * N], KHB0, BF16)

              ckpt(8)
              for half in range(2):
                  wro, kro, _ = load_group("ro%d" % half)
                  wgb, kgb, _ = load_group("gB%d" % half)
                  for cc in range(4):
                      c = half * 4 + cc
                      pA, kA = next_pm()
                      mm_chunk(pA[:, 0:N], kA, wro, kro, 8, cc * 128, (cc + 1) * 128, lambda kc: oT[kc], [("HB0", k) for k in range(8)])
                      pG, kG = next_pm()
                      mm_chunk(pG[:, 0:N], kG, wgb, kgb, 8, cc * 128, (cc + 1) * 128, rhs_h, ["hT"])
                      sgt, sgk = Ft[6 + c % 2], "F%d" % (6 + c % 2)
                      tt, tk = Ft[8 + c % 2], "F%d" % (8 + c % 2)
                      P.op("act", lambda e, pG=pG, sgt=sgt, c=c: e.activation(out=sgt[:], in_=pG[:, 0:N], func=AF.Sigmoid, bias=pc("b_gate", 8 + c)),
                           reads=[kG, "PT"], writes=[sgk])
                      P.op("dve", lambda e, pA=pA, sgt=sgt, tt=tt: e.tensor_tensor(out=tt[:], in0=pA[:, 0:N], in1=sgt[:], op=ALU.mult),
                           reads=[kA, sgk], writes=[tk])
                      P.op("pool", lambda e, tt=tt, c=c: e.tensor_tensor(out=ybv[c], in0=tt[:], in1=yA[c], op=ALU.add),
                           reads=[tk, ("BIG1", c)], writes=[("HB1", c)])
                  done_group(2)
              P.op("pool", lambda e: e.tensor_copy(out=hT[:, :, 0:1], in_=hT[:, :, N:N + 1]), reads=["hT"], writes=["hT"])

              for half in range(2):
                  wwo, kwo, _ = load_group("wo%d" % half)
                  for cc in range(4):
                      c = half * 4 + cc
                      pA, kA = next_pm()
                      mm_chunk(pA[:, 0:N], kA, wwo, kwo, 8, cc * 128, (cc + 1) * 128, lambda kc: ybv[kc], [("HB1", k) for k in range(8)])
                      P.op("dve", lambda e, pA=pA, c=c: e.scalar_tensor_tensor(out=xres[:, c, :], in0=xres[:, c, :], scalar=ALPHA, in1=pA[:, 0:N], op0=ALU.mult, op1=ALU.add),
                           reads=[kA, "xres"], writes=["xres"])
                  done_group(1)
              ln_fm([xres[:, c, :] for c in range(8)], ["xres"] * 8, onesln, "onesln", LN_EPS, "ln1_g", "ln1_b", list(range(8)),
                    [[(xres[:, c, :], "xres", AF.Identity), (h1T[:, c, 2:N + 2], "h1T", AF.Identity)] for c in range(8)])
              if it == 0:
                  dump("h1", xres[:, :, :], [128, 8, N], ["xres"])

              ckpt(9)
              NW = N + 2
              for j in range(11):
                  wfu, kfu, _ = load_group("fu%d" % j)
                  for ff in range(2):
                      f = 2 * j + ff
                      outs = []
                      for part in range(2):
                          ci = part * 22 + f
                          pU, kU = next_pm()
                          mm_chunk(pU[:, 0:NW], kU, wfu, kfu, 8, part * 256 + ff * 128, part * 256 + (ff + 1) * 128,
                                   lambda kc: h1T[:, kc, 0:NW], ["h1T"])
                          ta, tka = Ft[(4 * f + 2 * part) % 12], "F%d" % ((4 * f + 2 * part) % 12)
                          tb, tkb = Ft[(4 * f + 2 * part + 1) % 12], "F%d" % ((4 * f + 2 * part + 1) % 12)
                          P.op("act", lambda e, pU=pU, ta=ta, ci=ci: e.activation(out=ta[:], in_=pU[:, 2:NW], func=AF.Identity, scale=pc("ffn_dw", 2 * 44 + ci)),
                               reads=[kU, "PT"], writes=[tka])
                          P.op("dve", lambda e, pU=pU, ta=ta, tb=tb, ci=ci: e.scalar_tensor_tensor(out=tb[:], in0=pU[:, 1:NW - 1], scalar=pc("ffn_dw", 1 * 44 + ci), in1=ta[:], op0=ALU.mult, op1=ALU.add),
                               reads=[kU, tka, "PT"], writes=[tkb])
                          P.op("dve", lambda e, pU=pU, ta=ta, tb=tb, ci=ci: e.scalar_tensor_tensor(out=ta[:], in0=pU[:, 0:NW - 2], scalar=pc("ffn_dw", 0 * 44 + ci), in1=tb[:], op0=ALU.mult, op1=ALU.add),
                               reads=[kU, tkb, "PT"], writes=[tka])
                          outs.append((ta, tka, tb, tkb))
                      (la, lka, _, _), (ga_, gka, gb_, gkb) = outs
                      P.op("act", lambda e, ga_=ga_, gb_=gb_: e.activation(out=gb_[:], in_=ga_[:], func=AF.Silu), reads=[gka], writes=[gkb])
                      P.op("pool", lambda e, la=la, gb_=gb_, f=f: e.tensor_tensor(out=act[:, f, :], in0=la[:], in1=gb_[:], op=ALU.mult),
                           reads=[lka, gkb], writes=[("act", f)])
                  done_group(1)
              P.op("pool", lambda e: e.tensor_copy(out=h1T[:, :, 0:2], in_=h1T[:, :, N:N + 2]), reads=["h1T"], writes=["h1T"])
              for m in range(8):
                  wfd, kfd, _ = load_group("fd%d" % m)
                  pA, kA = next_pm()
                  mm_chunk(pA[:, 0:N], kA, wfd, kfd, 22, 0, 128, lambda kc: act[:, kc, :], [("act", k) for k in range(22)])
                  P.op("dve", lambda e, pA=pA, m=m: e.scalar_tensor_tensor(out=xres[:, m, :], in0=xres[:, m, :], scalar=ALPHA, in1=pA[:, 0:N], op0=ALU.mult, op1=ALU.add),
                       reads=[kA, "xres"], writes=["xres"])
                  done_group(1)
              ln_fm([xres[:, c, :] for c in range(8)], ["xres"] * 8, onesln, "onesln", LN_EPS, "ln2_g", "ln2_b", list(range(8)),
                    [[(xres[:, c, :], "xres", AF.Identity)] for c in range(8)])
              lo = max(t0, NMETA)
              hi = min(t0 + N, T_REAL)
              if hi > lo:
                  P.dma(lambda e, lo=lo, hi=hi, t0=t0: e.dma_start(
                      out=outT[:, lo - NMETA:hi - NMETA].rearrange("(c p) n -> p c n", p=128),
                      in_=xres[:, :, lo - t0:hi - t0]), reads=["xres"])

        except StopBuild:
            pass
        P.final_wait()
        P.emit()
        nops = P.nops
    return nc, dbg_out, nops


def host_layout(inputs):
    groups, totw = weight_groups()
    pcol, npar = par_layout()
    W = {k: np.asarray(inputs[k], np.float32)[0] for k in ("w_in", "w_conv_out", "w_rwkv_out", "w_o", "ffn_up", "ffn_down")}
    wall = np.empty((128, totw), np.float32)
    for g in groups:
        src = W[g["src"]][:, g["cols"]]
        kc = g["kc_src"]
        blk = src.reshape(kc, 128, g["m"]).transpose(1, 0, 2)
        wall[:, g["off"]:g["off"] + kc * g["m"]] = blk.reshape(128, kc * g["m"])
    par = np.zeros((128, npar), np.float32)
    def put(name, vec):
        cv = colvec(vec)
        par[:, pcol[name]:pcol[name] + cv.shape[1]] = cv
    put("ln_in_g", inputs["ln_in_g"]); put("ln_in_b", inputs["ln_in_b"])
    cdw = np.asarray(inputs["conv_dw"], np.float32)[0]
    par[:, pcol["conv_dw"]:pcol["conv_dw"] + CONVK * 8] = cdw.reshape(CONVK, 8, 128).transpose(2, 0, 1).reshape(128, CONVK * 8)
    for nm in ("conv_dw_b", "conv_ln_g", "conv_ln_b", "b_gate", "w0", "a0", "k_k", "k_a", "r_k", "lnx_g", "lnx_b",
               "ln1_g", "ln1_b", "ln2_g", "ln2_b"):
        put(nm, np.asarray(inputs[nm], np.float32)[0].reshape(-1))
    fdw = np.asarray(inputs["ffn_dw"], np.float32)[0]
    par[:, pcol["ffn_dw"]:pcol["ffn_dw"] + 3 * 44] = fdw.reshape(3, 44, 128).transpose(2, 0, 1).reshape(128, 3 * 44)
    mub = np.ascontiguousarray(np.broadcast_to(np.asarray(inputs["rwkv_mu"], np.float32)[0][None, :], (128, NRW)))
    lora = np.zeros((128, 4, D), np.float32)
    lora[0:64, 0] = np.asarray(inputs["w_decay_up"], np.float32)[0]
    lora[0:64, 1] = np.asarray(inputs["a_up"], np.float32)[0]
    gup = np.asarray(inputs["g_up"], np.float32)[0]
    lora[:, 2] = gup[0:128]
    lora[0:32, 3] = gup[128:160]
    x = np.asarray(inputs["x"], np.float32)
    meta = np.asarray(inputs["meta_tokens"], np.float32)
    xTs = []
    for b in range(x.shape[0]):
        xt = np.zeros((D, TP), np.float32)
        xt[:, 0:NMETA] = meta.T
        xt[:, NMETA:T_REAL] = x[b].T
        xTs.append(xt)
    return wall, par, mub, lora, xTs


_CACHE = {}


def kernel(**inputs):
    nt_run = int(os.environ.get("KNT", NT_FULL))
    dbg_names = tuple(x for x in os.environ.get("KDBG", "").split(",") if x)
    ncores = int(os.environ.get("KCORES", 8))
    key = (nt_run, dbg_names)
    if key not in _CACHE:
        _CACHE[key] = build_program(nt_run, dbg_names)
    nc, dbg_out, nops = _CACHE[key]
    wall, par, mub, lora, xTs = host_layout(inputs)
    in_maps = [{"xT": xTs[b], "wall": wall, "par": par, "mub": mub, "lora": lora} for b in range(ncores)]
    res = run_bass_kernel_spmd(nc, in_maps, core_ids=list(range(ncores)))
    out = np.zeros((8, SEQ, D), np.float32)
    for b in range(ncores):
        out[b] = np.asarray(res.results[b]["outT"], np.float32).T
    if dbg_names:
        kernel.dbg = [{n: np.asarray(res.results[b]["dbg_" + n]) for n in dbg_out} for b in range(ncores)]
    return out
```
